# Optimizing a Trainium2 kernel written in Bass

```python
import jax
import jax.numpy as jnp
from jax import lax
import numpy as np

D_MODEL = 4096
BATCH = 2
SEQ = 4096
DEPTH = 2

GROUP_W = D_MODEL // 4
HEAD_DIM = 128
N_HEADS = GROUP_W // HEAD_DIM
N_KV = N_HEADS // 4
HPG = N_HEADS // N_KV
ROPE_THETA = 500000.0
NORM_EPS = 1e-6
NEG = -1e30
FORCE = 1e6

LRU_BLOCKS = N_HEADS
LRU_C = 8.0
CONV_A = 4
CONV_D = 3

CMP_LEN = 32
CMP_STRIDE = 16
CMP_HID = HEAD_DIM
SLC_BLK = 64
SLC_TOPN = 16
WINDOW = 512
WIN_QBLK = 128
GATHER_QBLK = 64

IDX_HEADS = 8
IDX_DIM = 64
DSA_TOPK = 256

SPLIT_SIZES = (
    GROUP_W, GROUP_W,
    N_HEADS * HEAD_DIM, GROUP_W, 6 * N_KV * HEAD_DIM, 3 * N_HEADS,
    N_HEADS * HEAD_DIM, GROUP_W, 2 * N_KV * HEAD_DIM,
    IDX_HEADS * IDX_DIM, IDX_DIM, IDX_HEADS,
    GROUP_W, GROUP_W, GROUP_W, GROUP_W,
)
IN_WIDTH = sum(SPLIT_SIZES)
MIX_W = 4 * GROUP_W

kernel_name = 'hybrid_rglru_nsa_dsa_shortconv'


def _rmsnorm(x, g):
    xf = x.astype(jnp.float32)
    y = xf * lax.rsqrt(jnp.mean(xf * xf, axis=-1, keepdims=True) + NORM_EPS)
    return (y * g.astype(jnp.float32)).astype(x.dtype)


def _rope(x, pos):
    d = x.shape[-1]
    r = d // 4
    half = r // 2
    inv = ROPE_THETA ** (-jnp.arange(half, dtype=jnp.float32) * 2.0 / r)
    ang = pos.astype(jnp.float32)[:, None] * inv[None, :]
    cos = jnp.cos(ang)[None, :, None, :]
    sin = jnp.sin(ang)[None, :, None, :]
    xf = x.astype(jnp.float32)
    x1 = xf[..., :half]
    x2 = xf[..., half:r]
    out = jnp.concatenate([x1 * cos - x2 * sin, x2 * cos + x1 * sin, xf[..., r:]], axis=-1)
    return out.astype(x.dtype)


def _causal_dwconv(x, w):
    K, C = w.shape
    return lax.conv_general_dilated(x, w[:, None, :], window_strides=(1,), padding=[(K - 1, 0)],
                                    dimension_numbers=('NWC', 'WIO', 'NWC'), feature_group_count=C)


def _masked_softmax(s, m):
    return jax.nn.softmax(jnp.where(m, s, NEG), axis=-1)


def _rglru(xa, conv_w, conv_b, wa, ba, wx, bx, lam):
    B, S, C = xa.shape
    u = _causal_dwconv(xa, conv_w) + conv_b
    ub = u.reshape(B, S, LRU_BLOCKS, C // LRU_BLOCKS)
    r = jax.nn.sigmoid((jnp.einsum('bshi,hij->bshj', ub, wa).reshape(B, S, C) + ba).astype(jnp.float32))
    i = jax.nn.sigmoid((jnp.einsum('bshi,hij->bshj', ub, wx).reshape(B, S, C) + bx).astype(jnp.float32))
    log_a = -LRU_C * r * jax.nn.softplus(-lam.astype(jnp.float32))
    a = jnp.exp(log_a)
    b = jnp.sqrt(-jnp.expm1(2.0 * log_a)) * (i * u.astype(jnp.float32))

    def comb(c1, c2):
        a1, b1 = c1
        a2, b2 = c2
        return a1 * a2, a2 * b1 + b2

    _, h = lax.associative_scan(comb, (a, b), axis=1)
    return h.astype(xa.dtype)


def _compress(k, pe, w1, w2):
    B, S, G, d = k.shape
    nc = S // CMP_STRIDE
    ch = k.reshape(B, nc, CMP_STRIDE, G, d)
    blocks = jnp.concatenate([ch[:, :-1], ch[:, 1:]], axis=2) + pe[None, None, :, None, :]
    h = jax.nn.silu(jnp.einsum('bnlgd,lde->bnge', blocks, w1))
    return jnp.einsum('bnge,ef->bngf', h, w2)


def _selected_attn(qg, k, v, sel, pos, scale):
    B, S, G, hpg, d = qg.shape
    nblk = S // SLC_BLK
    n_top = sel.shape[-1]
    kb = k.reshape(B, nblk, SLC_BLK, G, d).transpose(0, 3, 1, 2, 4)
    vb = v.reshape(B, nblk, SLC_BLK, G, d).transpose(0, 3, 1, 2, 4)
    Q = GATHER_QBLK
    nq = S // Q
    bidx = jnp.arange(B)[:, None, None, None]
    gidx = jnp.arange(G)[None, None, :, None]
    off = jnp.arange(SLC_BLK)

    def chunk(t):
        return t.reshape((B, nq, Q) + t.shape[2:]).swapaxes(0, 1)

    def body(args):
        qc, sc, tpos = args
        kg = kb[bidx, gidx, sc].reshape(B, Q, G, n_top * SLC_BLK, d)
        vg = vb[bidx, gidx, sc].reshape(B, Q, G, n_top * SLC_BLK, d)
        kpos = (sc[..., None] * SLC_BLK + off).reshape(B, Q, G, n_top * SLC_BLK)
        m = (kpos <= tpos[None, :, None, None])[:, :, :, None, :]
        s = jnp.einsum('bqghd,bqgkd->bqghk', qc, kg).astype(jnp.float32) * scale
        p = _masked_softmax(s, m)
        return jnp.einsum('bqghk,bqgkd->bqghd', p.astype(vg.dtype), vg)

    out = lax.map(body, (chunk(qg), chunk(sel), pos.reshape(nq, Q)))
    return out.swapaxes(0, 1).reshape(B, S, G, hpg, d)


def _window_attn(qg, k, v, pos, scale):
    B, S, G, hpg, d = qg.shape
    QB = WIN_QBLK
    nb = S // QB
    nw = WINDOW // QB + 1
    KL = nw * QB
    pad = ((0, 0), (WINDOW, 0), (0, 0), (0, 0))
    kp = jnp.pad(k, pad).reshape(B, nb + nw - 1, QB, G, d)
    vp = jnp.pad(v, pad).reshape(B, nb + nw - 1, QB, G, d)
    bi = jnp.arange(nb)[:, None] + jnp.arange(nw)[None, :]
    kb = kp[:, bi].reshape(B, nb, KL, G, d)
    vb = vp[:, bi].reshape(B, nb, KL, G, d)
    qb = qg.reshape(B, nb, QB, G, hpg, d)
    s = jnp.einsum('bnqghd,bnkgd->bnqghk', qb, kb).astype(jnp.float32) * scale
    tq = pos.reshape(nb, QB)
    sk = (jnp.arange(nb) * QB - WINDOW)[:, None] + jnp.arange(KL)[None, :]
    diff = tq[:, :, None] - sk[:, None, :]
    m = ((diff >= 0) & (diff < WINDOW) & (sk[:, None, :] >= 0))[None, :, :, None, None, :]
    p = _masked_softmax(s, m)
    o = jnp.einsum('bnqghk,bnkgd->bnqghd', p.astype(vb.dtype), vb)
    return o.reshape(B, S, G, hpg, d)


def _nsa(q, kv, gate, pos, q_gain, k_gain, pe_k, w1_k, w2_k, pe_v, w1_v, w2_v):
    B, S, _ = q.shape
    H, G, d = N_HEADS, N_KV, HEAD_DIM
    scale = d ** -0.5
    q = _rmsnorm(q.reshape(B, S, H, d), q_gain)
    kv = kv.reshape(B, S, 6, G, d)
    k_c, v_c, k_s, v_s, k_w, v_w = [kv[:, :, j] for j in range(6)]
    kc = _rmsnorm(_compress(k_c, pe_k, w1_k, w2_k), k_gain)
    vc = _compress(v_c, pe_v, w1_v, w2_v)
    n_cmp = kc.shape[1]
    qg = q.reshape(B, S, G, HPG, d)
    s = jnp.einsum('bsghd,bngd->bsghn', qg, kc).astype(jnp.float32) * scale
    cmp_start = jnp.arange(n_cmp) * CMP_STRIDE
    cm = ((cmp_start + CMP_LEN - 1)[None, :] <= pos[:, None])[None, :, None, None, :]
    p = _masked_softmax(s, cm) * cm
    o_cmp = jnp.einsum('bsghn,bngd->bsghd', p.astype(vc.dtype), vc)
    n_slc = S // SLC_BLK
    slc_start = jnp.arange(n_slc) * SLC_BLK
    ov = jnp.clip(jnp.minimum(cmp_start[:, None] + CMP_LEN, slc_start[None, :] + SLC_BLK)
                  - jnp.maximum(cmp_start[:, None], slc_start[None, :]), 0, None).astype(jnp.float32) / CMP_LEN
    imp = jnp.einsum('bsgn,nk->bsgk', p.sum(axis=3), ov)
    blk_t = pos // SLC_BLK
    kk = jnp.arange(n_slc)[None, :]
    valid = kk <= blk_t[:, None]
    force = (kk == 0) | (kk == blk_t[:, None]) | (kk == blk_t[:, None] - 1)
    imp = jnp.where(force[None, :, None, :], FORCE, jnp.where(valid[None, :, None, :], imp, -FORCE))
    n_top = min(SLC_TOPN, n_slc)
    _, sel = lax.top_k(imp, n_top)
    qr = _rope(q, pos).reshape(B, S, G, HPG, d)
    ks = _rope(_rmsnorm(k_s, k_gain), pos)
    kw = _rope(_rmsnorm(k_w, k_gain), pos)
    o_slc = _selected_attn(qr, ks, v_s, sel, pos, scale)
    o_win = _window_attn(qr, kw, v_w, pos, scale)
    g = jax.nn.sigmoid(gate.astype(jnp.float32)).reshape(B, S, 3, H)
    f32 = jnp.float32
    out = (g[:, :, 0, :, None] * o_cmp.reshape(B, S, H, d).astype(f32)
           + g[:, :, 1, :, None] * o_slc.reshape(B, S, H, d).astype(f32)
           + g[:, :, 2, :, None] * o_win.reshape(B, S, H, d).astype(f32))
    return out.reshape(B, S, H * d).astype(q.dtype)


def _dsa(q, kv, iq, ik, iw, pos, q_gain, k_gain):
    B, S, _ = q.shape
    H, G, d = N_HEADS, N_KV, HEAD_DIM
    scale = d ** -0.5
    q = _rope(_rmsnorm(q.reshape(B, S, H, d), q_gain), pos).reshape(B, S, G, HPG, d)
    kv = kv.reshape(B, S, 2, G, d)
    k = _rope(_rmsnorm(kv[:, :, 0], k_gain), pos)
    v = kv[:, :, 1]
    iq = _rope(iq.reshape(B, S, IDX_HEADS, IDX_DIM), pos)
    ik = _rope(ik[:, :, None, :], pos)[:, :, 0]
    iw = iw.astype(jnp.float32) * (IDX_HEADS ** -0.5 * IDX_DIM ** -0.5)
    k_sel = min(DSA_TOPK, S // 4)
    Q = GATHER_QBLK
    nq = S // Q
    bidx = jnp.arange(B)[:, None, None]
    spos = jnp.arange(S)

    def chunk(t):
        return t.reshape((B, nq, Q) + t.shape[2:]).swapaxes(0, 1)

    def body(args):
        qc, iqc, iwc, tpos = args
        rel = jax.nn.relu(jnp.einsum('bqhe,bse->bqhs', iqc, ik).astype(jnp.float32))
        score = jnp.einsum('bqhs,bqh->bqs', rel, iwc)
        score = jnp.where((spos[None, :] <= tpos[:, None])[None], score, NEG)
        _, idx = lax.top_k(score, k_sel)
        kg = k[bidx, idx]
        vg = v[bidx, idx]
        m = (idx <= tpos[None, :, None])[:, :, None, None, :]
        s = jnp.einsum('bqghd,bqkgd->bqghk', qc, kg).astype(jnp.float32) * scale
        p = _masked_softmax(s, m)
        return jnp.einsum('bqghk,bqkgd->bqghd', p.astype(vg.dtype), vg)

    out = lax.map(body, (chunk(q), chunk(iq), chunk(iw), pos.reshape(nq, Q)))
    return out.swapaxes(0, 1).reshape(B, S, H * d)


def _short_conv(x_in, b_gate, c_gate, w):
    return b_gate * _causal_dwconv(c_gate * x_in, w)


def setup_inputs(seed: int = 0) -> dict:
    key = jax.random.key(seed)
    ks = jax.random.split(key, 24)
    L = DEPTH
    f32 = jnp.float32

    def nrm(k, shape, scale):
        return jax.random.normal(k, shape, f32) * scale

    u = jax.random.uniform(ks[10], (L, GROUP_W), f32, 0.9, 0.999)
    sa = u ** (1.0 / LRU_C)
    lam = jnp.log(sa) - jnp.log1p(-sa)
    blk = GROUP_W // LRU_BLOCKS
    return {
        'x': nrm(ks[0], (BATCH, SEQ, D_MODEL), 1.0),
        'norm_g': 1.0 + nrm(ks[1], (L, D_MODEL), 0.02),
        'w_in': nrm(ks[2], (L, D_MODEL, IN_WIDTH), D_MODEL ** -0.5),
        'w_out': nrm(ks[3], (L, MIX_W, D_MODEL), MIX_W ** -0.5),
        'lru_conv_w': nrm(ks[4], (L, CONV_A, GROUP_W), CONV_A ** -0.5),
        'lru_conv_b': nrm(ks[5], (L, GROUP_W), 0.01),
        'lru_wa': nrm(ks[6], (L, LRU_BLOCKS, blk, blk), blk ** -0.5),
        'lru_ba': nrm(ks[7], (L, GROUP_W), 0.01),
        'lru_wx': nrm(ks[8], (L, LRU_BLOCKS, blk, blk), blk ** -0.5),
        'lru_bx': nrm(ks[9], (L, GROUP_W), 0.01),
        'lru_lambda': lam,
        'nsa_q_gain': 1.0 + nrm(ks[11], (L, HEAD_DIM), 0.02),
        'nsa_k_gain': 1.0 + nrm(ks[12], (L, HEAD_DIM), 0.02),
        'cmp_pe_k': nrm(ks[13], (L, CMP_LEN, HEAD_DIM), 0.1),
        'cmp_w1_k': nrm(ks[14], (L, CMP_LEN, HEAD_DIM, CMP_HID), (CMP_LEN * HEAD_DIM) ** -0.5),
        'cmp_w2_k': nrm(ks[15], (L, CMP_HID, HEAD_DIM), CMP_HID ** -0.5),
        'cmp_pe_v': nrm(ks[16], (L, CMP_LEN, HEAD_DIM), 0.1),
        'cmp_w1_v': nrm(ks[17], (L, CMP_LEN, HEAD_DIM, CMP_HID), (CMP_LEN * HEAD_DIM) ** -0.5),
        'cmp_w2_v': nrm(ks[18], (L, CMP_HID, HEAD_DIM), CMP_HID ** -0.5),
        'dsa_q_gain': 1.0 + nrm(ks[19], (L, HEAD_DIM), 0.02),
        'dsa_k_gain': 1.0 + nrm(ks[20], (L, HEAD_DIM), 0.02),
        'sc_conv_w': nrm(ks[21], (L, CONV_D, GROUP_W), CONV_D ** -0.5),
    }


def reference(x, norm_g, w_in, w_out, lru_conv_w, lru_conv_b, lru_wa, lru_ba, lru_wx, lru_bx, lru_lambda,
              nsa_q_gain, nsa_k_gain, cmp_pe_k, cmp_w1_k, cmp_w2_k, cmp_pe_v, cmp_w1_v, cmp_w2_v,
              dsa_q_gain, dsa_k_gain, sc_conv_w):
    S = x.shape[1]
    pos = jnp.arange(S, dtype=jnp.int32)
    offsets = [int(o) for o in np.cumsum(SPLIT_SIZES)[:-1]]
    for l in range(DEPTH):
        h = _rmsnorm(x, norm_g[l])
        z = h @ w_in[l]
        (a_x, a_g, b_q, b_g, b_kv, b_gate, c_q, c_g, c_kv, c_iq, c_ik, c_iw,
         d_in, d_b, d_c, d_g) = jnp.split(z, offsets, axis=-1)
        y_a = _rglru(a_x, lru_conv_w[l], lru_conv_b[l], lru_wa[l], lru_ba[l], lru_wx[l], lru_bx[l],
                     lru_lambda[l]) * jax.nn.silu(a_g)
        y_b = _nsa(b_q, b_kv, b_gate, pos, nsa_q_gain[l], nsa_k_gain[l], cmp_pe_k[l], cmp_w1_k[l], cmp_w2_k[l],
                   cmp_pe_v[l], cmp_w1_v[l], cmp_w2_v[l]) * jax.nn.silu(b_g)
        y_c = _dsa(c_q, c_kv, c_iq, c_ik, c_iw, pos, dsa_q_gain[l], dsa_k_gain[l]) * jax.nn.silu(c_g)
        y_d = _short_conv(d_in, d_b, d_c, sc_conv_w[l]) * jax.nn.silu(d_g)
        y = jnp.concatenate([y_a, y_b, y_c, y_d], axis=-1)
        x = x + y @ w_out[l]
    return x
```

```python
import numpy as np
import ml_dtypes
import concourse.bass as bass
import concourse.mybir as mybir
from concourse.bass_utils import run_bass_kernel_spmd

F32 = mybir.dt.float32
BF16 = mybir.dt.bfloat16
ALU = mybir.AluOpType
AF = mybir.ActivationFunctionType
AX = mybir.AxisListType
ENGS = ('pe', 'act', 'dve', 'pool', 'sp')
NDS = 40

DM = 4096
SEQ = 4096
TOK = 1024
NIN = 12896
NT = 6240
NF = 6656
XT = 1028
EPS = 1e-6
SCALE = 128 ** -0.5
THETA = 500000.0
NBIS = 22
MBIG = 30720.0
INTERLEAVE_PV = True


class Res:
    __slots__ = ('name', 'w', 'r')

    def __init__(self, name=''):
        self.name = name
        self.w = {}
        self.r = {}


class KB:
    def __init__(self, nc):
        self.nc = nc
        self.ops = {e: [] for e in ENGS}
        self.h = {}
        self.cnt = {}
        for e in ('pe', 'act', 'dve', 'pool'):
            self.h[e] = nc.alloc_semaphore('s_' + e)
            self.cnt[e] = 0
        self.dval = [0] * NDS
        for i in range(NDS):
            self.h['d%d' % i] = nc.alloc_semaphore('sd%d' % i)
        self.dnext = 0
        self.h['cc'] = nc.alloc_semaphore('s_cc')
        self.ccnt = 0
        self.waited = {e: {} for e in ENGS}
        self.pending = {e: False for e in ENGS}
        self.sb_off = (nc.sbuf_base + 63) // 64 * 64
        self.sb_top = nc.sbuf_top
        self.sb_mark = []
        self.nalloc = 0
        self.ninstr = 0
        self.rr = 0
        self.abs = {e: [] for e in ENGS}

    def sb(self, shape, dtype, name=None):
        nbytes = int(np.prod(shape[1:])) * mybir.dt.size(dtype)
        nbytes = (nbytes + 63) // 64 * 64
        off = self.sb_off
        assert off + nbytes <= self.sb_top, ('sbuf overflow', name, off, nbytes, self.sb_top)
        self.sb_off += nbytes
        self.nalloc += 1
        return self.nc.alloc_sbuf_tensor_at('%s_%d' % (name or 't', self.nalloc), list(shape), dtype, offset=off)

    def push(self):
        self.sb_mark.append(self.sb_off)

    def pop(self):
        self.sb_off = self.sb_mark.pop()

    def _deps(self, eng, reads, writes):
        need = {}
        for r in reads:
            for k, v in r.w.items():
                if need.get(k, 0) < v:
                    need[k] = v
        for w in writes:
            for k, v in w.w.items():
                if need.get(k, 0) < v:
                    need[k] = v
            for k, v in w.r.items():
                if need.get(k, 0) < v:
                    need[k] = v
        wl = []
        wd = self.waited[eng]
        for k, v in need.items():
            if eng == 'pe' and k == 'pe':
                continue
            if wd.get(k, 0) < v:
                wd[k] = v
                wl.append((k, v))
        return wl

    def emit(self, eng, fn, reads=(), writes=(), inc=True):
        wl = self._deps(eng, reads, writes)
        if inc:
            self.cnt[eng] += 1
            ev = self.cnt[eng]
            self.pending[eng] = False
        else:
            ev = self.cnt[eng] + 1
            self.pending[eng] = True
        sem = self.h[eng]
        hh = self.h

        def thunk(e):
            for k, v in wl:
                e.wait_ge(hh[k], v)
            ins = fn(e)
            if inc:
                ins.then_inc(sem, 1)
        self.ops[eng].append(thunk)
        self.abs[eng].append((wl, (eng, 1) if inc else None))
        self.ninstr += 1
        for r in reads:
            if r.r.get(eng, 0) < ev:
                r.r[eng] = ev
        for w in writes:
            if w.w.get(eng, 0) < ev:
                w.w[eng] = ev

    def dma(self, q, out, in_, reads=(), writes=(), **kw):
        wl = self._deps(q, reads, writes)
        s = self.dnext
        self.dnext = (s + 1) % NDS
        key = 'd%d' % s
        prev = self.dval[s]
        if prev and self.waited[q].get(key, 0) < prev:
            wl.append((key, prev))
            self.waited[q][key] = prev
        ev = prev + 16
        self.dval[s] = ev
        hh = self.h

        def thunk(e):
            for k, v in wl:
                e.wait_ge(hh[k], v)
            e.dma_start(out=out, in_=in_, **kw).then_inc(hh[key], 16)
        self.ops[q].append(thunk)
        self.abs[q].append((wl, (key, 16)))
        self.ninstr += 1
        for r in reads:
            if r.r.get(key, 0) < ev:
                r.r[key] = ev
        for w in writes:
            if w.w.get(key, 0) < ev:
                w.w[key] = ev

    def allgather(self, src, dst, reads=(), writes=()):
        wl = self._deps('pool', reads, writes)
        self.ccnt += 1
        ev = self.ccnt
        hh = self.h

        def thunk(e):
            for k, v in wl:
                e.wait_ge(hh[k], v)
            e.collective_compute('AllGather', ALU.bypass, replica_groups=[[0, 1, 2, 3], [4, 5, 6, 7]],
                                 ins=[src.opt()], outs=[dst.opt()]).then_inc(hh['cc'])
            e.wait_ge(hh['cc'], ev)
        self.ops['pool'].append(thunk)
        self.abs['pool'].append((wl, ('cc', 1)))
        self.waited['pool']['cc'] = ev
        for r in reads:
            r.r['cc'] = ev
        for w in writes:
            w.w['cc'] = ev

    def barrier(self, engines=ENGS, include_cc=False):
        for e in ENGS:
            assert not self.pending[e], e
        allev = {k: v for k, v in self.cnt.items() if v > 0}
        if self.ccnt and include_cc:
            allev['cc'] = self.ccnt
        for i in range(NDS):
            if self.dval[i]:
                allev['d%d' % i] = self.dval[i]
        hh = self.h
        for e in engines:
            wl = []
            wd = self.waited[e]
            for k, v in allev.items():
                if wd.get(k, 0) < v:
                    wd[k] = v
                    wl.append((k, v))
            if wl:
                def thunk(en, wl=wl):
                    for k, v in wl:
                        en.wait_ge(hh[k], v)
                self.ops[e].append(thunk)
                self.abs[e].append((wl, None))

    def finish(self):
        self.barrier(include_cc=True)
        ops = self.ops
        with self.nc.Block() as block:
            @block.tensor
            def _(e):
                for t in ops['pe']:
                    t(e)

            @block.scalar
            def _(e):
                for t in ops['act']:
                    t(e)

            @block.vector
            def _(e):
                for t in ops['dve']:
                    t(e)

            @block.gpsimd
            def _(e):
                for t in ops['pool']:
                    t(e)

            @block.sync
            def _(e):
                for t in ops['sp']:
                    t(e)

    def mm(self, out, lhsT, rhs, start, stop, reads, writes, inc=True):
        self.emit('pe', lambda e: e.matmul(out, lhsT=lhsT, rhs=rhs, start=start, stop=stop), reads, writes, inc=inc)

    def tr(self, out, in_, ident, reads, writes, inc=True):
        self.emit('pe', lambda e: e.transpose(out, in_, ident), reads, writes, inc=inc)

    def act(self, out, in_, func, reads, writes, **kw):
        self.emit('act', lambda e: e.activation(out=out, in_=in_, func=func, **kw), reads, writes)

    def ts(self, out, in0, s1, s2, op0, op1, reads, writes, eng='dve', accum_out=None):
        if op1 is None:
            self.emit(eng, lambda e: e.tensor_scalar(out=out, in0=in0, scalar1=s1, scalar2=None, op0=op0), reads, writes)
        elif accum_out is None:
            self.emit(eng, lambda e: e.tensor_scalar(out=out, in0=in0, scalar1=s1, scalar2=s2, op0=op0, op1=op1), reads, writes)
        else:
            self.emit(eng, lambda e: e.tensor_scalar(out=out, in0=in0, scalar1=s1, scalar2=s2, op0=op0, op1=op1,
                                                     accum_out=accum_out), reads, writes)

    def tt(self, out, in0, in1, op, reads, writes, eng='dve'):
        self.emit(eng, lambda e: e.tensor_tensor(out=out, in0=in0, in1=in1, op=op), reads, writes)

    def stt(self, out, in0, scalar, in1, op0, op1, reads, writes, accum_out=None):
        if accum_out is None:
            self.emit('dve', lambda e: e.scalar_tensor_tensor(out=out, in0=in0, scalar=scalar, in1=in1, op0=op0, op1=op1),
                      reads, writes)
        else:
            self.emit('dve', lambda e: e.scalar_tensor_tensor(out=out, in0=in0, scalar=scalar, in1=in1, op0=op0, op1=op1,
                                                              accum_out=accum_out), reads, writes)

    def copy(self, eng, out, in_, reads, writes):
        if eng == 'act':
            self.emit('act', lambda e: e.activation(out=out, in_=in_, func=AF.Copy), reads, writes)
        else:
            self.emit(eng, lambda e: e.tensor_copy(out=out, in_=in_), reads, writes)

    def cpalt(self, out, in_, reads, writes):
        self.rr += 1
        self.copy('act' if self.rr % 2 else 'dve', out, in_, reads, writes)

    def memset(self, eng, ap, val, writes):
        self.emit(eng, lambda e: e.memset(ap, val), (), writes)

    def recip(self, out, in_, reads, writes):
        self.emit('dve', lambda e: e.reciprocal(out=out, in_=in_), reads, writes)

    def reduce(self, out, in_, op, reads, writes, axis=None):
        ax = axis or AX.X
        self.emit('dve', lambda e: e.tensor_reduce(out=out, in_=in_, axis=ax, op=op), reads, writes)


class PS:
    def __init__(self, nc):
        self.t = [nc.alloc_psum_tensor('pp%d' % i, [128, 1024], F32) for i in range(4)]
        self.r = [Res('bank%d' % i) for i in range(8)]

    def bank(self, i):
        return self.t[i // 2][:, (i % 2) * 512:(i % 2) * 512 + 512], self.r[i]

    def bank_bf(self, i):
        v = self.t[i // 2][:].bitcast(BF16)
        return v[:, (i % 2) * 1024:(i % 2) * 1024 + 1024], self.r[i]

    def pair(self, i):
        return self.t[i][:], [self.r[2 * i], self.r[2 * i + 1]]


def stage_norm_xT(kb, ps, G, x_ap, xh_ap, ng_ap):
    xT = G['xT']
    R_xT = G['R_xT']
    kb.push()
    g_bc = kb.sb([128, DM], F32, 'gbc')
    R_g = Res()
    kb.dma('sp', g_bc[:], ng_ap.partition_broadcast(128), writes=[R_g])
    xt = [kb.sb([128, DM], F32, 'xt') for _ in range(2)]
    hb = [kb.sb([128, DM], BF16, 'hb') for _ in range(2)]
    sm = kb.sb([128, 8], F32, 'sm')
    R_xt = [Res(), Res()]
    R_hb = [Res(), Res()]
    R_sm = [Res(), Res()]
    nb = 0
    for i in range(9):
        b = i % 2
        if i < 8:
            npt, src, tok0 = 128, x_ap[i * 128:(i + 1) * 128, :], 4 + i * 128
        else:
            npt, src, tok0 = 4, xh_ap, 0
        kb.dma('sp', xt[b][:npt], src, writes=[R_xt[b]])
        c0 = b * 4
        kb.act(hb[b][:npt], xt[b][:npt], AF.Square, [R_xt[b]], [R_hb[b], R_sm[b]], accum_out=sm[:npt, c0:c0 + 1])
        kb.ts(sm[:npt, c0 + 1:c0 + 2], sm[:npt, c0:c0 + 1], 1.0 / DM, EPS, ALU.mult, ALU.add, [R_sm[b]], [R_sm[b]])
        kb.act(sm[:npt, c0 + 2:c0 + 3], sm[:npt, c0 + 1:c0 + 2], AF.Sqrt, [R_sm[b]], [R_sm[b]])
        kb.recip(sm[:npt, c0 + 3:c0 + 4], sm[:npt, c0 + 2:c0 + 3], [R_sm[b]], [R_sm[b]])
        kb.stt(hb[b][:npt], xt[b][:npt], sm[:npt, c0 + 3:c0 + 4], g_bc[:npt], ALU.mult, ALU.mult,
               [R_xt[b], R_sm[b], R_g], [R_hb[b]])
        for cg in range(4):
            pb, R_pb = ps.bank_bf(6 + nb % 2)
            nb += 1
            for k in range(8):
                c = cg * 8 + k
                kb.tr(pb[:, k * 128:k * 128 + npt], hb[b][:npt, c * 128:(c + 1) * 128], G['identb'][:npt, :npt],
                      [R_hb[b], G['R_c']], [R_pb], inc=(k == 7))
            src_v = pb.rearrange('p (k t) -> p k t', k=8)[:, :, :npt]
            kb.cpalt(xT[:, cg * 8:(cg + 1) * 8, tok0:tok0 + npt], src_v, [R_pb], [R_xT])
    kb.barrier()
    kb.pop()


def stage_inproj(kb, ps, G, w_ap, jobs, zt, zf):
    xT = G['xT']
    R_xT = G['R_xT']
    kb.push()
    wv = w_ap.rearrange('(c p) n -> p c n', p=128)
    wb = [kb.sb([128, 32, 512], BF16, 'wb') for _ in range(2)]
    R_wb = [Res(), Res()]
    zst = [kb.sb([128, 512], F32, 'zst') for _ in range(4)]
    R_zst = [Res() for _ in range(4)]
    nb = 0
    nz = 0
    for ji, (orient, c0, w, d0, halo) in enumerate(jobs):
        b = ji % 2
        for q in range(4):
            kb.dma('pool', wb[b][:, q * 8:(q + 1) * 8, :w], wv[:, q * 8:(q + 1) * 8, c0:c0 + w], writes=[R_wb[b]])
        if orient == 'T':
            for tt_ in range(8):
                pb, R_pb = ps.bank(nb % 4)
                nb += 1
                for c in range(32):
                    kb.mm(pb[:, :w], xT[:, c, 4 + tt_ * 128:4 + (tt_ + 1) * 128], wb[b][:, c, :w], c == 0, c == 31,
                          [R_xT, R_wb[b]], [R_pb], inc=(c == 31))
                zs, R_zs = zst[nz % 4], R_zst[nz % 4]
                nz += 1
                kb.cpalt(zs[:, :w], pb[:, :w], [R_pb], [R_zs])
                kb.dma('sp', zt[tt_ * 128:(tt_ + 1) * 128, d0:d0 + w], zs[:, :w], reads=[R_zs], writes=[G['R_zt']])
        else:
            blocks = [(4, 512), (516, 512)]
            if halo:
                blocks = [(0, 4)] + blocks
            for j in range(w // 128):
                for (t0, n) in blocks:
                    pb, R_pb = ps.bank(nb % 4)
                    nb += 1
                    for c in range(32):
                        kb.mm(pb[:, :n], wb[b][:, c, j * 128:(j + 1) * 128], xT[:, c, t0:t0 + n], c == 0, c == 31,
                              [R_xT, R_wb[b]], [R_pb], inc=(c == 31))
                    zs, R_zs = zst[nz % 4], R_zst[nz % 4]
                    nz += 1
                    kb.cpalt(zs[:, :n], pb[:, :n], [R_pb], [R_zs])
                    kb.dma('sp', zf[d0 + j * 128:d0 + (j + 1) * 128, t0:t0 + n], zs[:, :n], reads=[R_zs],
                           writes=[G['R_zf']])
    kb.barrier()
    kb.pop()


def stage_lru(kb, ps, G, zf, P, mode, yT=None, lru_out=None, store=None, pre=None, R_lout=None):
    kb.push()
    R_p = Res()
    cw = kb.sb([128, 8, 4], F32, 'cw')
    vec = kb.sb([128, 8, 4], F32, 'vec')
    kb.dma('sp', cw[:], P['lru_cw'], writes=[R_p])
    kb.dma('sp', vec[:], P['lru_vec'], writes=[R_p])
    if pre is None:
        wa = kb.sb([128, 8, 128], BF16, 'wa')
        wx = kb.sb([128, 8, 128], BF16, 'wx')
        kb.dma('pool', wa[:], P['lru_wa'].rearrange('h i j -> i h j'), writes=[R_p])
        kb.dma('pool', wx[:], P['lru_wx'].rearrange('h i j -> i h j'), writes=[R_p])
        R_w = R_p
    else:
        wa, wx, R_w = pre
    cc = kb.sb([128, 8], F32, 'cc')
    sx = kb.sb([128, 8], F32, 'sx')
    s2 = kb.sb([128, 8], F32, 's2')
    pp_ = kb.sb([128, 8], F32, 'ppoly')
    kb.act(sx[:], vec[:, :, 3], AF.Exp, [R_p], [R_p], scale=-1.0)
    kb.ts(s2[:], sx[:], 2.0, None, ALU.add, None, [R_p], [R_p])
    kb.recip(s2[:], s2[:], [R_p], [R_p])
    kb.tt(sx[:], sx[:], s2[:], ALU.mult, [R_p], [R_p])
    kb.tt(s2[:], sx[:], sx[:], ALU.mult, [R_p], [R_p])
    kb.ts(pp_[:], s2[:], 1.0 / 9, 1.0 / 7, ALU.mult, ALU.add, [R_p], [R_p])
    for cst in (1.0 / 5, 1.0 / 3, 1.0):
        kb.tt(pp_[:], pp_[:], s2[:], ALU.mult, [R_p], [R_p])
        kb.ts(pp_[:], pp_[:], cst, None, ALU.add, None, [R_p], [R_p])
    kb.tt(cc[:], pp_[:], sx[:], ALU.mult, [R_p], [R_p])
    kb.ts(cc[:], cc[:], -16.0, None, ALU.mult, None, [R_p], [R_p])
    zeros = kb.sb([128, 1024], F32, 'zeros')
    kb.memset('dve', zeros[:], 0.0, [R_p])
    ends = kb.sb([128, 8, 2], F32, 'ends')
    R_ends = Res()
    hin = None
    if mode == 'M':
        la = kb.sb([128, 4, 8, 2], F32, 'la')
        ml = kb.sb([128, 4], F32, 'ml')
        kb.dma('sp', la[:].rearrange('p i c k -> p i (c k)'), P['lru_all'], reads=[G['R_in']], writes=[R_p])
        kb.dma('sp', ml[:], P['mlt'], writes=[R_p])
        hin = kb.sb([128, 8], F32, 'hin')
        tmp = kb.sb([128, 8], F32, 'hint')
        kb.memset('dve', hin[:], 0.0, [R_p])
        for i in range(3):
            kb.tt(tmp[:], la[:, i, :, 1], hin[:], ALU.mult, [R_p], [R_p])
            kb.tt(tmp[:], tmp[:], la[:, i, :, 0], ALU.add, [R_p], [R_p])
            kb.tt(tmp[:], tmp[:], hin[:], ALU.subtract, [R_p], [R_p])
            kb.stt(hin[:], tmp[:], ml[:, i:i + 1], hin[:], ALU.mult, ALU.add, [R_p], [R_p])
    S = []
    for s_ in range(2):
        d = {}
        d['ax'] = kb.sb([128, XT], F32, 'ax')
        for n in ('u', 'r', 'i', 'a', 't1', 't2', 'hl', 'Ac'):
            d[n] = kb.sb([128, 1024], F32, n)
        d['ub'] = kb.sb([128, 1024], BF16, 'ub')
        if mode == 'M':
            d['ag'] = kb.sb([128, 1024], F32, 'ag')
            d['yb'] = kb.sb([128, 1024], BF16, 'yb')
        d['R'] = {n: Res() for n in ('ax', 'u', 'r', 'i', 'a', 't1', 't2', 'hl', 'Ac', 'ub', 'ag', 'yb')}
        S.append(d)
    for ct in range(8):
        d = S[ct % 2]
        R = d['R']
        kb.dma('sp', d['ax'][:], zf[ct * 128:(ct + 1) * 128, :], reads=[G['R_zf']], writes=[R['ax']])
        if mode == 'M':
            kb.dma('sp', d['ag'][:], zf[1024 + ct * 128:1024 + (ct + 1) * 128, 4:XT], reads=[G['R_zf']], writes=[R['ag']])
        u = d['u']
        kb.ts(u[:], d['ax'][:, 1:1025], cw[:, ct, 0:1], vec[:, ct, 0:1], ALU.mult, ALU.add, [R['ax'], R_p], [R['u']])
        for k in range(1, 4):
            kb.stt(u[:], d['ax'][:, 1 + k:1025 + k], cw[:, ct, k:k + 1], u[:], ALU.mult, ALU.add, [R['ax'], R_p, R['u']],
                   [R['u']])
        kb.copy('act', d['ub'][:], u[:], [R['u']], [R['ub']])
        pr, R_pr = ps.pair(0)
        pi, R_pi = ps.pair(1)
        for hf in range(2):
            kb.mm(pr[:, hf * 512:(hf + 1) * 512], wa[:, ct, :], d['ub'][:, hf * 512:(hf + 1) * 512], True, True,
                  [R['ub'], R_w], [R_pr[hf]])
            kb.mm(pi[:, hf * 512:(hf + 1) * 512], wx[:, ct, :], d['ub'][:, hf * 512:(hf + 1) * 512], True, True,
                  [R['ub'], R_w], [R_pi[hf]])
        kb.act(d['r'][:], pr, AF.Sigmoid, R_pr + [R_p], [R['r']], bias=vec[:, ct, 1:2])
        kb.act(d['i'][:], pi, AF.Sigmoid, R_pi + [R_p], [R['i']], bias=vec[:, ct, 2:3])
        kb.act(d['a'][:], d['r'][:], AF.Exp, [R['r'], R_p], [R['a']], scale=cc[:, ct:ct + 1])
        kb.act(d['t1'][:], d['a'][:], AF.Square, [R['a']], [R['t1']])
        kb.ts(d['t1'][:], d['t1'][:], -1.0, 1.0, ALU.mult, ALU.add, [R['t1']], [R['t1']])
        kb.act(d['t1'][:], d['t1'][:], AF.Sqrt, [R['t1']], [R['t1']])
        kb.tt(d['t2'][:], d['i'][:], u[:], ALU.mult, [R['i'], R['u']], [R['t2']])
        kb.tt(d['t2'][:], d['t2'][:], d['t1'][:], ALU.mult, [R['t2'], R['t1']], [R['t2']])
        a_, b_, hl_, ac_ = d['a'], d['t2'], d['hl'], d['Ac']
        kb.emit('dve', lambda e, a_=a_, b_=b_, hl_=hl_: e.tensor_tensor_scan(out=hl_[:], data0=a_[:], data1=b_[:],
                                                                          initial=0.0, op0=ALU.mult, op1=ALU.add),
                [R['a'], R['t2']], [R['hl']])
        kb.emit('dve', lambda e, a_=a_, ac_=ac_: e.tensor_tensor_scan(out=ac_[:], data0=a_[:], data1=zeros[:],
                                                                      initial=1.0, op0=ALU.mult, op1=ALU.add),
                [R['a'], R_p], [R['Ac']])
        if mode == 'K':
            kb.copy('dve', ends[:, ct, 0:1], d['hl'][:, 1023:1024], [R['hl']], [R_ends])
            kb.copy('dve', ends[:, ct, 1:2], d['Ac'][:, 1023:1024], [R['Ac']], [R_ends])
            if store is not None:
                kb.dma('sp', store[0][ct * 128:(ct + 1) * 128, :], d['hl'][:], reads=[R['hl']], writes=[G['R_lst']])
                kb.dma('sp', store[1][ct * 128:(ct + 1) * 128, :], d['Ac'][:], reads=[R['Ac']], writes=[G['R_lst']])
        else:
            kb.stt(d['hl'][:], d['Ac'][:], hin[:, ct:ct + 1], d['hl'][:], ALU.mult, ALU.add, [R['Ac'], R['hl'], R_p],
                   [R['hl']])
            kb.act(d['ag'][:], d['ag'][:], AF.Silu, [R['ag']], [R['ag']])
            kb.tt(d['yb'][:], d['hl'][:], d['ag'][:], ALU.mult, [R['hl'], R['ag']], [R['yb']])
            kb.dma('sp', yT[ct * 128:(ct + 1) * 128, :], d['yb'][:], reads=[R['yb']], writes=[G['R_yT']])
    if mode == 'K':
        kb.dma('sp', lru_out, ends[:].rearrange('p c k -> p (c k)'), reads=[R_ends], writes=[G['R_out']] + ([R_lout] if R_lout else []))
    kb.barrier()
    kb.pop()


def stage_lru_fin(kb, ps, G, zf, P, yT, store):
    kb.push()
    R_p = Res()
    la = kb.sb([128, 4, 8, 2], F32, 'la')
    ml = kb.sb([128, 4], F32, 'ml')
    kb.dma('sp', la[:].rearrange('p i c k -> p i (c k)'), P['lru_all'], reads=[G['R_in_lru']], writes=[R_p])
    kb.dma('sp', ml[:], P['mlt'], writes=[R_p])
    hin = kb.sb([128, 8], F32, 'hin')
    tmp = kb.sb([128, 8], F32, 'hint')
    kb.memset('dve', hin[:], 0.0, [R_p])
    for i in range(3):
        kb.tt(tmp[:], la[:, i, :, 1], hin[:], ALU.mult, [R_p], [R_p])
        kb.tt(tmp[:], tmp[:], la[:, i, :, 0], ALU.add, [R_p], [R_p])
        kb.tt(tmp[:], tmp[:], hin[:], ALU.subtract, [R_p], [R_p])
        kb.stt(hin[:], tmp[:], ml[:, i:i + 1], hin[:], ALU.mult, ALU.add, [R_p], [R_p])
    S = []
    for s_ in range(3):
        d = {'hl': kb.sb([128, 1024], F32, 'hl'), 'Ac': kb.sb([128, 1024], F32, 'Ac'), 'ag': kb.sb([128, 1024], F32, 'ag'),
             'yb': kb.sb([128, 1024], BF16, 'yb')}
        d['R'] = {n: Res() for n in ('hl', 'Ac', 'ag', 'yb')}
        S.append(d)
    for ct in range(8):
        d = S[ct % 3]
        R = d['R']
        rs = slice(ct * 128, (ct + 1) * 128)
        kb.dma('sp', d['hl'][:], store[0][rs, :], reads=[G['R_lst']], writes=[R['hl']])
        kb.dma('sp', d['Ac'][:], store[1][rs, :], reads=[G['R_lst']], writes=[R['Ac']])
        kb.dma('sp', d['ag'][:], zf[1024 + ct * 128:1024 + (ct + 1) * 128, 4:XT], reads=[G['R_zf']], writes=[R['ag']])
        kb.stt(d['hl'][:], d['Ac'][:], hin[:, ct:ct + 1], d['hl'][:], ALU.mult, ALU.add, [R['Ac'], R['hl'], R_p], [R['hl']])
        kb.act(d['ag'][:], d['ag'][:], AF.Silu, [R['ag']], [R['ag']])
        kb.tt(d['yb'][:], d['hl'][:], d['ag'][:], ALU.mult, [R['hl'], R['ag']], [R['yb']])
        kb.dma('sp', yT[ct * 128:(ct + 1) * 128, :], d['yb'][:], reads=[R['yb']], writes=[G['R_yT']])
    kb.barrier()
    kb.pop()


def stage_sconv(kb, ps, G, zf, P, yT):
    kb.push()
    R_p = Res()
    sw = kb.sb([128, 8, 3], F32, 'scw')
    kb.dma('sp', sw[:], P['sc_w'], writes=[R_p])
    S = []
    for s_ in range(2):
        d = {'din': kb.sb([128, XT], F32, 'din'), 'dc': kb.sb([128, XT], F32, 'dc'),
             'db': kb.sb([128, 1024], F32, 'db'), 'dg': kb.sb([128, 1024], F32, 'dg'),
             'cv': kb.sb([128, 1024], F32, 'cv'), 'yb': kb.sb([128, 1024], BF16, 'yb')}
        d['R'] = {n: Res() for n in ('din', 'dc', 'db', 'dg', 'cv', 'yb')}
        S.append(d)
    for ct in range(8):
        d = S[ct % 2]
        R = d['R']
        r0 = 2560 + ct * 128
        kb.dma('sp', d['din'][:], zf[r0:r0 + 128, :], reads=[G['R_zf']], writes=[R['din']])
        kb.dma('sp', d['db'][:], zf[r0 + 1024:r0 + 1152, 4:XT], reads=[G['R_zf']], writes=[R['db']])
        kb.dma('sp', d['dc'][:], zf[r0 + 2048:r0 + 2176, :], reads=[G['R_zf']], writes=[R['dc']])
        kb.dma('sp', d['dg'][:], zf[r0 + 3072:r0 + 3200, 4:XT], reads=[G['R_zf']], writes=[R['dg']])
        kb.tt(d['din'][:], d['din'][:], d['dc'][:], ALU.mult, [R['din'], R['dc']], [R['din']])
        m = d['din']
        kb.ts(d['cv'][:], m[:, 2:1026], sw[:, ct, 0:1], None, ALU.mult, None, [R['din'], R_p], [R['cv']])
        for k in range(1, 3):
            kb.stt(d['cv'][:], m[:, 2 + k:1026 + k], sw[:, ct, k:k + 1], d['cv'][:], ALU.mult, ALU.add,
                   [R['din'], R_p, R['cv']], [R['cv']])
        kb.act(d['dg'][:], d['dg'][:], AF.Silu, [R['dg']], [R['dg']])
        kb.tt(d['cv'][:], d['cv'][:], d['db'][:], ALU.mult, [R['cv'], R['db']], [R['cv']])
        kb.tt(d['yb'][:], d['cv'][:], d['dg'][:], ALU.mult, [R['cv'], R['dg']], [R['yb']])
        kb.dma('sp', yT[3072 + ct * 128:3072 + (ct + 1) * 128, :], d['yb'][:], reads=[R['yb']], writes=[G['R_yT']])
    kb.barrier()
    kb.pop()


def rms_heads(kb, X, nh, gain_bc, sq, ss, R_X, R_g, R_tmp, d=128, Xg=None):
    if Xg is None:
        Xg = X
    kb.tt(sq, X, X, ALU.mult, [R_X], [R_tmp])
    kb.reduce(ss[:, 0:nh], sq, ALU.add, [R_tmp], [R_tmp])
    kb.ts(ss[:, 0:nh], ss[:, 0:nh], 1.0 / d, EPS, ALU.mult, ALU.add, [R_tmp], [R_tmp])
    kb.act(ss[:, 0:nh], ss[:, 0:nh], AF.Sqrt, [R_tmp], [R_tmp])
    kb.recip(ss[:, 0:nh], ss[:, 0:nh], [R_tmp], [R_tmp])
    kb.tt(X, X, ss[:, 0:nh].unsqueeze(2).to_broadcast([128, nh, d]), ALU.mult, [R_X, R_tmp], [R_X])
    kb.tt(Xg, Xg, gain_bc, ALU.mult, [R_X, R_g], [R_X])


def rope_heads(kb, X, Xo, nh, cos, sin, tmp, half, R_X, R_Xo, R_c, R_tmp):
    x1 = X[:, :, 0:half]
    x2 = X[:, :, half:2 * half]
    cb = cos.unsqueeze(1).to_broadcast([128, nh, half])
    sb_ = sin.unsqueeze(1).to_broadcast([128, nh, half])
    kb.tt(tmp[:, 0], x1, cb, ALU.mult, [R_X, R_c], [R_tmp])
    kb.tt(tmp[:, 1], x2, sb_, ALU.mult, [R_X, R_c], [R_tmp])
    kb.tt(tmp[:, 2], x2, cb, ALU.mult, [R_X, R_c], [R_tmp])
    kb.tt(tmp[:, 3], x1, sb_, ALU.mult, [R_X, R_c], [R_tmp])
    kb.tt(Xo[:, :, 0:half], tmp[:, 0], tmp[:, 1], ALU.subtract, [R_tmp], [R_Xo])
    kb.tt(Xo[:, :, half:2 * half], tmp[:, 2], tmp[:, 3], ALU.add, [R_tmp], [R_Xo])


def stage_kvprep(kb, ps, G, zt, zf, P, O):
    kb.push()
    R_p = Res()
    C = G['C']
    R_c = G['R_c']
    gk = kb.sb([128, 3, 128], F32, 'gk')
    kb.dma('sp', gk[:], P['kgain3'].partition_broadcast(128), writes=[R_p])
    kTst = kb.sb([128, 6, 1024], BF16, 'kTst')
    ikTst = kb.sb([64, 1024], BF16, 'ikTst')
    R_kTst = Res()
    R_ikTst = Res()
    S = []
    for s_ in range(2):
        d = {'k': kb.sb([128, 6, 128], F32, 'kall'), 'v': kb.sb([128, 6, 128], F32, 'vall'),
             'ik': kb.sb([128, 1, 64], F32, 'ik'), 'sq': kb.sb([128, 6, 128], F32, 'sq'),
             'ss': kb.sb([128, 8], F32, 'ss'), 'tmp': kb.sb([128, 4, 6, 16], F32, 'tmp'),
             'kb': kb.sb([128, 6, 128], BF16, 'kb16'), 'vb': kb.sb([128, 6, 128], BF16, 'vb16'),
             'ikb': kb.sb([128, 1, 64], BF16, 'ikb'), 'tmpi': kb.sb([128, 4, 1, 8], F32, 'tmpi')}
        d['R'] = {n: Res() for n in ('k', 'v', 'ik', 'tmp', 'kb', 'vb', 'ikb')}
        S.append(d)
    for t_ in range(8):
        d = S[t_ % 2]
        R = d['R']
        rows = slice(t_ * 128, (t_ + 1) * 128)
        kv6 = d['k'][:].rearrange('p (a g) d -> p a (g d)', a=3)
        vv6 = d['v'][:].rearrange('p (a g) d -> p a (g d)', a=3)
        for a, (kc, vc) in enumerate(((2048, 2304), (2560, 2816), (5144, 5400))):
            kb.dma('sp', kv6[:, a, :], zt[rows, kc:kc + 256], reads=[G['R_zt']], writes=[R['k']])
            kb.dma('sp', vv6[:, a, :], zt[rows, vc:vc + 256], reads=[G['R_zt']], writes=[R['v']])
        kb.dma('sp', d['ik'][:, 0, :], zt[rows, 6168:6232], reads=[G['R_zt']], writes=[R['ik']])
        gk_b = gk[:].unsqueeze(2).to_broadcast([128, 3, 2, 128])
        rms_heads(kb, d['k'][:], 6, gk_b, d['sq'][:], d['ss'][:], R['k'], R_p, R['tmp'],
                  Xg=d['k'][:].rearrange('p (a g) d -> p a g d', a=3))
        kb.copy('act', d['kb'][:], d['k'][:], [R['k']], [R['kb']])
        rope_heads(kb, d['k'][:], d['kb'][:], 6, C['cosq'][:, t_, :], C['sinq'][:, t_, :], d['tmp'][:], 16, R['k'],
                   R['kb'], R_c, R['tmp'])
        pb, R_pb = ps.bank_bf(6 + t_ % 2)
        for k in range(6):
            kb.tr(pb[:, k * 128:(k + 1) * 128], d['kb'][:, k, :], G['identb'][:], [R['kb'], R_c], [R_pb], inc=(k == 5))
        kb.cpalt(kTst[:, :, rows], pb[:, 0:768].rearrange('p (k t) -> p k t', k=6), [R_pb], [R_kTst])
        kb.copy('act', d['vb'][:], d['v'][:], [R['v']], [R['vb']])
        for a in range(3):
            kb.dma('sp', O['V'][a][rows, :], d['vb'][:, 2 * a:2 * a + 2, :].rearrange('p g d -> p (g d)'), reads=[R['vb']],
                   writes=[G['R_out']])
        kb.copy('act', d['ikb'][:], d['ik'][:], [R['ik']], [R['ikb']])
        rope_heads(kb, d['ik'][:], d['ikb'][:], 1, C['cosi'][:, t_, :], C['sini'][:, t_, :], d['tmpi'][:], 8, R['ik'],
                   R['ikb'], R_c, R['tmp'])
        pb2, R_pb2 = ps.bank_bf(4 + t_ % 2)
        kb.tr(pb2[0:64, 0:128], d['ikb'][:, 0, :], G['identb'][:], [R['ikb'], R_c], [R_pb2])
        kb.cpalt(ikTst[:, rows], pb2[0:64, 0:128], [R_pb2], [R_ikTst])
    for a in range(3):
        kb.dma('sp', O['kT'][a].rearrange('(g d) t -> d g t', d=128), kTst[:, 2 * a:2 * a + 2, :], reads=[R_kTst],
               writes=[G['R_out']])
    kb.dma('sp', O['ikT'], ikTst[:], reads=[R_ikTst], writes=[G['R_out']])
    kb.barrier()
    peT = kb.sb([128, 2, 32], F32, 'peT')
    kb.dma('sp', peT[:], P['cmp_peT'], writes=[R_p])
    w1 = kb.sb([128, 2, 32, 128], BF16, 'w1')
    kb.dma('pool', w1[:, 0], P['cmp_w1_k'].rearrange('l d e -> d l e'), writes=[R_p])
    kb.dma('pool', w1[:, 1], P['cmp_w1_v'].rearrange('l d e -> d l e'), writes=[R_p])
    hst = kb.sb([128, 8, 64], F32, 'hst')
    R_hst = Res()
    kc = [kb.sb([128, XT], F32, 'kcT') for _ in range(2)]
    kpe = [kb.sb([128, 2, 64, 16], BF16, 'kpe') for _ in range(2)]
    R_kc = [Res(), Res()]
    R_kpe = [Res(), Res()]
    for kv in range(2):
        for g in range(2):
            i = kv * 2 + g
            b = i % 2
            kb.dma('sp', kc[b][:], zf[2048 + i * 128:2048 + (i + 1) * 128, :], reads=[G['R_zf']], writes=[R_kc[b]])
            kv3 = kc[b][:, 4:XT].rearrange('p (c l) -> p c l', l=16)
            for hf in range(2):
                kb.tt(kpe[b][:, hf], kv3, peT[:, kv, hf * 16:(hf + 1) * 16].unsqueeze(1).to_broadcast([128, 64, 16]),
                      ALU.add, [R_kc[b], R_p], [R_kpe[b]])
            for hf in range(2):
                pb, R_pb = ps.bank((i * 2 + hf) % 4)
                for l in range(16):
                    kb.mm(pb[:, 0:64], w1[:, kv, hf * 16 + l, :], kpe[b][:, hf, :, l], l == 0, l == 15,
                          [R_kpe[b], R_p], [R_pb], inc=(l == 15))
                kb.cpalt(hst[:, i * 2 + hf, :], pb[:, 0:64], [R_pb], [R_hst])
    kb.dma('sp', O['hsT'].rearrange('(a e) c -> e a c', e=128), hst[:], reads=[R_hst], writes=[G['R_out']])
    kb.barrier()
    kb.pop()


def attn_dense(kb, ps, G, A, qT, R_q, kv_fn, mask_fn, acc, R_acc, coef_fn, first, kts_fn=None):
    PT, Eb, R_PT, R_Eb, sm, R_smx = (A[k] for k in ('PT', 'Eb', 'R_PT', 'R_Eb', 'sm', 'R_sm'))
    st = {'ne': 0, 'npt': 0, 'nst': 0}

    class PV:
        def __init__(self, pt, R_pt, V1, R_V1, kts, qb, h):
            self.a = (pt, R_pt, V1, R_V1, kts, qb, h)
            self.pos = 0

        def emit(self, n):
            pt, R_pt, V1, R_V1, kts, qb, h = self.a
            nk = len(kts)
            for _ in range(n):
                if self.pos >= 4 * nk:
                    return
                qt, i = divmod(self.pos, nk)
                self.pos += 1
                pO, R_pO = ps.bank(qt)
                kb.mm(pO[:, 0:129], pt[:, i, qt * 128:(qt + 1) * 128], V1[:, kts[i], :], i == 0, i == nk - 1,
                      [R_pt, R_V1], [R_pO], inc=(i == nk - 1))
                if i == nk - 1:
                    self.fin_qt(qt)

        def flush(self):
            self.emit(4 * len(self.a[4]))

        def fin_qt(self, qt):
            pt, R_pt, V1, R_V1, kts, qb, h = self.a
            tt_ = qb * 4 + qt
            pO, R_pO = ps.bank(qt)
            sc = sm[:, qt * 2:qt * 2 + 1]
            R_sc = R_smx[qt]
            kb.recip(sc, pO[:, 128:129], [R_pO], [R_sc])
            cf = coef_fn(tt_, h) if coef_fn else None
            if cf is not None:
                kb.tt(sc, sc, cf[0], ALU.mult, [R_sc, cf[1]], [R_sc])
            dst = acc[:, tt_, h * 128:(h + 1) * 128]
            if first:
                kb.ts(dst, pO[:, 0:128], sc, None, ALU.mult, None, [R_pO, R_sc], [R_acc])
            else:
                kb.stt(dst, pO[:, 0:128], sc, dst, ALU.mult, ALU.add, [R_pO, R_sc, R_acc], [R_acc])

    pend = None
    for g in range(2):
        if pend is not None:
            pend.flush()
            pend = None
        kTg, V1, R_kTg, R_V1 = kv_fn(g)
        for qb in range(2):
            kts = kts_fn(qb) if kts_fn else list(range(32))
            mask, R_mask = mask_fn(g, qb)
            for hh in range(4):
                h = g * 4 + hh
                pt, R_pt = PT[st['npt'] % 2], R_PT[st['npt'] % 2]
                st['npt'] += 1
                assert len(kts) % 2 == 0
                npairs = len(kts) // 2
                for i in range(0, len(kts), 2):
                    pp_, R_pp = ps.pair(2 + st['nst'] % 2)
                    st['nst'] += 1
                    for u in range(2):
                        kt = kts[i + u]
                        kb.mm(pp_[:, u * 512:(u + 1) * 512], kTg[:, kt * 128:(kt + 1) * 128], qT[:, h, qb * 512:(qb + 1) * 512],
                              True, True, [R_kTg, R_q], [R_pp[u]])
                    eb, R_eb = Eb[st['ne'] % 3], R_Eb[st['ne'] % 3]
                    st['ne'] += 1
                    kb.act(eb[:], pp_, AF.Exp, R_pp, [R_eb], scale=SCALE)
                    rm = [R_mask[i], R_mask[i + 1]] if isinstance(R_mask, list) else [R_mask]
                    kb.tt(pt[:, i:i + 2, :].rearrange('p a b -> p (a b)'), eb[:],
                          mask[:, i:i + 2, :].rearrange('p a b -> p (a b)'), ALU.mult, [R_eb] + rm, [R_pt])
                    if pend is not None and INTERLEAVE_PV:
                        pend.emit(-(-4 * len(pend.a[4]) // npairs))
                if pend is not None:
                    pend.flush()
                pend = PV(pt, R_pt, V1, R_V1, kts, qb, h)
                if not INTERLEAVE_PV:
                    pend.flush()
                    pend = None
    if pend is not None:
        pend.flush()


def attn_window(kb, ps, G, qT, R_q, kTw, V1w, R_kw, R_vw, acc, R_acc, gates, R_g):
    C = G['C']
    R_c = G['R_c']
    kb.push()
    wm = kb.sb([128, 8, 5, 128], BF16, 'wm')
    R_wm = Res()
    tw = [kb.sb([128, 128], F32, 'tw') for _ in range(2)]
    R_tw = [Res(), Res()]
    n = 0
    for i in range(8):
        for m in range(5):
            kt = i + m
            t_, R_t = tw[n % 2], R_tw[n % 2]
            n += 1
            kb.act(t_[:], C['tposr'][:, i * 128:(i + 1) * 128], AF.Abs, [R_c], [R_t], bias=C['nwposb'][:, kt:kt + 1])
            kb.ts(wm[:, i, m, :], t_[:], 255.5, None, ALU.is_le, None, [R_t], [R_wm])
    PTw = [kb.sb([128, 5, 512], BF16, 'PTw') for _ in range(2)]
    R_PTw = [Res(), Res()]
    Ew = [kb.sb([128, 512], BF16, 'Ew') for _ in range(3)]
    R_Ew = [Res() for _ in range(3)]
    sm = kb.sb([128, 8], F32, 'wsm')
    R_sm = [Res() for _ in range(4)]
    npt = 0
    ne = 0

    class WPV:
        def __init__(self, pt, R_pt, g, i):
            self.a = (pt, R_pt, g, i)
            self.pos = 0

        def emit(self, n):
            pt, R_pt, g, i = self.a
            for _ in range(n):
                if self.pos >= 20:
                    return
                hh, m = divmod(self.pos, 5)
                self.pos += 1
                h = 4 * g + hh
                pO, R_pO = ps.bank(hh)
                kb.mm(pO[:, 0:129], pt[:, m, hh * 128:(hh + 1) * 128], V1w[:, g, i + m, :], m == 0, m == 4,
                      [R_pt, R_vw], [R_pO], inc=(m == 4))
                if m == 4:
                    sc = sm[:, hh * 2:hh * 2 + 1]
                    R_sc = R_sm[hh]
                    kb.recip(sc, pO[:, 128:129], [R_pO], [R_sc])
                    kb.tt(sc, sc, gates[:, i, 16 + h:17 + h], ALU.mult, [R_sc, R_g], [R_sc])
                    dst = acc[:, i, h * 128:(h + 1) * 128]
                    kb.stt(dst, pO[:, 0:128], sc, dst, ALU.mult, ALU.add, [R_pO, R_sc, R_acc], [R_acc])

        def flush(self):
            self.emit(20)

    pend = None
    for g in range(2):
        for i in range(8):
            pt, R_pt = PTw[npt % 2], R_PTw[npt % 2]
            npt += 1
            for m in range(5):
                kt = i + m
                pS, R_pS = ps.bank(4 + ne % 3)
                e_, R_e = Ew[ne % 3], R_Ew[ne % 3]
                ne += 1
                kb.mm(pS, kTw[:, g, kt * 128:(kt + 1) * 128], qT[:, 4 * g:4 * g + 4, i * 128:(i + 1) * 128], True, True,
                      [R_kw, R_q], [R_pS])
                kb.act(e_[:], pS, AF.Exp, [R_pS], [R_e], scale=SCALE)
                kb.tt(pt[:, m, :].rearrange('p (h q) -> p h q', h=4), e_[:].rearrange('p (h q) -> p h q', h=4),
                      wm[:, i, m, :].unsqueeze(1).to_broadcast([128, 4, 128]), ALU.mult, [R_e, R_wm], [R_pt])
                if pend is not None:
                    pend.emit(4)
            if pend is not None:
                pend.flush()
            pend = WPV(pt, R_pt, g, i)
    if pend is not None:
        pend.flush()
    kb.barrier()
    kb.pop()


def kv_from_dram(kb, G, A, kT_ap, V_ap, R_src):
    def kv_fn(g):
        kb.dma('sp', A['kTg'][:].rearrange('d (i t) -> d i t', i=4), kT_ap(g), reads=[R_src], writes=[A['R_kTg']])
        for q4 in range(4):
            kb.dma('sp', A['V1'][:, q4 * 8:(q4 + 1) * 8, 0:128], V_ap(g, q4), reads=[R_src], writes=[A['R_V1']])
        return A['kTg'], A['V1'], A['R_kTg'], A['R_V1']
    return kv_fn


def attn_alloc(kb):
    A = {}
    A['kTg'] = kb.sb([128, SEQ], BF16, 'kTg')
    A['V1'] = kb.sb([128, 32, 129], BF16, 'V1')
    A['PT'] = [kb.sb([128, 32, 512], BF16, 'PT') for _ in range(2)]
    A['Eb'] = [kb.sb([128, 1024], BF16, 'Eb') for _ in range(3)]
    A['sm'] = kb.sb([128, 8], F32, 'asm')
    A['R_kTg'] = Res()
    A['R_V1'] = Res()
    A['R_PT'] = [Res(), Res()]
    A['R_Eb'] = [Res() for _ in range(3)]
    A['R_sm'] = [Res() for _ in range(4)]
    kb.memset('dve', A['V1'][:, :, 128:129], 1.0, [A['R_V1']])
    return A


def q_prep(kb, ps, G, zt, col0, gain_ap, P, qT_list, rope_flags, R_qT):
    kb.push()
    C = G['C']
    R_c = G['R_c']
    R_p = Res()
    gq = kb.sb([128, 1, 128], F32, 'gq')
    kb.dma('sp', gq[:, 0, :], gain_ap.partition_broadcast(128), writes=[R_p])
    S = []
    for s_ in range(2):
        d = {'q': kb.sb([128, 8, 128], F32, 'q'), 'sq': kb.sb([128, 8, 128], F32, 'qsq'), 'ss': kb.sb([128, 8], F32, 'qss'),
             'tmp': kb.sb([128, 4, 8, 16], F32, 'qtmp'), 'qb': [kb.sb([128, 8, 128], BF16, 'qb') for _ in qT_list]}
        d['R'] = {n: Res() for n in ('q', 'tmp')}
        d['R_qb'] = [Res() for _ in qT_list]
        S.append(d)
    npb = 0
    for t_ in range(8):
        d = S[t_ % 2]
        R = d['R']
        rows = slice(t_ * 128, (t_ + 1) * 128)
        kb.dma('sp', d['q'][:].rearrange('p h d -> p (h d)'), zt[rows, col0:col0 + 1024], reads=[G['R_zt']], writes=[R['q']])
        rms_heads(kb, d['q'][:], 8, gq[:].to_broadcast([128, 8, 128]), d['sq'][:], d['ss'][:], R['q'], R_p, R['tmp'])
        for qi, (qT, rp) in enumerate(zip(qT_list, rope_flags)):
            qb = d['qb'][qi]
            R_qb = d['R_qb'][qi]
            kb.copy('act', qb[:], d['q'][:], [R['q']], [R_qb])
            if rp:
                rope_heads(kb, d['q'][:], qb[:], 8, C['cosq'][:, t_, :], C['sinq'][:, t_, :], d['tmp'][:], 16, R['q'], R_qb,
                           R_c, R['tmp'])
            pb, R_pb = ps.bank_bf(6 + npb % 2)
            npb += 1
            for h in range(8):
                kb.tr(pb[:, h * 128:(h + 1) * 128], qb[:, h, :], G['identb'][:], [R_qb, R_c], [R_pb], inc=(h == 7))
            kb.cpalt(qT[:, :, rows], pb.rearrange('p (h t) -> p h t', h=8), [R_pb], [R_qT[qi]])
    kb.barrier()
    kb.pop()


def stage_nsa(kb, ps, G, zt, P, IN, yT):
    C = G['C']
    R_c = G['R_c']
    kb.push()
    R_p = Res()
    qrT = kb.sb([128, 8, 1024], BF16, 'qrT')
    R_qc = Res()
    R_qr = Res()
    acc = kb.sb([128, 8, 1024], F32, 'accB')
    R_acc = Res()
    gates = kb.sb([128, 8, 24], F32, 'gates')
    R_g = Res()
    for t_ in range(8):
        kb.dma('sp', gates[:, t_, :], zt[t_ * 128:(t_ + 1) * 128, 3072:3096], reads=[G['R_zt']], writes=[R_g])
    kb.act(gates[:], gates[:], AF.Sigmoid, [R_g], [R_g])
    selT = kb.sb([64, 2, 1024], BF16, 'selT')
    R_selT = Res()
    kb.push()
    qcT = kb.sb([128, 8, 1024], BF16, 'qcT')
    q_prep(kb, ps, G, zt, 0, P['nsa_q_gain'], P, [qcT, qrT], [False, True], [R_qc, R_qr])
    HT = kb.sb([128, 8, 256], F32, 'HT')
    for i4 in range(4):
        kb.dma('sp', HT[:, :, i4 * 64:(i4 + 1) * 64], IN['hsT'](i4), reads=[G['R_in_hs']], writes=[R_p])
    w2 = kb.sb([128, 2, 128], BF16, 'w2')
    kb.dma('pool', w2[:, 0, :], P['cmp_w2_k'], writes=[R_p])
    kb.dma('pool', w2[:, 1, :], P['cmp_w2_v'], writes=[R_p])
    gk = kb.sb([128, 1, 128], F32, 'gkn')
    kb.dma('sp', gk[:, 0, :], P['nsa_k_gain'].partition_broadcast(128), writes=[R_p])
    hb = kb.sb([128, 4, 256], F32, 'hblk')
    hbb = kb.sb([128, 4, 256], BF16, 'hblkb')
    kb.memset('dve', hb[:], 0.0, [R_p])
    HT4 = HT[:].rearrange('e (a h) c -> e a h c', h=2)
    kb.tt(hb[:, :, 0:255], HT4[:, :, 0, 0:255], HT4[:, :, 1, 1:256], ALU.add, [R_p], [R_p])
    kb.act(hbb[:], hb[:], AF.Silu, [R_p], [R_p])
    kcT = kb.sb([128, 2, 256], BF16, 'kcT')
    vc = kb.sb([128, 2, 2, 128], BF16, 'vc')
    ckv = kb.sb([128, 1, 128], F32, 'ckv')
    ckb = kb.sb([128, 128], BF16, 'ckb')
    sq = kb.sb([128, 1, 128], F32, 'csq')
    ss = kb.sb([128, 8], F32, 'css')
    R_w = Res()
    for kv in range(2):
        for g in range(2):
            for nt in range(2):
                pb, R_pb = ps.bank((kv * 4 + g * 2 + nt) % 4)
                kb.mm(pb[:, 0:128], hbb[:, kv * 2 + g, nt * 128:(nt + 1) * 128], w2[:, kv, :], True, True, [R_p], [R_pb])
                if kv == 0:
                    kb.copy('act', ckv[:, 0, :], pb[:, 0:128], [R_pb], [R_w])
                    rms_heads(kb, ckv[:], 1, gk[:], sq[:], ss[:], R_w, R_p, R_w)
                    kb.copy('act', ckb[:], ckv[:, 0, :], [R_w], [R_w])
                    pt_, R_pt_ = ps.bank_bf(6 + nt % 2)
                    kb.tr(pt_[:, 0:128], ckb[:], G['identb'][:], [R_w, R_c], [R_pt_])
                    kb.cpalt(kcT[:, g, nt * 128:(nt + 1) * 128], pt_[:, 0:128], [R_pt_], [R_p])
                else:
                    kb.cpalt(vc[:, g, nt, :], pb[:, 0:128], [R_pb], [R_p])
    cthr = C['cthr']
    S = []
    for s_ in range(2):
        d = {'cm': kb.sb([128, 256], F32, 'cm'), 'P': kb.sb([128, 8, 256], F32, 'cP'),
             'pg': kb.sb([128, 2, 256], F32, 'pg'), 'pbf': kb.sb([128, 8, 256], BF16, 'pbf'),
             'pT': kb.sb([128, 8, 2, 128], BF16, 'pT'), 'den': kb.sb([128, 16], F32, 'den'),
             'imp': kb.sb([128, 64], F32, 'imp'), 'v': kb.sb([128, 64], F32, 'vv'), 'f': kb.sb([128, 64], F32, 'ff'),
             'imp2': kb.sb([128, 64], F32, 'imp2'), 'm8': kb.sb([128, 16], F32, 'm8'), 'sel': kb.sb([128, 64], BF16, 'sel')}
        d['R'] = {n: Res() for n in ('cm', 'P', 'pg', 'pbf', 'pT', 'den', 'imp', 'sel')}
        S.append(d)
    for t_ in range(8):
        d = S[t_ % 2]
        R = d['R']
        rows = slice(t_ * 128, (t_ + 1) * 128)
        kb.ts(d['cm'][:], cthr[:], C['tposc'][:, t_:t_ + 1], None, ALU.is_le, None, [R_c], [R['cm']])
        for h in range(8):
            pS, R_pS = ps.bank(h // 2)
            kb.mm(pS[:, (h % 2) * 256:(h % 2 + 1) * 256], qcT[:, h, rows], kcT[:, h // 4, :], True, True, [R_qc, R_p], [R_pS])
        for pr in range(2):
            pp_, R_pp = ps.pair(pr)
            kb.act(d['P'][:, pr * 4:(pr + 1) * 4, :].rearrange('p h n -> p (h n)'), pp_, AF.Exp, R_pp, [R['P']], scale=SCALE)
        kb.tt(d['P'][:], d['P'][:], d['cm'][:].unsqueeze(1).to_broadcast([128, 8, 256]), ALU.mult, [R['P'], R['cm']], [R['P']])
        kb.reduce(d['den'][:, 0:8], d['P'][:], ALU.add, [R['P']], [R['den']])
        kb.ts(d['den'][:, 0:8], d['den'][:, 0:8], 1e-30, None, ALU.max, None, [R['den']], [R['den']])
        kb.recip(d['den'][:, 8:16], d['den'][:, 0:8], [R['den']], [R['den']])
        kb.tt(d['P'][:], d['P'][:], d['den'][:, 8:16].unsqueeze(2).to_broadcast([128, 8, 256]), ALU.mult, [R['P'], R['den']],
              [R['P']])
        kb.reduce(d['pg'][:], d['P'][:].rearrange('p (g h) n -> p g n h', g=2), ALU.add, [R['P']], [R['pg']])
        kb.copy('act', d['pbf'][:], d['P'][:], [R['P']], [R['pbf']])
        for hf in range(2):
            pt_, R_pt_ = ps.bank_bf(6 + hf)
            for k in range(8):
                h, nt = hf * 4 + k // 2, k % 2
                kb.tr(pt_[:, k * 128:(k + 1) * 128], d['pbf'][:, h, nt * 128:(nt + 1) * 128], G['identb'][:],
                      [R['pbf'], R_c], [R_pt_], inc=(k == 7))
            kb.cpalt(d['pT'][:, hf * 4:(hf + 1) * 4].rearrange('p h n t -> p (h n t)'), pt_, [R_pt_], [R['pT']])
        for hf in range(2):
            pO, R_pO = ps.bank(4 + hf)
            for hh in range(4):
                h = hf * 4 + hh
                for nt in range(2):
                    kb.mm(pO[:, hh * 128:(hh + 1) * 128], d['pT'][:, h, nt, :], vc[:, h // 4, nt, :], nt == 0, nt == 1,
                          [R['pT'], R_p], [R_pO], inc=(hh == 3 and nt == 1))
            kb.tt(acc[:, t_, hf * 512:(hf + 1) * 512].rearrange('p (h d) -> p h d', h=4),
                  pO.rearrange('p (h d) -> p h d', h=4),
                  gates[:, t_, hf * 4:(hf + 1) * 4].unsqueeze(2).to_broadcast([128, 4, 128]), ALU.mult, [R_pO, R_g], [R_acc])
        for g in range(2):
            pg4 = d['pg'][:, g, :].rearrange('p (k r) -> p k r', r=4)
            imp = d['imp']
            kb.tt(imp[:], pg4[:, :, 0], pg4[:, :, 1], ALU.add, [R['pg']], [R['imp']])
            kb.tt(imp[:], imp[:], pg4[:, :, 2], ALU.add, [R['pg'], R['imp']], [R['imp']])
            kb.stt(imp[:], pg4[:, :, 3], 0.5, imp[:], ALU.mult, ALU.add, [R['pg'], R['imp']], [R['imp']])
            kb.stt(imp[:, 1:64], pg4[:, 0:63, 3], 0.5, imp[:, 1:64], ALU.mult, ALU.add, [R['pg'], R['imp']], [R['imp']])
            blk = C['blkc'][:, t_:t_ + 1]
            kb.ts(d['v'][:], C['kk'][:], blk, None, ALU.is_le, None, [R_c], [R['imp']])
            kb.ts(d['f'][:], C['kk'][:], blk, -1.0, ALU.subtract, ALU.is_ge, [R_c], [R['imp']])
            kb.tt(d['f'][:], d['f'][:], d['v'][:], ALU.mult, [R['imp']], [R['imp']])
            kb.tt(d['f'][:], d['f'][:], C['e0'][:], ALU.max, [R['imp'], R_c], [R['imp']])
            kb.tt(imp[:], imp[:], d['v'][:], ALU.mult, [R['imp']], [R['imp']])
            kb.ts(d['v'][:], d['v'][:], -1.0, 1e6, ALU.add, ALU.mult, [R['imp']], [R['imp']])
            kb.tt(imp[:], imp[:], d['v'][:], ALU.add, [R['imp']], [R['imp']])
            kb.stt(imp[:], d['f'][:], 2e6, imp[:], ALU.mult, ALU.add, [R['imp']], [R['imp']])
            m8, imp2 = d['m8'], d['imp2']
            kb.emit('dve', lambda e, m8=m8, imp=imp: e.max(out=m8[:, 0:8], in_=imp[:]), [R['imp']], [R['imp']])
            kb.emit('dve', lambda e, m8=m8, imp=imp, imp2=imp2: e.match_replace(out=imp2[:], in_to_replace=m8[:, 0:8],
                                                                               in_values=imp[:], imm_value=-3e6),
                    [R['imp']], [R['imp']])
            kb.emit('dve', lambda e, m8=m8, imp2=imp2: e.max(out=m8[:, 8:16], in_=imp2[:]), [R['imp']], [R['imp']])
            kb.ts(d['sel'][:], imp[:], d['m8'][:, 15:16], None, ALU.is_ge, None, [R['imp']], [R['sel']])
            pt_, R_pt_ = ps.bank_bf(6 + g % 2)
            kb.tr(pt_[0:64, 0:128], d['sel'][:], G['identb'][:], [R['sel'], R_c], [R_pt_])
            kb.cpalt(selT[:, g, rows], pt_[0:64, 0:128], [R_pt_], [R_selT])
    kb.barrier()
    kb.pop()
    ohp = G['ohp']
    kTw = kb.sb([128, 2, 1536], BF16, 'kTw')
    V1w = kb.sb([128, 2, 12, 129], BF16, 'V1w')
    R_kw = Res()
    R_vw = Res()
    kb.memset('dve', V1w[:, :, :, 128:129], 1.0, [R_vw])
    kb.dma('sp', kTw[:, :, 512:1536], IN['own_kT'][1].rearrange('(g d) t -> d g t', d=128), reads=[G['R_in_kv1']],
           writes=[R_kw])
    for g in range(2):
        kb.dma('sp', V1w[:, g, 4:12, 0:128], IN['own_V'][1][:, g * 128:(g + 1) * 128].rearrange('(k p) d -> p k d', p=128),
               reads=[G['R_in_kv1']], writes=[R_vw])

    kb.push()
    A = attn_alloc(kb)
    mask = kb.sb([128, 32, 512], BF16, 'mask')
    R_mask = Res()

    R_mk = [Res() for _ in range(32)]

    def mask_slc(g, qb):
        for kt in range(32):
            pE, R_pE = ps.bank(kt % 4)
            kb.mm(pE, C['ovE'][:, kt, :], selT[:, g, qb * 512:(qb + 1) * 512], True, True, [R_selT, R_c], [R_pE])
            kb.stt(mask[:, kt, :], C['tposr'][:, qb * 512:(qb + 1) * 512], C['sposc'][:, kt:kt + 1], pE, ALU.is_ge, ALU.mult,
                   [R_pE, R_c], [R_mk[kt]])
        return mask, R_mk

    tmpw = kb.sb([128, 512], F32, 'tmpw')
    R_tw = Res()
    def mask_win(g, qb):
        for i in range(8):
            kt = 4 * qb + i
            kb.act(tmpw[:], C['tposr'][:, qb * 512:(qb + 1) * 512], AF.Abs, [R_c], [R_tw], bias=C['nwposb'][:, kt:kt + 1])
            kb.ts(mask[:, i, :], tmpw[:], 255.5, None, ALU.is_le, None, [R_tw], [R_mk[i]])
        return mask, R_mk

    attn_dense(kb, ps, G, A, qrT, R_qr, kv_from_dram(kb, G, A, IN['kT'](0), IN['V'](0), G['R_in_kv0']), mask_slc, acc, R_acc,
               lambda t_, h: (gates[:, t_, 8 + h:9 + h], R_g), False)
    kb.barrier()
    tmpk = A['PT'][0][:].rearrange('p a b -> p (a b)')[:, 0:4096].rearrange('p (i g t) -> p i g t', i=4, g=2)
    tmpv = A['PT'][1][:].rearrange('p a b -> p (a b)')[:, 0:4096].rearrange('p (i k c) -> p i k c', i=4, k=4)
    R_t = Res()
    gk1 = IN['g_kT'][1].rearrange('i (g d) t -> i d g t', g=2)
    gv1 = IN['g_V'][1].rearrange('i r q c -> i (r q) c')
    for i in range(4):
        kb.dma('sp', tmpk[:, i], gk1[i][:, :, 512:1024], reads=[G['R_in_kv1']], writes=[R_t])
        kb.dma('sp', tmpv[:, i], gv1[i][512:1024, :].rearrange('(k p) c -> p k c', p=128), reads=[G['R_in_kv1']], writes=[R_t])
    kb.ts(kTw[:, :, 0:512], tmpk[:, 0], ohp[:, 0:1], None, ALU.mult, None, [R_t, R_c], [R_kw])
    for i in range(1, 4):
        kb.stt(kTw[:, :, 0:512], tmpk[:, i], ohp[:, i:i + 1], kTw[:, :, 0:512], ALU.mult, ALU.add, [R_t, R_c, R_kw], [R_kw])
    for g in range(2):
        dstv = V1w[:, g, 0:4, 0:128]
        kb.ts(dstv, tmpv[:, 0, :, g * 128:(g + 1) * 128], ohp[:, 0:1], None, ALU.mult, None, [R_t, R_c], [R_vw])
        for i in range(1, 4):
            kb.stt(dstv, tmpv[:, i, :, g * 128:(g + 1) * 128], ohp[:, i:i + 1], dstv, ALU.mult, ALU.add, [R_t, R_c, R_vw],
                   [R_vw])
    kb.barrier()
    kb.pop()
    attn_window(kb, ps, G, qrT, R_qr, kTw, V1w, R_kw, R_vw, acc, R_acc, gates, R_g)
    finish_tokmajor(kb, ps, G, zt, 1024, acc, R_acc, yT, 1024)
    kb.pop()


def finish_tokmajor(kb, ps, G, zt, gcol, acc, R_acc, yT, yrow0):
    kb.push()
    R_c = G['R_c']
    bg = [kb.sb([128, 1024], F32, 'bg') for _ in range(2)]
    yb = [kb.sb([128, 1024], BF16, 'ybt') for _ in range(2)]
    yst = [kb.sb([128, 8, 128], BF16, 'yst') for _ in range(2)]
    R_bg = [Res(), Res()]
    R_yb = [Res(), Res()]
    R_yst = [Res(), Res()]
    for t_ in range(8):
        b = t_ % 2
        rows = slice(t_ * 128, (t_ + 1) * 128)
        kb.dma('sp', bg[b][:], zt[rows, gcol:gcol + 1024], reads=[G['R_zt']], writes=[R_bg[b]])
        kb.act(bg[b][:], bg[b][:], AF.Silu, [R_bg[b]], [R_bg[b]])
        kb.tt(yb[b][:], acc[:, t_, :], bg[b][:], ALU.mult, [R_acc, R_bg[b]], [R_yb[b]])
        pb, R_pb = ps.bank_bf(6 + b)
        for h in range(8):
            kb.tr(pb[:, h * 128:(h + 1) * 128], yb[b][:, h * 128:(h + 1) * 128], G['identb'][:], [R_yb[b], R_c], [R_pb],
                  inc=(h == 7))
        kb.cpalt(yst[b][:], pb.rearrange('p (h t) -> p h t', h=8), [R_pb], [R_yst[b]])
        kb.dma('sp', yT[yrow0:yrow0 + 1024, rows].rearrange('(h p) t -> p h t', p=128), yst[b][:], reads=[R_yst[b]],
               writes=[G['R_yT']])
    kb.barrier()
    kb.pop()


def stage_dsa(kb, ps, G, zt, P, IN, yT, mT):
    C = G['C']
    R_c = G['R_c']
    kb.push()
    R_p = Res()
    qdT = kb.sb([128, 8, 1024], BF16, 'qdT')
    R_qd = Res()
    q_prep(kb, ps, G, zt, 3096, P['dsa_q_gain'], P, [qdT], [True], [R_qd])
    kb.push()
    iqT = kb.sb([128, 4, 1024], BF16, 'iqT')
    R_iqT = Res()
    sgn = kb.sb([128, 8, 8], F32, 'sgn')
    R_sgn = Res()
    ikbd = kb.sb([128, 16, 512], BF16, 'ikbd')
    kb.memset('dve', ikbd[:], 0.0, [R_p])
    for hs in range(2):
        for i4 in range(4):
            kb.dma('sp', ikbd[hs * 64:(hs + 1) * 64, i4 * 4:(i4 + 1) * 4, hs * 256:(hs + 1) * 256],
                   IN['ikT'][:, i4, :].rearrange('e (q c) -> e q c', c=256), reads=[G['R_in_ik']], writes=[R_p])
    kb.push()
    S = []
    for s_ in range(2):
        d = {'iq': kb.sb([128, 8, 64], F32, 'iq'), 'iw': kb.sb([128, 8], F32, 'iw'), 'aw': kb.sb([128, 8], F32, 'aw'),
             'tmp': kb.sb([128, 4, 8, 8], F32, 'itmp'), 'iqr': kb.sb([128, 8, 64], F32, 'iqr'),
             'iqb': kb.sb([128, 8, 64], BF16, 'iqb')}
        d['R'] = {n: Res() for n in ('iq', 'iw', 'tmp', 'iqr', 'iqb')}
        S.append(d)
    for t_ in range(8):
        d = S[t_ % 2]
        R = d['R']
        rows = slice(t_ * 128, (t_ + 1) * 128)
        kb.dma('sp', d['iq'][:].rearrange('p h e -> p (h e)'), zt[rows, 5656:6168], reads=[G['R_zt']], writes=[R['iq']])
        kb.dma('sp', d['iw'][:], zt[rows, 6232:6240], reads=[G['R_zt']], writes=[R['iw']])
        kb.copy('dve', d['iqr'][:], d['iq'][:], [R['iq']], [R['iqr']])
        rope_heads(kb, d['iq'][:], d['iqr'][:], 8, C['cosi'][:, t_, :], C['sini'][:, t_, :], d['tmp'][:], 8, R['iq'], R['iqr'],
                   R_c, R['tmp'])
        kb.act(d['aw'][:], d['iw'][:], AF.Abs, [R['iw']], [R['iw']], scale=8 ** -0.5 * 64 ** -0.5)
        kb.act(sgn[:, t_, :], d['iw'][:], AF.Sign, [R['iw']], [R_sgn])
        kb.tt(d['iqb'][:], d['iqr'][:], d['aw'][:].unsqueeze(2).to_broadcast([128, 8, 64]), ALU.mult, [R['iqr'], R['iw']],
              [R['iqb']])
        pb, R_pb = ps.bank_bf(6 + t_ % 2)
        for k in range(4):
            kb.tr(pb[:, k * 128:(k + 1) * 128], d['iqb'][:, 2 * k:2 * k + 2, :].rearrange('p h e -> p (h e)'), G['identb'][:],
                  [R['iqb'], R_c], [R_pb], inc=(k == 3))
        kb.cpalt(iqT[:, :, rows], pb[:, 0:512].rearrange('p (k t) -> p k t', k=4), [R_pb], [R_iqT])
    kb.barrier()
    kb.pop()
    sposr = kb.sb([128, SEQ], F32, 'sposr')
    kb.emit('pool', lambda e: e.iota(sposr[:], [[1, SEQ]], base=0, channel_multiplier=0,
                                     allow_small_or_imprecise_dtypes=True), (), [R_c])
    sc = [kb.sb([128, SEQ], F32, 'score') for _ in range(4)]
    R_sc = [Res() for _ in range(4)]
    rb = [kb.sb([128, 512], BF16, 'rbuf') for _ in range(4)]
    R_rb = [Res() for _ in range(4)]
    dg = [kb.sb([128, 8, 128], BF16, 'dg') for _ in range(2)]
    R_dg = [Res(), Res()]
    cmk = kb.sb([128, SEQ], BF16, 'cmk')
    R_cmk = Res()
    junk = [kb.sb([128, SEQ], BF16, 'junk') for _ in range(2)]
    R_junk = [Res(), Res()]
    Mb = [kb.sb([128, SEQ], BF16, 'Mb') for _ in range(2)]
    R_Mb = [Res(), Res()]
    mst = [kb.sb([128, 8, 128], BF16, 'mst') for _ in range(2)]
    R_mst = [Res(), Res()]
    bs = [kb.sb([128, 16], F32, 'bs') for _ in range(4)]
    R_bs = [Res() for _ in range(4)]
    LO, HI, W0, STP, MID, NMID, CNT, FLG, THR = (kb.sb([128, 4], F32, n) for n in
                                                 ('LO', 'HI', 'W0', 'STP', 'MID', 'NMID', 'CNT', 'FLG', 'THR'))
    R_L = Res()
    R_cn = [Res() for _ in range(4)]
    nr = 0
    nm = 0
    nsb = 0
    NACT = 3
    kb.memset('dve', THR[:, 0:1], 255.5, [R_L])
    kb.memset('dve', THR[:, 1:2], 2 * 255.5 - (SEQ - 2560), [R_L])
    kb.memset('dve', THR[:, 2:4], 2 * 255.5 - SEQ, [R_L])
    X1 = kb.sb([128, 2], F32, 'X1')
    R_x1 = [Res(), Res()]
    for grp in range(2):
        for sl in range(4):
            t_ = grp * 4 + sl
            rows = slice(t_ * 128, (t_ + 1) * 128)
            s_, R_s = sc[sl], R_sc[sl]
            dgt, R_dgt = dg[t_ % 2], R_dg[t_ % 2]
            for h in range(8):
                kb.ts(dgt[:, h, :], G['identb'][:], sgn[:, t_, h:h + 1], None, ALU.mult, None, [R_c, R_sgn], [R_dgt])
            for b256 in range(16):
                ks = slice(b256 * 256, (b256 + 1) * 256)
                pS, R_pS = ps.bank(4 + nsb % 2)
                nsb += 1
                pend = []
                for k in range(4):
                    pI, R_pI = ps.bank(k)
                    kb.mm(pI, iqT[:, k, rows], ikbd[:, b256, :], True, True, [R_iqT, R_p], [R_pI])
                    r_, R_r = rb[nr % 4], R_rb[nr % 4]
                    nr += 1
                    kb.act(r_[:], pI, AF.Relu, [R_pI], [R_r])
                    pend.append((k, r_, R_r))
                    if len(pend) > 3:
                        k0, r0, R_r0 = pend.pop(0)
                        for hs in range(2):
                            h0 = 2 * k0 + hs
                            kb.mm(pS[:, 0:256], dgt[:, h0, :], r0[:, hs * 256:(hs + 1) * 256], h0 == 0, False,
                                  [R_dgt, R_r0], [R_pS], inc=False)
                for (k0, r0, R_r0) in pend:
                    for hs in range(2):
                        h0 = 2 * k0 + hs
                        kb.mm(pS[:, 0:256], dgt[:, h0, :], r0[:, hs * 256:(hs + 1) * 256], h0 == 0, h0 == 7,
                              [R_dgt, R_r0], [R_pS], inc=(h0 == 7))
                kb.copy('dve', s_[:, ks], pS[:, 0:256], [R_pS], [R_s])
            B = bs[sl]
            R_B = R_bs[sl]
            kb.reduce(B[:, 0:1], s_[:], ALU.min, [R_s], [R_B])
            kb.ts(B[:, 0:1], B[:, 0:1], -1.0, None, ALU.add, None, [R_B], [R_B])
            kb.ts(cmk[:], sposr[:], C['tposc'][:, t_:t_ + 1], None, ALU.is_le, None, [R_c], [R_cmk])
            kb.stt(s_[:], s_[:], B[:, 0:1], cmk[:], ALU.subtract, ALU.mult, [R_s, R_B, R_cmk], [R_s])
            kb.reduce(HI[:, sl:sl + 1], s_[:], ALU.max, [R_s], [R_L])
        kb.memset('dve', LO[:], 0.5, [R_L])
        kb.ts(W0[:], HI[:], -0.5, None, ALU.add, None, [R_L], [R_L])
        for it in range(NBIS):
            ck = 0.5 ** (it + 1)
            kb.ts(STP[:], W0[:], ck, None, ALU.mult, None, [R_L], [R_L])
            kb.tt(MID[:], STP[:], LO[:], ALU.add, [R_L], [R_L])
            kb.ts(NMID[:], MID[:], -1.0, None, ALU.mult, None, [R_L], [R_L])
            HALF = 2560
            for sl in range(4):
                s_, R_s = sc[sl], R_sc[sl]
                if sl == 0:
                    kb.ts(junk[0][:], s_[:], MID[:, sl:sl + 1], 0.0, ALU.is_ge, ALU.add, [R_s, R_L], [R_junk[0], R_cn[sl]],
                          accum_out=CNT[:, sl:sl + 1])
                elif sl == 1:
                    kb.ts(junk[0][:, 0:HALF], s_[:, 0:HALF], MID[:, sl:sl + 1], 0.0, ALU.is_ge, ALU.add, [R_s, R_L],
                          [R_junk[0], R_x1[0]], accum_out=X1[:, 0:1])
                    kb.act(junk[1][:, 0:SEQ - HALF], s_[:, HALF:SEQ], AF.Sign, [R_s, R_L], [R_junk[1], R_x1[1]],
                           bias=NMID[:, sl:sl + 1], accum_out=X1[:, 1:2])
                else:
                    kb.act(junk[1][:], s_[:], AF.Sign, [R_s, R_L], [R_junk[1], R_cn[sl]], bias=NMID[:, sl:sl + 1],
                           accum_out=CNT[:, sl:sl + 1])
            kb.stt(CNT[:, 1:2], X1[:, 0:1], 2.0, X1[:, 1:2], ALU.mult, ALU.add, R_x1, [R_cn[1]])
            kb.tt(FLG[:], CNT[:], THR[:], ALU.is_ge, R_cn + [R_L], [R_L])
            kb.tt(FLG[:], FLG[:], STP[:], ALU.mult, [R_L], [R_L])
            kb.tt(LO[:], LO[:], FLG[:], ALU.add, [R_L], [R_L])
        for sl in range(4):
            t_ = grp * 4 + sl
            rows = slice(t_ * 128, (t_ + 1) * 128)
            s_, R_s = sc[sl], R_sc[sl]
            B = bs[sl]
            R_B = R_bs[sl]
            b = t_ % 2
            kb.ts(Mb[b][:], s_[:], LO[:, sl:sl + 1], None, ALU.is_ge, None, [R_s, R_L], [R_Mb[b]])
            for q4 in range(4):
                pb, R_pb = ps.bank_bf(6 + nm % 2)
                ms_, R_ms = mst[nm % 2], R_mst[nm % 2]
                nm += 1
                for k in range(8):
                    kt = q4 * 8 + k
                    kb.tr(pb[:, k * 128:(k + 1) * 128], Mb[b][:, kt * 128:(kt + 1) * 128], G['identb'][:], [R_Mb[b], R_c],
                          [R_pb], inc=(k == 7))
                kb.cpalt(ms_[:], pb.rearrange('p (k t) -> p k t', k=8), [R_pb], [R_ms])
                kb.dma('sp', mT[q4 * 1024:(q4 + 1) * 1024, rows].rearrange('(k p) t -> p k t', p=128), ms_[:], reads=[R_ms],
                       writes=[G['R_mT']])
    kb.barrier()
    kb.pop()
    acc = kb.sb([128, 8, 1024], F32, 'accC')
    R_acc = Res()
    kb.push()
    A = attn_alloc(kb)
    mask = kb.sb([128, 32, 512], BF16, 'maskd')
    R_mask = Res()
    mTv = mT.rearrange('(k p) t -> p k t', p=128)

    R_mq = [Res() for _ in range(4)]

    def mask_dsa(g, qb):
        for q4 in range(4):
            kb.dma('sp', mask[:, q4 * 8:(q4 + 1) * 8, :], mTv[:, q4 * 8:(q4 + 1) * 8, qb * 512:(qb + 1) * 512],
                   reads=[G['R_mT']], writes=[R_mq[q4]])
        return mask, [R_mq[k // 8] for k in range(32)]

    attn_dense(kb, ps, G, A, qdT, R_qd, kv_from_dram(kb, G, A, IN['kT'](2), IN['V'](2), G['R_in_kv2']), mask_dsa, acc, R_acc, None, True)
    kb.barrier()
    kb.pop()
    finish_tokmajor(kb, ps, G, zt, 4120, acc, R_acc, yT, 2048)
    kb.pop()


def stage_outproj(kb, ps, G, yT, w_ap, x_ap, xo_ap):
    kb.push()
    yTs = kb.sb([128, 32, 1024], BF16, 'yTs')
    R_y = Res()
    yv = yT.rearrange('(c p) t -> p c t', p=128)
    for q in range(4):
        kb.dma('sp', yTs[:, q * 8:(q + 1) * 8, :], yv[:, q * 8:(q + 1) * 8, :], reads=[G['R_yT']], writes=[R_y])
    wv = w_ap.rearrange('(c p) n -> p c n', p=128)
    wb = [kb.sb([128, 32, 512], BF16, 'wob') for _ in range(2)]
    R_wb = [Res(), Res()]
    xs = [kb.sb([128, 512], F32, 'xs') for _ in range(4)]
    R_xs = [Res() for _ in range(4)]
    nb = 0
    for ji in range(8):
        b = ji % 2
        c0 = ji * 512
        for q in range(4):
            kb.dma('pool', wb[b][:, q * 8:(q + 1) * 8, :], wv[:, q * 8:(q + 1) * 8, c0:c0 + 512], writes=[R_wb[b]])
        for t_ in range(8):
            rows = slice(t_ * 128, (t_ + 1) * 128)
            x_, R_x = xs[nb % 4], R_xs[nb % 4]
            kb.dma('sp', x_[:], x_ap[rows, c0:c0 + 512], writes=[R_x])
            pb, R_pb = ps.bank(nb % 4)
            nb += 1
            for c in range(32):
                kb.mm(pb, yTs[:, c, rows], wb[b][:, c, :], c == 0, c == 31, [R_y, R_wb[b]], [R_pb], inc=(c == 31))
            kb.tt(x_[:], x_[:], pb, ALU.add, [R_x, R_pb], [R_x])
            kb.dma('sp', xo_ap[rows, c0:c0 + 512], x_[:], reads=[R_x], writes=[G['R_out']])
    kb.barrier()
    kb.pop()


CONST_SPECS = [('identb', [128, 128], BF16), ('tposc', [128, 8], F32), ('tposr', [128, 1024], F32),
               ('cosq', [128, 8, 16], F32), ('sinq', [128, 8, 16], F32), ('cosi', [128, 8, 8], F32),
               ('sini', [128, 8, 8], F32), ('sposc', [128, 32], F32), ('nwposb', [128, 12], F32), ('cthr', [128, 256], F32),
               ('blkc', [128, 8], F32), ('kk', [128, 64], F32), ('e0', [128, 64], F32), ('ovE', [64, 32, 128], BF16)]

T_JOBS_M = [('T', 2048 + i * 512, 512, i * 512, False) for i in range(4)] + \
           [('T', 4608 + i * 512, 512, 2048 + i * 512, False) for i in range(8)] + [('T', 8704, 96, 6144, False)]
F_JOBS_M = [('F', i * 512, 512, i * 512, i < 2) for i in range(4)] + [('F', 4096, 512, 2048, False)] + \
           [('F', 8800 + i * 512, 512, 2560 + i * 512, i in (0, 1, 4, 5)) for i in range(8)]
K_COLS = np.concatenate([np.arange(0, 1024), np.arange(4096, 4608), np.arange(4608, 5632), np.arange(7704, 8216),
                         np.arange(8728, 8792)])
JOBS_K = [('F', 0, 512, 0, True), ('F', 512, 512, 512, True), ('F', 1024, 512, 2048, False),
          ('T', 1536, 512, 2048, False), ('T', 2048, 512, 2560, False), ('T', 2560, 512, 5144, False),
          ('T', 3072, 64, 6168, False)]


def setup_common(nc, kb, mode):
    G = {}
    C = {}
    R_c = Res()
    for name, shape, dt in CONST_SPECS:
        ap = nc.dram_tensor('c_' + name, shape, dt, kind='ExternalInput').ap()
        t = kb.sb(shape, dt, 'c_' + name)
        kb.dma('sp', t[:], ap, writes=[R_c])
        C[name] = t
    G['C'] = C
    G['R_c'] = R_c
    G['identb'] = C['identb']
    G['R_xT'] = Res()
    for n in ('R_zt', 'R_zf', 'R_yT', 'R_out', 'R_in', 'R_mT', 'R_lst', 'R_in_lru', 'R_in_hs', 'R_in_ik', 'R_in_kv0',
              'R_in_kv1', 'R_in_kv2'):
        G[n] = Res()
    return G


def build_fused(dbg=None):
    nc = bass.Bass('TRN2', target_bir_lowering=False)
    kb = KB(nc)
    ps = PS(nc)

    def din(name, shape, dt=F32):
        return nc.dram_tensor(name, shape, dt, kind='ExternalInput').ap()

    def dout(name, shape, dt=F32):
        return nc.dram_tensor(name, shape, dt, kind='ExternalOutput').ap()

    def dint(name, shape, dt=F32):
        return nc.dram_tensor(name, shape, dt).ap()

    x = din('x', [TOK, DM])
    xh0 = din('xh', [4, DM])
    G = setup_common(nc, kb, 'F')
    ohp_d = din('ohp', [128, 4])
    ohp = kb.sb([128, 4], F32, 'ohp')
    kb.dma('sp', ohp[:], ohp_d, writes=[G['R_c']])
    G['ohp'] = ohp
    W = {}
    Wl = [{'w_in': din('w_in%d' % l, [DM, NIN]), 'w_out': din('w_out%d' % l, [DM, DM])} for l in range(2)]
    for name, shape in (('ng', [DM]), ('lru_cw', [128, 8, 4]),
                        ('lru_vec', [128, 8, 4]), ('lru_wa', [8, 128, 128]), ('lru_wx', [8, 128, 128]),
                        ('kgain3', [3, 128]), ('cmp_peT', [128, 2, 32]), ('cmp_w1_k', [32, 128, 128]),
                        ('cmp_w1_v', [32, 128, 128]), ('sc_w', [128, 8, 3]), ('nsa_q_gain', [128]), ('nsa_k_gain', [128]),
                        ('dsa_q_gain', [128]), ('cmp_w2_k', [128, 128]), ('cmp_w2_v', [128, 128])):
        W[name] = din(name, [2] + shape)
    mlt = din('mlt', [128, 4])
    xo = dout('xo', [TOK, DM])
    zt = dint('zt', [TOK, NT])
    zf = dint('zf', [NF, XT])
    yT = dint('yT', [DM, TOK], BF16)
    mT = dint('mT', [SEQ, TOK], BF16)
    xres = dint('xres', [TOK, DM])
    lst = (dint('lru_hl', [1024, TOK]), dint('lru_ac', [1024, TOK]))
    xh_d = dint('xh_d', [4, DM])
    ex_xh = dint('ex_xh', [4, DM])
    g_xh = dint('g_xh', [16, DM])
    for l in range(2):
        P = {k: v[l] for k, v in W.items()}
        P.update(Wl[l])
        P['mlt'] = mlt
        exkv = [dint('ex_kv%d_%d' % (l, a), [512, TOK], BF16) for a in range(3)]
        gakv = [dint('g_kv%d_%d' % (l, a), [4 * 512, TOK], BF16) for a in range(3)]
        ex = {'kT': [t[0:256, :] for t in exkv],
              'V': [t[256:512, :].rearrange('r (q c) -> (r q) c', c=256) for t in exkv],
              'ikT': dint('ex_ik%d' % l, [64, TOK], BF16), 'hsT': dint('ex_hs%d' % l, [1024, 64]),
              'lru': dint('ex_lru%d' % l, [128, 16])}
        ga = {'kT': [t.rearrange('(i r) t -> i r t', i=4)[:, 0:256, :] for t in gakv],
              'V': [t.rearrange('(i r) (q c) -> i r q c', i=4, c=256)[:, 256:512] for t in gakv],
              'ikT': dint('g_ik%d' % l, [4 * 64, TOK], BF16), 'hsT': dint('g_hs%d' % l, [4 * 1024, 64]),
              'lru': dint('g_lru%d' % l, [4 * 128, 16])}
        x_src = x if l == 0 else xres
        xh_src = xh0 if l == 0 else xh_d
        x_dst = xres if l == 0 else xo
        kb.push()
        G['xT'] = kb.sb([128, 32, XT], BF16, 'xT')
        stage_norm_xT(kb, ps, G, x_src, xh_src, P['ng'])
        stage_inproj(kb, ps, G, P['w_in'], F_JOBS_M + T_JOBS_M, zt, zf)
        kb.pop()
        kb.push()
        wa_ = kb.sb([128, 8, 128], BF16, 'wa')
        wx_ = kb.sb([128, 8, 128], BF16, 'wx')
        R_w = Res()
        kb.dma('pool', wa_[:], P['lru_wa'].rearrange('h i j -> i h j'), writes=[R_w])
        kb.dma('pool', wx_[:], P['lru_wx'].rearrange('h i j -> i h j'), writes=[R_w])
        stage_kvprep(kb, ps, G, zt, zf, P, ex)
        R_g = Res()
        for k, rn in (('hsT', 'R_in_hs'), ('ikT', 'R_in_ik')):
            kb.allgather(ex[k], ga[k], reads=[G['R_out']], writes=[R_g, G[rn]])
        for a in (0, 1, 2):
            kb.allgather(exkv[a], gakv[a], reads=[G['R_out']], writes=[R_g, G['R_in_kv%d' % a]])
        R_lo = Res()
        stage_lru(kb, ps, G, zf, P, 'K', lru_out=ex['lru'], store=lst, pre=(wa_, wx_, R_w), R_lout=R_lo)
        kb.allgather(ex['lru'], ga['lru'], reads=[R_lo], writes=[R_g, G['R_in_lru']])
        kb.pop()
        stage_sconv(kb, ps, G, zf, P, yT)
        gkT = [t.rearrange('i (g d) t -> g d i t', g=2) for t in ga['kT']]
        gV = [t.rearrange('i r q (g d) -> i (r q) g d', g=2) for t in ga['V']]
        IN = {'kT': lambda a, gkT=gkT: (lambda g: gkT[a][g]),
              'V': lambda a, gV=gV: (lambda g, q4: gV[a][q4, :, g, :].rearrange('(k p) d -> p k d', p=128)),
              'ikT': ga['ikT'].rearrange('(i e) t -> e i t', i=4),
              'hsT': lambda i4, gh=ga['hsT'].rearrange('(i a e) c -> i e a c', i=4, a=8): gh[i4],
              'own_kT': ex['kT'], 'own_V': ex['V'], 'g_kT': ga['kT'], 'g_V': ga['V']}
        P['lru_all'] = ga['lru'].rearrange('(i p) ck -> p i ck', i=4)
        stage_lru_fin(kb, ps, G, zf, P, yT, lst)
        stage_nsa(kb, ps, G, zt, P, IN, yT)
        stage_dsa(kb, ps, G, zt, P, IN, yT, mT)
        stage_outproj(kb, ps, G, yT, P['w_out'], x_src, x_dst)
        if l == 0:
            if dbg is not None and 'x1' in dbg:
                d1 = dout('dbg_x1', [TOK, DM])
                kb.dma('sp', d1, xres, writes=[G['R_out']])
            R_h = Res()
            kb.dma('sp', ex_xh, xres[1020:1024, :], writes=[R_h])
            kb.allgather(ex_xh, g_xh, reads=[R_h], writes=[R_h])
            kb.push()
            ht = kb.sb([128, 4, 128], F32, 'ht')
            ha = kb.sb([128, 128], F32, 'ha')
            gv = g_xh.rearrange('(i r) (a b) -> r a i b', i=4, b=128)
            for r in range(4):
                kb.dma('sp', ht[r * 32:(r + 1) * 32], gv[r], reads=[R_h], writes=[R_h])
            kb.ts(ha[:], ht[:, 0, :], ohp[:, 0:1], None, ALU.mult, None, [R_h, G['R_c']], [R_h])
            for i in range(1, 4):
                kb.stt(ha[:], ht[:, i, :], ohp[:, i:i + 1], ha[:], ALU.mult, ALU.add, [R_h, G['R_c']], [R_h])
            xv = xh_d.rearrange('r (a b) -> r a b', b=128)
            for r in range(4):
                kb.dma('sp', xv[r], ha[r * 32:(r + 1) * 32, :], reads=[R_h], writes=[R_h])
            kb.barrier()
            kb.pop()
    kb.finish()
    return nc


def rope_tables(pos, r):
    half = r // 2
    inv = np.float32(THETA) ** (-np.arange(half, dtype=np.float32) * np.float32(2.0) / np.float32(r))
    ang = pos.astype(np.float32)[:, None] * inv[None, :].astype(np.float32)
    return np.cos(ang).astype(np.float32), np.sin(ang).astype(np.float32)


def core_consts(j):
    bf = ml_dtypes.bfloat16
    c = {}
    c['identb'] = np.eye(128, dtype=np.float32).astype(bf)
    pos = (1024 * j + np.arange(1024)).astype(np.int64)
    c['tposc'] = pos.reshape(8, 128).T.astype(np.float32).copy()
    c['tposr'] = np.broadcast_to(pos.astype(np.float32)[None, :], (128, 1024)).copy()
    cq, sq = rope_tables(pos, 32)
    ci, si = rope_tables(pos, 16)
    c['cosq'] = cq.reshape(8, 128, 16).transpose(1, 0, 2).copy()
    c['sinq'] = sq.reshape(8, 128, 16).transpose(1, 0, 2).copy()
    c['cosi'] = ci.reshape(8, 128, 8).transpose(1, 0, 2).copy()
    c['sini'] = si.reshape(8, 128, 8).transpose(1, 0, 2).copy()
    c['sposc'] = np.arange(4096).reshape(32, 128).T.astype(np.float32).copy()
    wpos = np.zeros((128, 12), np.float32)
    for kt in range(12):
        base = 1024 * j - 512 + 128 * kt
        wpos[:, kt] = base + np.arange(128)
        if base < 0:
            wpos[:, kt] = -1e6
    c['nwposb'] = -(wpos + np.float32(255.5))
    c['cthr'] = np.broadcast_to((16 * np.arange(256) + 31).astype(np.float32)[None, :], (128, 256)).copy()
    c['blkc'] = (pos // 64).reshape(8, 128).T.astype(np.float32).copy()
    c['kk'] = np.broadcast_to(np.arange(64, dtype=np.float32)[None, :], (128, 64)).copy()
    e0 = np.zeros((128, 64), np.float32)
    e0[:, 0] = 1.0
    c['e0'] = e0
    ov = np.zeros((64, 32, 128), np.float32)
    for kt in range(32):
        ov[2 * kt, kt, 0:64] = 1.0
        ov[2 * kt + 1, kt, 64:128] = 1.0
    c['ovE'] = ov.astype(bf)
    d = {'c_' + k: v for k, v in c.items()}
    ml = np.zeros((128, 4), np.float32)
    ml[:, :j] = 1.0
    d['mlt'] = ml
    oh = np.zeros((128, 4), np.float32)
    if j > 0:
        oh[:, j - 1] = 1.0
    d['ohp'] = oh
    return d


def pvec(v):
    return np.ascontiguousarray(v.reshape(v.shape[0], 8, 128).transpose(0, 2, 1))


def layout_params(p):
    f = lambda a: np.ascontiguousarray(a, dtype=np.float32)
    d = {}
    d['ng'] = f(p['norm_g'])
    for l in range(2):
        d['w_in%d' % l] = f(p['w_in'][l])
        d['w_out%d' % l] = f(p['w_out'][l])
    d['lru_cw'] = f(p['lru_conv_w'].reshape(2, 4, 8, 128).transpose(0, 3, 2, 1))
    d['lru_vec'] = f(np.stack([pvec(p['lru_conv_b']), pvec(p['lru_ba']), pvec(p['lru_bx']), pvec(p['lru_lambda'])], axis=-1))
    d['lru_wa'] = f(p['lru_wa'])
    d['lru_wx'] = f(p['lru_wx'])
    d['kgain3'] = f(np.stack([p['nsa_k_gain'], p['nsa_k_gain'], p['dsa_k_gain']], axis=1))
    d['cmp_peT'] = f(np.stack([p['cmp_pe_k'].transpose(0, 2, 1), p['cmp_pe_v'].transpose(0, 2, 1)], axis=2))
    d['cmp_w1_k'] = f(p['cmp_w1_k'])
    d['cmp_w1_v'] = f(p['cmp_w1_v'])
    d['sc_w'] = f(p['sc_conv_w'].reshape(2, 3, 8, 128).transpose(0, 3, 2, 1))
    d['nsa_q_gain'] = f(p['nsa_q_gain'])
    d['nsa_k_gain'] = f(p['nsa_k_gain'])
    d['dsa_q_gain'] = f(p['dsa_q_gain'])
    d['cmp_w2_k'] = f(p['cmp_w2_k'])
    d['cmp_w2_v'] = f(p['cmp_w2_v'])
    return d


_NC_CACHE = {}


def get_nc(dbg=None):
    key = tuple(dbg) if dbg else None
    if key not in _NC_CACHE:
        _NC_CACHE[key] = build_fused(dbg)
    return _NC_CACHE[key]


def run_fused(p, n_cores=8, dbg=None):
    x = np.ascontiguousarray(p['x'], dtype=np.float32)
    shared = layout_params(p)
    in_maps = []
    for c in range(n_cores):
        b, j = c // 4, c % 4
        m = dict(shared)
        m.update(core_consts(j))
        m['x'] = np.ascontiguousarray(x[b, j * 1024:(j + 1) * 1024, :])
        h = np.zeros((4, DM), np.float32)
        if j > 0:
            h[1:4] = x[b, j * 1024 - 3:j * 1024, :]
        m['xh'] = h
        in_maps.append(m)
    return run_bass_kernel_spmd(get_nc(dbg), in_maps, core_ids=list(range(n_cores))).results


def kernel(**inputs):
    p = {k: np.asarray(v) for k, v in inputs.items()}
    res = run_fused(p)
    out = np.zeros((2, SEQ, DM), np.float32)
    for c in range(8):
        out[c // 4, (c % 4) * 1024:(c % 4 + 1) * 1024, :] = res[c]['xo']
    return out
```

```python
import numpy as np
import ml_dtypes
import concourse.bass as bass
import concourse.mybir as mybir
from concourse.bass_utils import run_bass_kernel_spmd

F32 = mybir.dt.float32
BF16 = mybir.dt.bfloat16
ALU = mybir.AluOpType
AF = mybir.ActivationFunctionType
AX = mybir.AxisListType
ENGS = ('pe', 'act', 'dve', 'pool', 'sp')
NDS = 40

DM = 4096
SEQ = 4096
TOK = 1024
NIN = 12896
NT = 6240
NF = 6656
XT = 1028
EPS = 1e-6
SCALE = 128 ** -0.5
THETA = 500000.0
NBIS = 22
MBIG = 30720.0
INTERLEAVE_PV = True


class Res:
    __slots__ = ('name', 'w', 'r')

    def __init__(self, name=''):
        self.name = name
        self.w = {}
        self.r = {}


class KB:
    def __init__(self, nc):
        self.nc = nc
        self.ops = {e: [] for e in ENGS}
        self.h = {}
        self.cnt = {}
        for e in ('pe', 'act', 'dve', 'pool'):
            self.h[e] = nc.alloc_semaphore('s_' + e)
            self.cnt[e] = 0
        self.dval = [0] * NDS
        for i in range(NDS):
            self.h['d%d' % i] = nc.alloc_semaphore('sd%d' % i)
        self.dnext = 0
        self.h['cc'] = nc.alloc_semaphore('s_cc')
        self.ccnt = 0
        self.waited = {e: {} for e in ENGS}
        self.pending = {e: False for e in ENGS}
        self.sb_off = (nc.sbuf_base + 63) // 64 * 64
        self.sb_top = nc.sbuf_top
        self.sb_mark = []
        self.nalloc = 0
        self.ninstr = 0
        self.rr = 0
        self.abs = {e: [] for e in ENGS}

    def sb(self, shape, dtype, name=None):
        nbytes = int(np.prod(shape[1:])) * mybir.dt.size(dtype)
        nbytes = (nbytes + 63) // 64 * 64
        off = self.sb_off
        assert off + nbytes <= self.sb_top, ('sbuf overflow', name, off, nbytes, self.sb_top)
        self.sb_off += nbytes
        self.nalloc += 1
        return self.nc.alloc_sbuf_tensor_at('%s_%d' % (name or 't', self.nalloc), list(shape), dtype, offset=off)

    def push(self):
        self.sb_mark.append(self.sb_off)

    def pop(self):
        self.sb_off = self.sb_mark.pop()

    def _deps(self, eng, reads, writes):
        need = {}
        for r in reads:
            for k, v in r.w.items():
                if need.get(k, 0) < v:
                    need[k] = v
        for w in writes:
            for k, v in w.w.items():
                if need.get(k, 0) < v:
                    need[k] = v
            for k, v in w.r.items():
                if need.get(k, 0) < v:
                    need[k] = v
        wl = []
        wd = self.waited[eng]
        for k, v in need.items():
            if eng == 'pe' and k == 'pe':
                continue
            if wd.get(k, 0) < v:
                wd[k] = v
                wl.append((k, v))
        return wl

    def emit(self, eng, fn, reads=(), writes=(), inc=True):
        wl = self._deps(eng, reads, writes)
        if inc:
            self.cnt[eng] += 1
            ev = self.cnt[eng]
            self.pending[eng] = False
        else:
            ev = self.cnt[eng] + 1
            self.pending[eng] = True
        sem = self.h[eng]
        hh = self.h

        def thunk(e):
            for k, v in wl:
                e.wait_ge(hh[k], v)
            ins = fn(e)
            if inc:
                ins.then_inc(sem, 1)
        self.ops[eng].append(thunk)
        self.abs[eng].append((wl, (eng, 1) if inc else None))
        self.ninstr += 1
        for r in reads:
            if r.r.get(eng, 0) < ev:
                r.r[eng] = ev
        for w in writes:
            if w.w.get(eng, 0) < ev:
                w.w[eng] = ev

    def dma(self, q, out, in_, reads=(), writes=(), **kw):
        wl = self._deps(q, reads, writes)
        s = self.dnext
        self.dnext = (s + 1) % NDS
        key = 'd%d' % s
        prev = self.dval[s]
        if prev and self.waited[q].get(key, 0) < prev:
            wl.append((key, prev))
            self.waited[q][key] = prev
        ev = prev + 16
        self.dval[s] = ev
        hh = self.h

        def thunk(e):
            for k, v in wl:
                e.wait_ge(hh[k], v)
            e.dma_start(out=out, in_=in_, **kw).then_inc(hh[key], 16)
        self.ops[q].append(thunk)
        self.abs[q].append((wl, (key, 16)))
        self.ninstr += 1
        for r in reads:
            if r.r.get(key, 0) < ev:
                r.r[key] = ev
        for w in writes:
            if w.w.get(key, 0) < ev:
                w.w[key] = ev

    def allgather(self, src, dst, reads=(), writes=()):
        wl = self._deps('pool', reads, writes)
        self.ccnt += 1
        ev = self.ccnt
        hh = self.h

        def thunk(e):
            for k, v in wl:
                e.wait_ge(hh[k], v)
            e.collective_compute('AllGather', ALU.bypass, replica_groups=[[0, 1, 2, 3], [4, 5, 6, 7]],
                                 ins=[src.opt()], outs=[dst.opt()]).then_inc(hh['cc'])
            e.wait_ge(hh['cc'], ev)
        self.ops['pool'].append(thunk)
        self.abs['pool'].append((wl, ('cc', 1)))
        self.waited['pool']['cc'] = ev
        for r in reads:
            r.r['cc'] = ev
        for w in writes:
            w.w['cc'] = ev

    def barrier(self, engines=ENGS, include_cc=False):
        for e in ENGS:
            assert not self.pending[e], e
        allev = {k: v for k, v in self.cnt.items() if v > 0}
        if self.ccnt and include_cc:
            allev['cc'] = self.ccnt
        for i in range(NDS):
            if self.dval[i]:
                allev['d%d' % i] = self.dval[i]
        hh = self.h
        for e in engines:
            wl = []
            wd = self.waited[e]
            for k, v in allev.items():
                if wd.get(k, 0) < v:
                    wd[k] = v
                    wl.append((k, v))
            if wl:
                def thunk(en, wl=wl):
                    for k, v in wl:
                        en.wait_ge(hh[k], v)
                self.ops[e].append(thunk)
                self.abs[e].append((wl, None))

    def finish(self):
        self.barrier(include_cc=True)
        ops = self.ops
        with self.nc.Block() as block:
            @block.tensor
            def _(e):
                for t in ops['pe']:
                    t(e)

            @block.scalar
            def _(e):
                for t in ops['act']:
                    t(e)

            @block.vector
            def _(e):
                for t in ops['dve']:
                    t(e)

            @block.gpsimd
            def _(e):
                for t in ops['pool']:
                    t(e)

            @block.sync
            def _(e):
                for t in ops['sp']:
                    t(e)

    def mm(self, out, lhsT, rhs, start, stop, reads, writes, inc=True):
        self.emit('pe', lambda e: e.matmul(out, lhsT=lhsT, rhs=rhs, start=start, stop=stop), reads, writes, inc=inc)

    def tr(self, out, in_, ident, reads, writes, inc=True):
        self.emit('pe', lambda e: e.transpose(out, in_, ident), reads, writes, inc=inc)

    def act(self, out, in_, func, reads, writes, **kw):
        self.emit('act', lambda e: e.activation(out=out, in_=in_, func=func, **kw), reads, writes)

    def ts(self, out, in0, s1, s2, op0, op1, reads, writes, eng='dve', accum_out=None):
        if op1 is None:
            self.emit(eng, lambda e: e.tensor_scalar(out=out, in0=in0, scalar1=s1, scalar2=None, op0=op0), reads, writes)
        elif accum_out is None:
            self.emit(eng, lambda e: e.tensor_scalar(out=out, in0=in0, scalar1=s1, scalar2=s2, op0=op0, op1=op1), reads, writes)
        else:
            self.emit(eng, lambda e: e.tensor_scalar(out=out, in0=in0, scalar1=s1, scalar2=s2, op0=op0, op1=op1,
                                                     accum_out=accum_out), reads, writes)

    def tt(self, out, in0, in1, op, reads, writes, eng='dve'):
        self.emit(eng, lambda e: e.tensor_tensor(out=out, in0=in0, in1=in1, op=op), reads, writes)

    def stt(self, out, in0, scalar, in1, op0, op1, reads, writes, accum_out=None):
        if accum_out is None:
            self.emit('dve', lambda e: e.scalar_tensor_tensor(out=out, in0=in0, scalar=scalar, in1=in1, op0=op0, op1=op1),
                      reads, writes)
        else:
            self.emit('dve', lambda e: e.scalar_tensor_tensor(out=out, in0=in0, scalar=scalar, in1=in1, op0=op0, op1=op1,
                                                              accum_out=accum_out), reads, writes)

    def copy(self, eng, out, in_, reads, writes):
        if eng == 'act':
            self.emit('act', lambda e: e.activation(out=out, in_=in_, func=AF.Copy), reads, writes)
        else:
            self.emit(eng, lambda e: e.tensor_copy(out=out, in_=in_), reads, writes)

    def cpalt(self, out, in_, reads, writes):
        self.rr += 1
        self.copy('act' if self.rr % 2 else 'dve', out, in_, reads, writes)

    def memset(self, eng, ap, val, writes):
        self.emit(eng, lambda e: e.memset(ap, val), (), writes)

    def recip(self, out, in_, reads, writes):
        self.emit('dve', lambda e: e.reciprocal(out=out, in_=in_), reads, writes)

    def reduce(self, out, in_, op, reads, writes, axis=None):
        ax = axis or AX.X
        self.emit('dve', lambda e: e.tensor_reduce(out=out, in_=in_, axis=ax, op=op), reads, writes)


class PS:
    def __init__(self, nc):
        self.t = [nc.alloc_psum_tensor('pp%d' % i, [128, 1024], F32) for i in range(4)]
        self.r = [Res('bank%d' % i) for i in range(8)]

    def bank(self, i):
        return self.t[i // 2][:, (i % 2) * 512:(i % 2) * 512 + 512], self.r[i]

    def bank_bf(self, i):
        v = self.t[i // 2][:].bitcast(BF16)
        return v[:, (i % 2) * 1024:(i % 2) * 1024 + 1024], self.r[i]

    def pair(self, i):
        return self.t[i][:], [self.r[2 * i], self.r[2 * i + 1]]


def stage_norm_xT(kb, ps, G, x_ap, xh_ap, ng_ap):
    xT = G['xT']
    R_xT = G['R_xT']
    kb.push()
    g_bc = kb.sb([128, DM], F32, 'gbc')
    R_g = Res()
    kb.dma('sp', g_bc[:], ng_ap.partition_broadcast(128), writes=[R_g])
    xt = [kb.sb([128, DM], F32, 'xt') for _ in range(2)]
    hb = [kb.sb([128, DM], BF16, 'hb') for _ in range(2)]
    sm = kb.sb([128, 8], F32, 'sm')
    R_xt = [Res(), Res()]
    R_hb = [Res(), Res()]
    R_sm = [Res(), Res()]
    nb = 0
    for i in range(9):
        b = i % 2
        if i < 8:
            npt, src, tok0 = 128, x_ap[i * 128:(i + 1) * 128, :], 4 + i * 128
        else:
            npt, src, tok0 = 4, xh_ap, 0
        kb.dma('sp', xt[b][:npt], src, writes=[R_xt[b]])
        c0 = b * 4
        kb.act(hb[b][:npt], xt[b][:npt], AF.Square, [R_xt[b]], [R_hb[b], R_sm[b]], accum_out=sm[:npt, c0:c0 + 1])
        kb.ts(sm[:npt, c0 + 1:c0 + 2], sm[:npt, c0:c0 + 1], 1.0 / DM, EPS, ALU.mult, ALU.add, [R_sm[b]], [R_sm[b]])
        kb.act(sm[:npt, c0 + 2:c0 + 3], sm[:npt, c0 + 1:c0 + 2], AF.Sqrt, [R_sm[b]], [R_sm[b]])
        kb.recip(sm[:npt, c0 + 3:c0 + 4], sm[:npt, c0 + 2:c0 + 3], [R_sm[b]], [R_sm[b]])
        kb.stt(hb[b][:npt], xt[b][:npt], sm[:npt, c0 + 3:c0 + 4], g_bc[:npt], ALU.mult, ALU.mult,
               [R_xt[b], R_sm[b], R_g], [R_hb[b]])
        for cg in range(4):
            pb, R_pb = ps.bank_bf(6 + nb % 2)
            nb += 1
            for k in range(8):
                c = cg * 8 + k
                kb.tr(pb[:, k * 128:k * 128 + npt], hb[b][:npt, c * 128:(c + 1) * 128], G['identb'][:npt, :npt],
                      [R_hb[b], G['R_c']], [R_pb], inc=(k == 7))
            src_v = pb.rearrange('p (k t) -> p k t', k=8)[:, :, :npt]
            kb.cpalt(xT[:, cg * 8:(cg + 1) * 8, tok0:tok0 + npt], src_v, [R_pb], [R_xT])
    kb.barrier()
    kb.pop()


def stage_inproj(kb, ps, G, w_ap, jobs, zt, zf):
    xT = G['xT']
    R_xT = G['R_xT']
    kb.push()
    wv = w_ap.rearrange('(c p) n -> p c n', p=128)
    wb = [kb.sb([128, 32, 512], BF16, 'wb') for _ in range(2)]
    R_wb = [Res(), Res()]
    zst = [kb.sb([128, 512], F32, 'zst') for _ in range(4)]
    R_zst = [Res() for _ in range(4)]
    nb = 0
    nz = 0
    for ji, (orient, c0, w, d0, halo) in enumerate(jobs):
        b = ji % 2
        for q in range(4):
            kb.dma('pool', wb[b][:, q * 8:(q + 1) * 8, :w], wv[:, q * 8:(q + 1) * 8, c0:c0 + w], writes=[R_wb[b]])
        if orient == 'T':
            for tt_ in range(8):
                pb, R_pb = ps.bank(nb % 4)
                nb += 1
                for c in range(32):
                    kb.mm(pb[:, :w], xT[:, c, 4 + tt_ * 128:4 + (tt_ + 1) * 128], wb[b][:, c, :w], c == 0, c == 31,
                          [R_xT, R_wb[b]], [R_pb], inc=(c == 31))
                zs, R_zs = zst[nz % 4], R_zst[nz % 4]
                nz += 1
                kb.cpalt(zs[:, :w], pb[:, :w], [R_pb], [R_zs])
                kb.dma('sp', zt[tt_ * 128:(tt_ + 1) * 128, d0:d0 + w], zs[:, :w], reads=[R_zs], writes=[G['R_zt']])
        else:
            blocks = [(4, 512), (516, 512)]
            if halo:
                blocks = [(0, 4)] + blocks
            for j in range(w // 128):
                for (t0, n) in blocks:
                    pb, R_pb = ps.bank(nb % 4)
                    nb += 1
                    for c in range(32):
                        kb.mm(pb[:, :n], wb[b][:, c, j * 128:(j + 1) * 128], xT[:, c, t0:t0 + n], c == 0, c == 31,
                              [R_xT, R_wb[b]], [R_pb], inc=(c == 31))
                    zs, R_zs = zst[nz % 4], R_zst[nz % 4]
                    nz += 1
                    kb.cpalt(zs[:, :n], pb[:, :n], [R_pb], [R_zs])
                    kb.dma('sp', zf[d0 + j * 128:d0 + (j + 1) * 128, t0:t0 + n], zs[:, :n], reads=[R_zs],
                           writes=[G['R_zf']])
    kb.barrier()
    kb.pop()


def stage_lru(kb, ps, G, zf, P, mode, yT=None, lru_out=None, store=None, pre=None, R_lout=None):
    kb.push()
    R_p = Res()
    cw = kb.sb([128, 8, 4], F32, 'cw')
    vec = kb.sb([128, 8, 4], F32, 'vec')
    kb.dma('sp', cw[:], P['lru_cw'], writes=[R_p])
    kb.dma('sp', vec[:], P['lru_vec'], writes=[R_p])
    if pre is None:
        wa = kb.sb([128, 8, 128], BF16, 'wa')
        wx = kb.sb([128, 8, 128], BF16, 'wx')
        kb.dma('pool', wa[:], P['lru_wa'].rearrange('h i j -> i h j'), writes=[R_p])
        kb.dma('pool', wx[:], P['lru_wx'].rearrange('h i j -> i h j'), writes=[R_p])
        R_w = R_p
    else:
        wa, wx, R_w = pre
    cc = kb.sb([128, 8], F32, 'cc')
    sx = kb.sb([128, 8], F32, 'sx')
    s2 = kb.sb([128, 8], F32, 's2')
    pp_ = kb.sb([128, 8], F32, 'ppoly')
    kb.act(sx[:], vec[:, :, 3], AF.Exp, [R_p], [R_p], scale=-1.0)
    kb.ts(s2[:], sx[:], 2.0, None, ALU.add, None, [R_p], [R_p])
    kb.recip(s2[:], s2[:], [R_p], [R_p])
    kb.tt(sx[:], sx[:], s2[:], ALU.mult, [R_p], [R_p])
    kb.tt(s2[:], sx[:], sx[:], ALU.mult, [R_p], [R_p])
    kb.ts(pp_[:], s2[:], 1.0 / 9, 1.0 / 7, ALU.mult, ALU.add, [R_p], [R_p])
    for cst in (1.0 / 5, 1.0 / 3, 1.0):
        kb.tt(pp_[:], pp_[:], s2[:], ALU.mult, [R_p], [R_p])
        kb.ts(pp_[:], pp_[:], cst, None, ALU.add, None, [R_p], [R_p])
    kb.tt(cc[:], pp_[:], sx[:], ALU.mult, [R_p], [R_p])
    kb.ts(cc[:], cc[:], -16.0, None, ALU.mult, None, [R_p], [R_p])
    zeros = kb.sb([128, 1024], F32, 'zeros')
    kb.memset('dve', zeros[:], 0.0, [R_p])
    ends = kb.sb([128, 8, 2], F32, 'ends')
    R_ends = Res()
    hin = None
    if mode == 'M':
        la = kb.sb([128, 4, 8, 2], F32, 'la')
        ml = kb.sb([128, 4], F32, 'ml')
        kb.dma('sp', la[:].rearrange('p i c k -> p i (c k)'), P['lru_all'], reads=[G['R_in']], writes=[R_p])
        kb.dma('sp', ml[:], P['mlt'], writes=[R_p])
        hin = kb.sb([128, 8], F32, 'hin')
        tmp = kb.sb([128, 8], F32, 'hint')
        kb.memset('dve', hin[:], 0.0, [R_p])
        for i in range(3):
            kb.tt(tmp[:], la[:, i, :, 1], hin[:], ALU.mult, [R_p], [R_p])
            kb.tt(tmp[:], tmp[:], la[:, i, :, 0], ALU.add, [R_p], [R_p])
            kb.tt(tmp[:], tmp[:], hin[:], ALU.subtract, [R_p], [R_p])
            kb.stt(hin[:], tmp[:], ml[:, i:i + 1], hin[:], ALU.mult, ALU.add, [R_p], [R_p])
    S = []
    for s_ in range(2):
        d = {}
        d['ax'] = kb.sb([128, XT], F32, 'ax')
        for n in ('u', 'r', 'i', 'a', 't1', 't2', 'hl', 'Ac'):
            d[n] = kb.sb([128, 1024], F32, n)
        d['ub'] = kb.sb([128, 1024], BF16, 'ub')
        if mode == 'M':
            d['ag'] = kb.sb([128, 1024], F32, 'ag')
            d['yb'] = kb.sb([128, 1024], BF16, 'yb')
        d['R'] = {n: Res() for n in ('ax', 'u', 'r', 'i', 'a', 't1', 't2', 'hl', 'Ac', 'ub', 'ag', 'yb')}
        S.append(d)
    for ct in range(8):
        d = S[ct % 2]
        R = d['R']
        kb.dma('sp', d['ax'][:], zf[ct * 128:(ct + 1) * 128, :], reads=[G['R_zf']], writes=[R['ax']])
        if mode == 'M':
            kb.dma('sp', d['ag'][:], zf[1024 + ct * 128:1024 + (ct + 1) * 128, 4:XT], reads=[G['R_zf']], writes=[R['ag']])
        u = d['u']
        kb.ts(u[:], d['ax'][:, 1:1025], cw[:, ct, 0:1], vec[:, ct, 0:1], ALU.mult, ALU.add, [R['ax'], R_p], [R['u']])
        for k in range(1, 4):
            kb.stt(u[:], d['ax'][:, 1 + k:1025 + k], cw[:, ct, k:k + 1], u[:], ALU.mult, ALU.add, [R['ax'], R_p, R['u']],
                   [R['u']])
        kb.copy('act', d['ub'][:], u[:], [R['u']], [R['ub']])
        pr, R_pr = ps.pair(0)
        pi, R_pi = ps.pair(1)
        for hf in range(2):
            kb.mm(pr[:, hf * 512:(hf + 1) * 512], wa[:, ct, :], d['ub'][:, hf * 512:(hf + 1) * 512], True, True,
                  [R['ub'], R_w], [R_pr[hf]])
            kb.mm(pi[:, hf * 512:(hf + 1) * 512], wx[:, ct, :], d['ub'][:, hf * 512:(hf + 1) * 512], True, True,
                  [R['ub'], R_w], [R_pi[hf]])
        kb.act(d['r'][:], pr, AF.Sigmoid, R_pr + [R_p], [R['r']], bias=vec[:, ct, 1:2])
        kb.act(d['i'][:], pi, AF.Sigmoid, R_pi + [R_p], [R['i']], bias=vec[:, ct, 2:3])
        kb.act(d['a'][:], d['r'][:], AF.Exp, [R['r'], R_p], [R['a']], scale=cc[:, ct:ct + 1])
        kb.act(d['t1'][:], d['a'][:], AF.Square, [R['a']], [R['t1']])
        kb.ts(d['t1'][:], d['t1'][:], -1.0, 1.0, ALU.mult, ALU.add, [R['t1']], [R['t1']])
        kb.act(d['t1'][:], d['t1'][:], AF.Sqrt, [R['t1']], [R['t1']])
        kb.tt(d['t2'][:], d['i'][:], u[:], ALU.mult, [R['i'], R['u']], [R['t2']])
        kb.tt(d['t2'][:], d['t2'][:], d['t1'][:], ALU.mult, [R['t2'], R['t1']], [R['t2']])
        a_, b_, hl_, ac_ = d['a'], d['t2'], d['hl'], d['Ac']
        kb.emit('dve', lambda e, a_=a_, b_=b_, hl_=hl_: e.tensor_tensor_scan(out=hl_[:], data0=a_[:], data1=b_[:],
                                                                          initial=0.0, op0=ALU.mult, op1=ALU.add),
                [R['a'], R['t2']], [R['hl']])
        kb.emit('dve', lambda e, a_=a_, ac_=ac_: e.tensor_tensor_scan(out=ac_[:], data0=a_[:], data1=zeros[:],
                                                                      initial=1.0, op0=ALU.mult, op1=ALU.add),
                [R['a'], R_p], [R['Ac']])
        if mode == 'K':
            kb.copy('dve', ends[:, ct, 0:1], d['hl'][:, 1023:1024], [R['hl']], [R_ends])
            kb.copy('dve', ends[:, ct, 1:2], d['Ac'][:, 1023:1024], [R['Ac']], [R_ends])
            if store is not None:
                kb.dma('sp', store[0][ct * 128:(ct + 1) * 128, :], d['hl'][:], reads=[R['hl']], writes=[G['R_lst']])
                kb.dma('sp', store[1][ct * 128:(ct + 1) * 128, :], d['Ac'][:], reads=[R['Ac']], writes=[G['R_lst']])
        else:
            kb.stt(d['hl'][:], d['Ac'][:], hin[:, ct:ct + 1], d['hl'][:], ALU.mult, ALU.add, [R['Ac'], R['hl'], R_p],
                   [R['hl']])
            kb.act(d['ag'][:], d['ag'][:], AF.Silu, [R['ag']], [R['ag']])
            kb.tt(d['yb'][:], d['hl'][:], d['ag'][:], ALU.mult, [R['hl'], R['ag']], [R['yb']])
            kb.dma('sp', yT[ct * 128:(ct + 1) * 128, :], d['yb'][:], reads=[R['yb']], writes=[G['R_yT']])
    if mode == 'K':
        kb.dma('sp', lru_out, ends[:].rearrange('p c k -> p (c k)'), reads=[R_ends], writes=[G['R_out']] + ([R_lout] if R_lout else []))
    kb.barrier()
    kb.pop()


def stage_lru_fin(kb, ps, G, zf, P, yT, store):
    kb.push()
    R_p = Res()
    la = kb.sb([128, 4, 8, 2], F32, 'la')
    ml = kb.sb([128, 4], F32, 'ml')
    kb.dma('sp', la[:].rearrange('p i c k -> p i (c k)'), P['lru_all'], reads=[G['R_in_lru']], writes=[R_p])
    kb.dma('sp', ml[:], P['mlt'], writes=[R_p])
    hin = kb.sb([128, 8], F32, 'hin')
    tmp = kb.sb([128, 8], F32, 'hint')
    kb.memset('dve', hin[:], 0.0, [R_p])
    for i in range(3):
        kb.tt(tmp[:], la[:, i, :, 1], hin[:], ALU.mult, [R_p], [R_p])
        kb.tt(tmp[:], tmp[:], la[:, i, :, 0], ALU.add, [R_p], [R_p])
        kb.tt(tmp[:], tmp[:], hin[:], ALU.subtract, [R_p], [R_p])
        kb.stt(hin[:], tmp[:], ml[:, i:i + 1], hin[:], ALU.mult, ALU.add, [R_p], [R_p])
    S = []
    for s_ in range(3):
        d = {'hl': kb.sb([128, 1024], F32, 'hl'), 'Ac': kb.sb([128, 1024], F32, 'Ac'), 'ag': kb.sb([128, 1024], F32, 'ag'),
             'yb': kb.sb([128, 1024], BF16, 'yb')}
        d['R'] = {n: Res() for n in ('hl', 'Ac', 'ag', 'yb')}
        S.append(d)
    for ct in range(8):
        d = S[ct % 3]
        R = d['R']
        rs = slice(ct * 128, (ct + 1) * 128)
        kb.dma('sp', d['hl'][:], store[0][rs, :], reads=[G['R_lst']], writes=[R['hl']])
        kb.dma('sp', d['Ac'][:], store[1][rs, :], reads=[G['R_lst']], writes=[R['Ac']])
        kb.dma('sp', d['ag'][:], zf[1024 + ct * 128:1024 + (ct + 1) * 128, 4:XT], reads=[G['R_zf']], writes=[R['ag']])
        kb.stt(d['hl'][:], d['Ac'][:], hin[:, ct:ct + 1], d['hl'][:], ALU.mult, ALU.add, [R['Ac'], R['hl'], R_p], [R['hl']])
        kb.act(d['ag'][:], d['ag'][:], AF.Silu, [R['ag']], [R['ag']])
        kb.tt(d['yb'][:], d['hl'][:], d['ag'][:], ALU.mult, [R['hl'], R['ag']], [R['yb']])
        kb.dma('sp', yT[ct * 128:(ct + 1) * 128, :], d['yb'][:], reads=[R['yb']], writes=[G['R_yT']])
    kb.barrier()
    kb.pop()


def stage_sconv(kb, ps, G, zf, P, yT):
    kb.push()
    R_p = Res()
    sw = kb.sb([128, 8, 3], F32, 'scw')
    kb.dma('sp', sw[:], P['sc_w'], writes=[R_p])
    S = []
    for s_ in range(2):
        d = {'din': kb.sb([128, XT], F32, 'din'), 'dc': kb.sb([128, XT], F32, 'dc'),
             'db': kb.sb([128, 1024], F32, 'db'), 'dg': kb.sb([128, 1024], F32, 'dg'),
             'cv': kb.sb([128, 1024], F32, 'cv'), 'yb': kb.sb([128, 1024], BF16, 'yb')}
        d['R'] = {n: Res() for n in ('din', 'dc', 'db', 'dg', 'cv', 'yb')}
        S.append(d)
    for ct in range(8):
        d = S[ct % 2]
        R = d['R']
        r0 = 2560 + ct * 128
        kb.dma('sp', d['din'][:], zf[r0:r0 + 128, :], reads=[G['R_zf']], writes=[R['din']])
        kb.dma('sp', d['db'][:], zf[r0 + 1024:r0 + 1152, 4:XT], reads=[G['R_zf']], writes=[R['db']])
        kb.dma('sp', d['dc'][:], zf[r0 + 2048:r0 + 2176, :], reads=[G['R_zf']], writes=[R['dc']])
        kb.dma('sp', d['dg'][:], zf[r0 + 3072:r0 + 3200, 4:XT], reads=[G['R_zf']], writes=[R['dg']])
        kb.tt(d['din'][:], d['din'][:], d['dc'][:], ALU.mult, [R['din'], R['dc']], [R['din']])
        m = d['din']
        kb.ts(d['cv'][:], m[:, 2:1026], sw[:, ct, 0:1], None, ALU.mult, None, [R['din'], R_p], [R['cv']])
        for k in range(1, 3):
            kb.stt(d['cv'][:], m[:, 2 + k:1026 + k], sw[:, ct, k:k + 1], d['cv'][:], ALU.mult, ALU.add,
                   [R['din'], R_p, R['cv']], [R['cv']])
        kb.act(d['dg'][:], d['dg'][:], AF.Silu, [R['dg']], [R['dg']])
        kb.tt(d['cv'][:], d['cv'][:], d['db'][:], ALU.mult, [R['cv'], R['db']], [R['cv']])
        kb.tt(d['yb'][:], d['cv'][:], d['dg'][:], ALU.mult, [R['cv'], R['dg']], [R['yb']])
        kb.dma('sp', yT[3072 + ct * 128:3072 + (ct + 1) * 128, :], d['yb'][:], reads=[R['yb']], writes=[G['R_yT']])
    kb.barrier()
    kb.pop()


def rms_heads(kb, X, nh, gain_bc, sq, ss, R_X, R_g, R_tmp, d=128, Xg=None):
    if Xg is None:
        Xg = X
    kb.tt(sq, X, X, ALU.mult, [R_X], [R_tmp])
    kb.reduce(ss[:, 0:nh], sq, ALU.add, [R_tmp], [R_tmp])
    kb.ts(ss[:, 0:nh], ss[:, 0:nh], 1.0 / d, EPS, ALU.mult, ALU.add, [R_tmp], [R_tmp])
    kb.act(ss[:, 0:nh], ss[:, 0:nh], AF.Sqrt, [R_tmp], [R_tmp])
    kb.recip(ss[:, 0:nh], ss[:, 0:nh], [R_tmp], [R_tmp])
    kb.tt(X, X, ss[:, 0:nh].unsqueeze(2).to_broadcast([128, nh, d]), ALU.mult, [R_X, R_tmp], [R_X])
    kb.tt(Xg, Xg, gain_bc, ALU.mult, [R_X, R_g], [R_X])


def rope_heads(kb, X, Xo, nh, cos, sin, tmp, half, R_X, R_Xo, R_c, R_tmp):
    x1 = X[:, :, 0:half]
    x2 = X[:, :, half:2 * half]
    cb = cos.unsqueeze(1).to_broadcast([128, nh, half])
    sb_ = sin.unsqueeze(1).to_broadcast([128, nh, half])
    kb.tt(tmp[:, 0], x1, cb, ALU.mult, [R_X, R_c], [R_tmp])
    kb.tt(tmp[:, 1], x2, sb_, ALU.mult, [R_X, R_c], [R_tmp])
    kb.tt(tmp[:, 2], x2, cb, ALU.mult, [R_X, R_c], [R_tmp])
    kb.tt(tmp[:, 3], x1, sb_, ALU.mult, [R_X, R_c], [R_tmp])
    kb.tt(Xo[:, :, 0:half], tmp[:, 0], tmp[:, 1], ALU.subtract, [R_tmp], [R_Xo])
    kb.tt(Xo[:, :, half:2 * half], tmp[:, 2], tmp[:, 3], ALU.add, [R_tmp], [R_Xo])


def stage_kvprep(kb, ps, G, zt, zf, P, O):
    kb.push()
    R_p = Res()
    C = G['C']
    R_c = G['R_c']
    gk = kb.sb([128, 3, 128], F32, 'gk')
    kb.dma('sp', gk[:], P['kgain3'].partition_broadcast(128), writes=[R_p])
    kTst = kb.sb([128, 6, 1024], BF16, 'kTst')
    ikTst = kb.sb([64, 1024], BF16, 'ikTst')
    R_kTst = Res()
    R_ikTst = Res()
    S = []
    for s_ in range(2):
        d = {'k': kb.sb([128, 6, 128], F32, 'kall'), 'v': kb.sb([128, 6, 128], F32, 'vall'),
             'ik': kb.sb([128, 1, 64], F32, 'ik'), 'sq': kb.sb([128, 6, 128], F32, 'sq'),
             'ss': kb.sb([128, 8], F32, 'ss'), 'tmp': kb.sb([128, 4, 6, 16], F32, 'tmp'),
             'kb': kb.sb([128, 6, 128], BF16, 'kb16'), 'vb': kb.sb([128, 6, 128], BF16, 'vb16'),
             'ikb': kb.sb([128, 1, 64], BF16, 'ikb'), 'tmpi': kb.sb([128, 4, 1, 8], F32, 'tmpi')}
        d['R'] = {n: Res() for n in ('k', 'v', 'ik', 'tmp', 'kb', 'vb', 'ikb')}
        S.append(d)
    for t_ in range(8):
        d = S[t_ % 2]
        R = d['R']
        rows = slice(t_ * 128, (t_ + 1) * 128)
        kv6 = d['k'][:].rearrange('p (a g) d -> p a (g d)', a=3)
        vv6 = d['v'][:].rearrange('p (a g) d -> p a (g d)', a=3)
        for a, (kc, vc) in enumerate(((2048, 2304), (2560, 2816), (5144, 5400))):
            kb.dma('sp', kv6[:, a, :], zt[rows, kc:kc + 256], reads=[G['R_zt']], writes=[R['k']])
            kb.dma('sp', vv6[:, a, :], zt[rows, vc:vc + 256], reads=[G['R_zt']], writes=[R['v']])
        kb.dma('sp', d['ik'][:, 0, :], zt[rows, 6168:6232], reads=[G['R_zt']], writes=[R['ik']])
        gk_b = gk[:].unsqueeze(2).to_broadcast([128, 3, 2, 128])
        rms_heads(kb, d['k'][:], 6, gk_b, d['sq'][:], d['ss'][:], R['k'], R_p, R['tmp'],
                  Xg=d['k'][:].rearrange('p (a g) d -> p a g d', a=3))
        kb.copy('act', d['kb'][:], d['k'][:], [R['k']], [R['kb']])
        rope_heads(kb, d['k'][:], d['kb'][:], 6, C['cosq'][:, t_, :], C['sinq'][:, t_, :], d['tmp'][:], 16, R['k'],
                   R['kb'], R_c, R['tmp'])
        pb, R_pb = ps.bank_bf(6 + t_ % 2)
        for k in range(6):
            kb.tr(pb[:, k * 128:(k + 1) * 128], d['kb'][:, k, :], G['identb'][:], [R['kb'], R_c], [R_pb], inc=(k == 5))
        kb.cpalt(kTst[:, :, rows], pb[:, 0:768].rearrange('p (k t) -> p k t', k=6), [R_pb], [R_kTst])
        kb.copy('act', d['vb'][:], d['v'][:], [R['v']], [R['vb']])
        for a in range(3):
            kb.dma('sp', O['V'][a][rows, :], d['vb'][:, 2 * a:2 * a + 2, :].rearrange('p g d -> p (g d)'), reads=[R['vb']],
                   writes=[G['R_out']])
        kb.copy('act', d['ikb'][:], d['ik'][:], [R['ik']], [R['ikb']])
        rope_heads(kb, d['ik'][:], d['ikb'][:], 1, C['cosi'][:, t_, :], C['sini'][:, t_, :], d['tmpi'][:], 8, R['ik'],
                   R['ikb'], R_c, R['tmp'])
        pb2, R_pb2 = ps.bank_bf(4 + t_ % 2)
        kb.tr(pb2[0:64, 0:128], d['ikb'][:, 0, :], G['identb'][:], [R['ikb'], R_c], [R_pb2])
        kb.cpalt(ikTst[:, rows], pb2[0:64, 0:128], [R_pb2], [R_ikTst])
    for a in range(3):
        kb.dma('sp', O['kT'][a].rearrange('(g d) t -> d g t', d=128), kTst[:, 2 * a:2 * a + 2, :], reads=[R_kTst],
               writes=[G['R_out']])
    kb.dma('sp', O['ikT'], ikTst[:], reads=[R_ikTst], writes=[G['R_out']])
    kb.barrier()
    peT = kb.sb([128, 2, 32], F32, 'peT')
    kb.dma('sp', peT[:], P['cmp_peT'], writes=[R_p])
    w1 = kb.sb([128, 2, 32, 128], BF16, 'w1')
    kb.dma('pool', w1[:, 0], P['cmp_w1_k'].rearrange('l d e -> d l e'), writes=[R_p])
    kb.dma('pool', w1[:, 1], P['cmp_w1_v'].rearrange('l d e -> d l e'), writes=[R_p])
    hst = kb.sb([128, 8, 64], F32, 'hst')
    R_hst = Res()
    kc = [kb.sb([128, XT], F32, 'kcT') for _ in range(2)]
    kpe = [kb.sb([128, 2, 64, 16], BF16, 'kpe') for _ in range(2)]
    R_kc = [Res(), Res()]
    R_kpe = [Res(), Res()]
    for kv in range(2):
        for g in range(2):
            i = kv * 2 + g
            b = i % 2
            kb.dma('sp', kc[b][:], zf[2048 + i * 128:2048 + (i + 1) * 128, :], reads=[G['R_zf']], writes=[R_kc[b]])
            kv3 = kc[b][:, 4:XT].rearrange('p (c l) -> p c l', l=16)
            for hf in range(2):
                kb.tt(kpe[b][:, hf], kv3, peT[:, kv, hf * 16:(hf + 1) * 16].unsqueeze(1).to_broadcast([128, 64, 16]),
                      ALU.add, [R_kc[b], R_p], [R_kpe[b]])
            for hf in range(2):
                pb, R_pb = ps.bank((i * 2 + hf) % 4)
                for l in range(16):
                    kb.mm(pb[:, 0:64], w1[:, kv, hf * 16 + l, :], kpe[b][:, hf, :, l], l == 0, l == 15,
                          [R_kpe[b], R_p], [R_pb], inc=(l == 15))
                kb.cpalt(hst[:, i * 2 + hf, :], pb[:, 0:64], [R_pb], [R_hst])
    kb.dma('sp', O['hsT'].rearrange('(a e) c -> e a c', e=128), hst[:], reads=[R_hst], writes=[G['R_out']])
    kb.barrier()
    kb.pop()


def attn_dense(kb, ps, G, A, qT, R_q, kv_fn, mask_fn, acc, R_acc, coef_fn, first, kts_fn=None):
    PT, Eb, R_PT, R_Eb, sm, R_smx = (A[k] for k in ('PT', 'Eb', 'R_PT', 'R_Eb', 'sm', 'R_sm'))
    st = {'ne': 0, 'npt': 0, 'nst': 0}

    class PV:
        def __init__(self, pt, R_pt, V1, R_V1, kts, qb, h):
            self.a = (pt, R_pt, V1, R_V1, kts, qb, h)
            self.pos = 0

        def emit(self, n):
            pt, R_pt, V1, R_V1, kts, qb, h = self.a
            nk = len(kts)
            for _ in range(n):
                if self.pos >= 4 * nk:
                    return
                qt, i = divmod(self.pos, nk)
                self.pos += 1
                pO, R_pO = ps.bank(qt % 2)
                kb.mm(pO[:, 0:129], pt[:, i, qt * 128:(qt + 1) * 128], V1[:, kts[i], :], i == 0, i == nk - 1,
                      [R_pt, R_V1], [R_pO], inc=(i == nk - 1))
                if i == nk - 1:
                    self.fin_qt(qt)

        def flush(self):
            self.emit(4 * len(self.a[4]))

        def fin_qt(self, qt):
            pt, R_pt, V1, R_V1, kts, qb, h = self.a
            tt_ = qb * 4 + qt
            pO, R_pO = ps.bank(qt % 2)
            sc = sm[:, qt * 2:qt * 2 + 1]
            R_sc = R_smx[qt]
            kb.recip(sc, pO[:, 128:129], [R_pO], [R_sc])
            cf = coef_fn(tt_, h) if coef_fn else None
            if cf is not None:
                kb.tt(sc, sc, cf[0], ALU.mult, [R_sc, cf[1]], [R_sc])
            dst = acc[:, tt_, h * 128:(h + 1) * 128]
            if first:
                kb.ts(dst, pO[:, 0:128], sc, None, ALU.mult, None, [R_pO, R_sc], [R_acc])
            else:
                kb.stt(dst, pO[:, 0:128], sc, dst, ALU.mult, ALU.add, [R_pO, R_sc, R_acc], [R_acc])

    pend = None
    for g in range(2):
        if pend is not None:
            pend.flush()
            pend = None
        kTg, V1, R_kTg, R_V1 = kv_fn(g)
        for qb in range(2):
            kts = kts_fn(qb) if kts_fn else list(range(32))
            mask, R_mask = mask_fn(g, qb)
            for hh in range(4):
                h = g * 4 + hh
                pt, R_pt = PT[st['npt'] % 2], R_PT[st['npt'] % 2]
                st['npt'] += 1
                assert len(kts) % 2 == 0
                npairs = len(kts) // 2
                for i in range(0, len(kts), 2):
                    pp_, R_pp = ps.pair(1 + st['nst'] % 3)
                    st['nst'] += 1
                    for u in range(2):
                        kt = kts[i + u]
                        kb.mm(pp_[:, u * 512:(u + 1) * 512], kTg[:, kt * 128:(kt + 1) * 128], qT[:, h, qb * 512:(qb + 1) * 512],
                              True, True, [R_kTg, R_q], [R_pp[u]])
                    eb, R_eb = Eb[st['ne'] % 3], R_Eb[st['ne'] % 3]
                    st['ne'] += 1
                    kb.act(eb[:], pp_, AF.Exp, R_pp, [R_eb], scale=SCALE)
                    rm = [R_mask[i], R_mask[i + 1]] if isinstance(R_mask, list) else [R_mask]
                    kb.tt(pt[:, i:i + 2, :].rearrange('p a b -> p (a b)'), eb[:],
                          mask[:, i:i + 2, :].rearrange('p a b -> p (a b)'), ALU.mult, [R_eb] + rm, [R_pt])
                    if pend is not None and INTERLEAVE_PV:
                        pend.emit(-(-4 * len(pend.a[4]) // npairs))
                if pend is not None:
                    pend.flush()
                pend = PV(pt, R_pt, V1, R_V1, kts, qb, h)
                if not INTERLEAVE_PV:
                    pend.flush()
                    pend = None
    if pend is not None:
        pend.flush()


def attn_window(kb, ps, G, qT, R_q, kTw, V1w, R_kw, R_vw, acc, R_acc, gates, R_g):
    C = G['C']
    R_c = G['R_c']
    kb.push()
    wm = kb.sb([128, 8, 5, 128], BF16, 'wm')
    R_wm = Res()
    tw = [kb.sb([128, 128], F32, 'tw') for _ in range(2)]
    R_tw = [Res(), Res()]
    n = 0
    for i in range(8):
        for m in range(5):
            kt = i + m
            t_, R_t = tw[n % 2], R_tw[n % 2]
            n += 1
            kb.act(t_[:], C['tposr'][:, i * 128:(i + 1) * 128], AF.Abs, [R_c], [R_t], bias=C['nwposb'][:, kt:kt + 1])
            kb.ts(wm[:, i, m, :], t_[:], 255.5, None, ALU.is_le, None, [R_t], [R_wm])
    PTw = [kb.sb([128, 5, 512], BF16, 'PTw') for _ in range(2)]
    R_PTw = [Res(), Res()]
    Ew = [kb.sb([128, 512], BF16, 'Ew') for _ in range(3)]
    R_Ew = [Res() for _ in range(3)]
    sm = kb.sb([128, 8], F32, 'wsm')
    R_sm = [Res() for _ in range(4)]
    npt = 0
    ne = 0

    class WPV:
        def __init__(self, pt, R_pt, g, i):
            self.a = (pt, R_pt, g, i)
            self.pos = 0

        def emit(self, n):
            pt, R_pt, g, i = self.a
            for _ in range(n):
                if self.pos >= 20:
                    return
                hh, m = divmod(self.pos, 5)
                self.pos += 1
                h = 4 * g + hh
                pO, R_pO = ps.bank(hh)
                kb.mm(pO[:, 0:129], pt[:, m, hh * 128:(hh + 1) * 128], V1w[:, g, i + m, :], m == 0, m == 4,
                      [R_pt, R_vw], [R_pO], inc=(m == 4))
                if m == 4:
                    sc = sm[:, hh * 2:hh * 2 + 1]
                    R_sc = R_sm[hh]
                    kb.recip(sc, pO[:, 128:129], [R_pO], [R_sc])
                    kb.tt(sc, sc, gates[:, i, 16 + h:17 + h], ALU.mult, [R_sc, R_g], [R_sc])
                    dst = acc[:, i, h * 128:(h + 1) * 128]
                    kb.stt(dst, pO[:, 0:128], sc, dst, ALU.mult, ALU.add, [R_pO, R_sc, R_acc], [R_acc])

        def flush(self):
            self.emit(20)

    pend = None
    for g in range(2):
        for i in range(8):
            pt, R_pt = PTw[npt % 2], R_PTw[npt % 2]
            npt += 1
            for m in range(5):
                kt = i + m
                pS, R_pS = ps.bank(4 + ne % 3)
                e_, R_e = Ew[ne % 3], R_Ew[ne % 3]
                ne += 1
                kb.mm(pS, kTw[:, g, kt * 128:(kt + 1) * 128], qT[:, 4 * g:4 * g + 4, i * 128:(i + 1) * 128], True, True,
                      [R_kw, R_q], [R_pS])
                kb.act(e_[:], pS, AF.Exp, [R_pS], [R_e], scale=SCALE)
                kb.tt(pt[:, m, :].rearrange('p (h q) -> p h q', h=4), e_[:].rearrange('p (h q) -> p h q', h=4),
                      wm[:, i, m, :].unsqueeze(1).to_broadcast([128, 4, 128]), ALU.mult, [R_e, R_wm], [R_pt])
                if pend is not None:
                    pend.emit(4)
            if pend is not None:
                pend.flush()
            pend = WPV(pt, R_pt, g, i)
    if pend is not None:
        pend.flush()
    kb.barrier()
    kb.pop()


def kv_from_dram(kb, G, A, kT_ap, V_ap, R_src):
    def kv_fn(g):
        kb.dma('sp', A['kTg'][:].rearrange('d (i t) -> d i t', i=4), kT_ap(g), reads=[R_src], writes=[A['R_kTg']])
        for q4 in range(4):
            kb.dma('sp', A['V1'][:, q4 * 8:(q4 + 1) * 8, 0:128], V_ap(g, q4), reads=[R_src], writes=[A['R_V1']])
        return A['kTg'], A['V1'], A['R_kTg'], A['R_V1']
    return kv_fn


def attn_alloc(kb):
    A = {}
    A['kTg'] = kb.sb([128, SEQ], BF16, 'kTg')
    A['V1'] = kb.sb([128, 32, 129], BF16, 'V1')
    A['PT'] = [kb.sb([128, 32, 512], BF16, 'PT') for _ in range(2)]
    A['Eb'] = [kb.sb([128, 1024], BF16, 'Eb') for _ in range(3)]
    A['sm'] = kb.sb([128, 8], F32, 'asm')
    A['R_kTg'] = Res()
    A['R_V1'] = Res()
    A['R_PT'] = [Res(), Res()]
    A['R_Eb'] = [Res() for _ in range(3)]
    A['R_sm'] = [Res() for _ in range(4)]
    kb.memset('dve', A['V1'][:, :, 128:129], 1.0, [A['R_V1']])
    return A


def q_prep(kb, ps, G, zt, col0, gain_ap, P, qT_list, rope_flags, R_qT):
    kb.push()
    C = G['C']
    R_c = G['R_c']
    R_p = Res()
    gq = kb.sb([128, 1, 128], F32, 'gq')
    kb.dma('sp', gq[:, 0, :], gain_ap.partition_broadcast(128), writes=[R_p])
    S = []
    for s_ in range(2):
        d = {'q': kb.sb([128, 8, 128], F32, 'q'), 'sq': kb.sb([128, 8, 128], F32, 'qsq'), 'ss': kb.sb([128, 8], F32, 'qss'),
             'tmp': kb.sb([128, 4, 8, 16], F32, 'qtmp'), 'qb': [kb.sb([128, 8, 128], BF16, 'qb') for _ in qT_list]}
        d['R'] = {n: Res() for n in ('q', 'tmp')}
        d['R_qb'] = [Res() for _ in qT_list]
        S.append(d)
    npb = 0
    for t_ in range(8):
        d = S[t_ % 2]
        R = d['R']
        rows = slice(t_ * 128, (t_ + 1) * 128)
        kb.dma('sp', d['q'][:].rearrange('p h d -> p (h d)'), zt[rows, col0:col0 + 1024], reads=[G['R_zt']], writes=[R['q']])
        rms_heads(kb, d['q'][:], 8, gq[:].to_broadcast([128, 8, 128]), d['sq'][:], d['ss'][:], R['q'], R_p, R['tmp'])
        for qi, (qT, rp) in enumerate(zip(qT_list, rope_flags)):
            qb = d['qb'][qi]
            R_qb = d['R_qb'][qi]
            kb.copy('act', qb[:], d['q'][:], [R['q']], [R_qb])
            if rp:
                rope_heads(kb, d['q'][:], qb[:], 8, C['cosq'][:, t_, :], C['sinq'][:, t_, :], d['tmp'][:], 16, R['q'], R_qb,
                           R_c, R['tmp'])
            pb, R_pb = ps.bank_bf(6 + npb % 2)
            npb += 1
            for h in range(8):
                kb.tr(pb[:, h * 128:(h + 1) * 128], qb[:, h, :], G['identb'][:], [R_qb, R_c], [R_pb], inc=(h == 7))
            kb.cpalt(qT[:, :, rows], pb.rearrange('p (h t) -> p h t', h=8), [R_pb], [R_qT[qi]])
    kb.barrier()
    kb.pop()


def stage_nsa(kb, ps, G, zt, P, IN, yT):
    C = G['C']
    R_c = G['R_c']
    kb.push()
    R_p = Res()
    qrT = kb.sb([128, 8, 1024], BF16, 'qrT')
    R_qc = Res()
    R_qr = Res()
    acc = kb.sb([128, 8, 1024], F32, 'accB')
    R_acc = Res()
    gates = kb.sb([128, 8, 24], F32, 'gates')
    R_g = Res()
    for t_ in range(8):
        kb.dma('sp', gates[:, t_, :], zt[t_ * 128:(t_ + 1) * 128, 3072:3096], reads=[G['R_zt']], writes=[R_g])
    kb.act(gates[:], gates[:], AF.Sigmoid, [R_g], [R_g])
    selT = kb.sb([64, 2, 1024], BF16, 'selT')
    R_selT = Res()
    kb.push()
    qcT = kb.sb([128, 8, 1024], BF16, 'qcT')
    q_prep(kb, ps, G, zt, 0, P['nsa_q_gain'], P, [qcT, qrT], [False, True], [R_qc, R_qr])
    HT = kb.sb([128, 8, 256], F32, 'HT')
    for i4 in range(4):
        kb.dma('sp', HT[:, :, i4 * 64:(i4 + 1) * 64], IN['hsT'](i4), reads=[G['R_in_hs']], writes=[R_p])
    w2 = kb.sb([128, 2, 128], BF16, 'w2')
    kb.dma('pool', w2[:, 0, :], P['cmp_w2_k'], writes=[R_p])
    kb.dma('pool', w2[:, 1, :], P['cmp_w2_v'], writes=[R_p])
    gk = kb.sb([128, 1, 128], F32, 'gkn')
    kb.dma('sp', gk[:, 0, :], P['nsa_k_gain'].partition_broadcast(128), writes=[R_p])
    hb = kb.sb([128, 4, 256], F32, 'hblk')
    hbb = kb.sb([128, 4, 256], BF16, 'hblkb')
    kb.memset('dve', hb[:], 0.0, [R_p])
    HT4 = HT[:].rearrange('e (a h) c -> e a h c', h=2)
    kb.tt(hb[:, :, 0:255], HT4[:, :, 0, 0:255], HT4[:, :, 1, 1:256], ALU.add, [R_p], [R_p])
    kb.act(hbb[:], hb[:], AF.Silu, [R_p], [R_p])
    kcT = kb.sb([128, 2, 256], BF16, 'kcT')
    vc = kb.sb([128, 2, 2, 128], BF16, 'vc')
    ckv = kb.sb([128, 1, 128], F32, 'ckv')
    ckb = kb.sb([128, 128], BF16, 'ckb')
    sq = kb.sb([128, 1, 128], F32, 'csq')
    ss = kb.sb([128, 8], F32, 'css')
    R_w = Res()
    for kv in range(2):
        for g in range(2):
            for nt in range(2):
                pb, R_pb = ps.bank((kv * 4 + g * 2 + nt) % 4)
                kb.mm(pb[:, 0:128], hbb[:, kv * 2 + g, nt * 128:(nt + 1) * 128], w2[:, kv, :], True, True, [R_p], [R_pb])
                if kv == 0:
                    kb.copy('act', ckv[:, 0, :], pb[:, 0:128], [R_pb], [R_w])
                    rms_heads(kb, ckv[:], 1, gk[:], sq[:], ss[:], R_w, R_p, R_w)
                    kb.copy('act', ckb[:], ckv[:, 0, :], [R_w], [R_w])
                    pt_, R_pt_ = ps.bank_bf(6 + nt % 2)
                    kb.tr(pt_[:, 0:128], ckb[:], G['identb'][:], [R_w, R_c], [R_pt_])
                    kb.cpalt(kcT[:, g, nt * 128:(nt + 1) * 128], pt_[:, 0:128], [R_pt_], [R_p])
                else:
                    kb.cpalt(vc[:, g, nt, :], pb[:, 0:128], [R_pb], [R_p])
    cthr = C['cthr']
    S = []
    for s_ in range(2):
        d = {'cm': kb.sb([128, 256], F32, 'cm'), 'P': kb.sb([128, 8, 256], F32, 'cP'),
             'pg': kb.sb([128, 2, 256], F32, 'pg'), 'pbf': kb.sb([128, 8, 256], BF16, 'pbf'),
             'pT': kb.sb([128, 8, 2, 128], BF16, 'pT'), 'den': kb.sb([128, 16], F32, 'den'),
             'imp': kb.sb([128, 64], F32, 'imp'), 'v': kb.sb([128, 64], F32, 'vv'), 'f': kb.sb([128, 64], F32, 'ff'),
             'imp2': kb.sb([128, 64], F32, 'imp2'), 'm8': kb.sb([128, 16], F32, 'm8'), 'sel': kb.sb([128, 64], BF16, 'sel')}
        d['R'] = {n: Res() for n in ('cm', 'P', 'pg', 'pbf', 'pT', 'den', 'imp', 'sel')}
        S.append(d)
    for t_ in range(8):
        d = S[t_ % 2]
        R = d['R']
        rows = slice(t_ * 128, (t_ + 1) * 128)
        kb.ts(d['cm'][:], cthr[:], C['tposc'][:, t_:t_ + 1], None, ALU.is_le, None, [R_c], [R['cm']])
        for h in range(8):
            pS, R_pS = ps.bank(h // 2)
            kb.mm(pS[:, (h % 2) * 256:(h % 2 + 1) * 256], qcT[:, h, rows], kcT[:, h // 4, :], True, True, [R_qc, R_p], [R_pS])
        for pr in range(2):
            pp_, R_pp = ps.pair(pr)
            kb.act(d['P'][:, pr * 4:(pr + 1) * 4, :].rearrange('p h n -> p (h n)'), pp_, AF.Exp, R_pp, [R['P']], scale=SCALE)
        kb.tt(d['P'][:], d['P'][:], d['cm'][:].unsqueeze(1).to_broadcast([128, 8, 256]), ALU.mult, [R['P'], R['cm']], [R['P']])
        kb.reduce(d['den'][:, 0:8], d['P'][:], ALU.add, [R['P']], [R['den']])
        kb.ts(d['den'][:, 0:8], d['den'][:, 0:8], 1e-30, None, ALU.max, None, [R['den']], [R['den']])
        kb.recip(d['den'][:, 8:16], d['den'][:, 0:8], [R['den']], [R['den']])
        kb.tt(d['P'][:], d['P'][:], d['den'][:, 8:16].unsqueeze(2).to_broadcast([128, 8, 256]), ALU.mult, [R['P'], R['den']],
              [R['P']])
        kb.reduce(d['pg'][:], d['P'][:].rearrange('p (g h) n -> p g n h', g=2), ALU.add, [R['P']], [R['pg']])
        kb.copy('act', d['pbf'][:], d['P'][:], [R['P']], [R['pbf']])
        for hf in range(2):
            pt_, R_pt_ = ps.bank_bf(6 + hf)
            for k in range(8):
                h, nt = hf * 4 + k // 2, k % 2
                kb.tr(pt_[:, k * 128:(k + 1) * 128], d['pbf'][:, h, nt * 128:(nt + 1) * 128], G['identb'][:],
                      [R['pbf'], R_c], [R_pt_], inc=(k == 7))
            kb.cpalt(d['pT'][:, hf * 4:(hf + 1) * 4].rearrange('p h n t -> p (h n t)'), pt_, [R_pt_], [R['pT']])
        for hf in range(2):
            pO, R_pO = ps.bank(4 + hf)
            for hh in range(4):
                h = hf * 4 + hh
                for nt in range(2):
                    kb.mm(pO[:, hh * 128:(hh + 1) * 128], d['pT'][:, h, nt, :], vc[:, h // 4, nt, :], nt == 0, nt == 1,
                          [R['pT'], R_p], [R_pO], inc=(hh == 3 and nt == 1))
            kb.tt(acc[:, t_, hf * 512:(hf + 1) * 512].rearrange('p (h d) -> p h d', h=4),
                  pO.rearrange('p (h d) -> p h d', h=4),
                  gates[:, t_, hf * 4:(hf + 1) * 4].unsqueeze(2).to_broadcast([128, 4, 128]), ALU.mult, [R_pO, R_g], [R_acc])
        for g in range(2):
            pg4 = d['pg'][:, g, :].rearrange('p (k r) -> p k r', r=4)
            imp = d['imp']
            kb.tt(imp[:], pg4[:, :, 0], pg4[:, :, 1], ALU.add, [R['pg']], [R['imp']])
            kb.tt(imp[:], imp[:], pg4[:, :, 2], ALU.add, [R['pg'], R['imp']], [R['imp']])
            kb.stt(imp[:], pg4[:, :, 3], 0.5, imp[:], ALU.mult, ALU.add, [R['pg'], R['imp']], [R['imp']])
            kb.stt(imp[:, 1:64], pg4[:, 0:63, 3], 0.5, imp[:, 1:64], ALU.mult, ALU.add, [R['pg'], R['imp']], [R['imp']])
            blk = C['blkc'][:, t_:t_ + 1]
            kb.ts(d['v'][:], C['kk'][:], blk, None, ALU.is_le, None, [R_c], [R['imp']])
            kb.ts(d['f'][:], C['kk'][:], blk, -1.0, ALU.subtract, ALU.is_ge, [R_c], [R['imp']])
            kb.tt(d['f'][:], d['f'][:], d['v'][:], ALU.mult, [R['imp']], [R['imp']])
            kb.tt(d['f'][:], d['f'][:], C['e0'][:], ALU.max, [R['imp'], R_c], [R['imp']])
            kb.tt(imp[:], imp[:], d['v'][:], ALU.mult, [R['imp']], [R['imp']])
            kb.ts(d['v'][:], d['v'][:], -1.0, 1e6, ALU.add, ALU.mult, [R['imp']], [R['imp']])
            kb.tt(imp[:], imp[:], d['v'][:], ALU.add, [R['imp']], [R['imp']])
            kb.stt(imp[:], d['f'][:], 2e6, imp[:], ALU.mult, ALU.add, [R['imp']], [R['imp']])
            m8, imp2 = d['m8'], d['imp2']
            kb.emit('dve', lambda e, m8=m8, imp=imp: e.max(out=m8[:, 0:8], in_=imp[:]), [R['imp']], [R['imp']])
            kb.emit('dve', lambda e, m8=m8, imp=imp, imp2=imp2: e.match_replace(out=imp2[:], in_to_replace=m8[:, 0:8],
                                                                               in_values=imp[:], imm_value=-3e6),
                    [R['imp']], [R['imp']])
            kb.emit('dve', lambda e, m8=m8, imp2=imp2: e.max(out=m8[:, 8:16], in_=imp2[:]), [R['imp']], [R['imp']])
            kb.ts(d['sel'][:], imp[:], d['m8'][:, 15:16], None, ALU.is_ge, None, [R['imp']], [R['sel']])
            pt_, R_pt_ = ps.bank_bf(6 + g % 2)
            kb.tr(pt_[0:64, 0:128], d['sel'][:], G['identb'][:], [R['sel'], R_c], [R_pt_])
            kb.cpalt(selT[:, g, rows], pt_[0:64, 0:128], [R_pt_], [R_selT])
    kb.barrier()
    kb.pop()
    ohp = G['ohp']
    kTw = kb.sb([128, 2, 1536], BF16, 'kTw')
    V1w = kb.sb([128, 2, 12, 129], BF16, 'V1w')
    R_kw = Res()
    R_vw = Res()
    kb.memset('dve', V1w[:, :, :, 128:129], 1.0, [R_vw])
    kb.dma('sp', kTw[:, :, 512:1536], IN['own_kT'][1].rearrange('(g d) t -> d g t', d=128), reads=[G['R_in_kv1']],
           writes=[R_kw])
    for g in range(2):
        kb.dma('sp', V1w[:, g, 4:12, 0:128], IN['own_V'][1][:, g * 128:(g + 1) * 128].rearrange('(k p) d -> p k d', p=128),
               reads=[G['R_in_kv1']], writes=[R_vw])

    kb.push()
    A = attn_alloc(kb)
    mask = kb.sb([128, 32, 512], BF16, 'mask')
    R_mask = Res()

    R_mk = [Res() for _ in range(32)]

    def mask_slc(g, qb):
        for kt in range(32):
            pE, R_pE = ps.bank(kt % 4)
            kb.mm(pE, C['ovE'][:, kt, :], selT[:, g, qb * 512:(qb + 1) * 512], True, True, [R_selT, R_c], [R_pE])
            kb.stt(mask[:, kt, :], C['tposr'][:, qb * 512:(qb + 1) * 512], C['sposc'][:, kt:kt + 1], pE, ALU.is_ge, ALU.mult,
                   [R_pE, R_c], [R_mk[kt]])
        return mask, R_mk

    tmpw = kb.sb([128, 512], F32, 'tmpw')
    R_tw = Res()
    def mask_win(g, qb):
        for i in range(8):
            kt = 4 * qb + i
            kb.act(tmpw[:], C['tposr'][:, qb * 512:(qb + 1) * 512], AF.Abs, [R_c], [R_tw], bias=C['nwposb'][:, kt:kt + 1])
            kb.ts(mask[:, i, :], tmpw[:], 255.5, None, ALU.is_le, None, [R_tw], [R_mk[i]])
        return mask, R_mk

    attn_dense(kb, ps, G, A, qrT, R_qr, kv_from_dram(kb, G, A, IN['kT'](0), IN['V'](0), G['R_in_kv0']), mask_slc, acc, R_acc,
               lambda t_, h: (gates[:, t_, 8 + h:9 + h], R_g), False)
    kb.barrier()
    tmpk = A['PT'][0][:].rearrange('p a b -> p (a b)')[:, 0:4096].rearrange('p (i g t) -> p i g t', i=4, g=2)
    tmpv = A['PT'][1][:].rearrange('p a b -> p (a b)')[:, 0:4096].rearrange('p (i k c) -> p i k c', i=4, k=4)
    R_t = Res()
    gk1 = IN['g_kT'][1].rearrange('i (g d) t -> i d g t', g=2)
    gv1 = IN['g_V'][1].rearrange('i r q c -> i (r q) c')
    for i in range(4):
        kb.dma('sp', tmpk[:, i], gk1[i][:, :, 512:1024], reads=[G['R_in_kv1']], writes=[R_t])
        kb.dma('sp', tmpv[:, i], gv1[i][512:1024, :].rearrange('(k p) c -> p k c', p=128), reads=[G['R_in_kv1']], writes=[R_t])
    kb.ts(kTw[:, :, 0:512], tmpk[:, 0], ohp[:, 0:1], None, ALU.mult, None, [R_t, R_c], [R_kw])
    for i in range(1, 4):
        kb.stt(kTw[:, :, 0:512], tmpk[:, i], ohp[:, i:i + 1], kTw[:, :, 0:512], ALU.mult, ALU.add, [R_t, R_c, R_kw], [R_kw])
    for g in range(2):
        dstv = V1w[:, g, 0:4, 0:128]
        kb.ts(dstv, tmpv[:, 0, :, g * 128:(g + 1) * 128], ohp[:, 0:1], None, ALU.mult, None, [R_t, R_c], [R_vw])
        for i in range(1, 4):
            kb.stt(dstv, tmpv[:, i, :, g * 128:(g + 1) * 128], ohp[:, i:i + 1], dstv, ALU.mult, ALU.add, [R_t, R_c, R_vw],
                   [R_vw])
    kb.barrier()
    kb.pop()
    attn_window(kb, ps, G, qrT, R_qr, kTw, V1w, R_kw, R_vw, acc, R_acc, gates, R_g)
    finish_tokmajor(kb, ps, G, zt, 1024, acc, R_acc, yT, 1024)
    kb.pop()


def finish_tokmajor(kb, ps, G, zt, gcol, acc, R_acc, yT, yrow0):
    kb.push()
    R_c = G['R_c']
    bg = [kb.sb([128, 1024], F32, 'bg') for _ in range(2)]
    yb = [kb.sb([128, 1024], BF16, 'ybt') for _ in range(2)]
    yst = [kb.sb([128, 8, 128], BF16, 'yst') for _ in range(2)]
    R_bg = [Res(), Res()]
    R_yb = [Res(), Res()]
    R_yst = [Res(), Res()]
    for t_ in range(8):
        b = t_ % 2
        rows = slice(t_ * 128, (t_ + 1) * 128)
        kb.dma('sp', bg[b][:], zt[rows, gcol:gcol + 1024], reads=[G['R_zt']], writes=[R_bg[b]])
        kb.act(bg[b][:], bg[b][:], AF.Silu, [R_bg[b]], [R_bg[b]])
        kb.tt(yb[b][:], acc[:, t_, :], bg[b][:], ALU.mult, [R_acc, R_bg[b]], [R_yb[b]])
        pb, R_pb = ps.bank_bf(6 + b)
        for h in range(8):
            kb.tr(pb[:, h * 128:(h + 1) * 128], yb[b][:, h * 128:(h + 1) * 128], G['identb'][:], [R_yb[b], R_c], [R_pb],
                  inc=(h == 7))
        kb.cpalt(yst[b][:], pb.rearrange('p (h t) -> p h t', h=8), [R_pb], [R_yst[b]])
        kb.dma('sp', yT[yrow0:yrow0 + 1024, rows].rearrange('(h p) t -> p h t', p=128), yst[b][:], reads=[R_yst[b]],
               writes=[G['R_yT']])
    kb.barrier()
    kb.pop()


def stage_dsa(kb, ps, G, zt, P, IN, yT, mT):
    C = G['C']
    R_c = G['R_c']
    kb.push()
    R_p = Res()
    qdT = kb.sb([128, 8, 1024], BF16, 'qdT')
    R_qd = Res()
    q_prep(kb, ps, G, zt, 3096, P['dsa_q_gain'], P, [qdT], [True], [R_qd])
    kb.push()
    iqT = kb.sb([128, 4, 1024], BF16, 'iqT')
    R_iqT = Res()
    sgn = kb.sb([128, 8, 8], F32, 'sgn')
    R_sgn = Res()
    ikbd = kb.sb([128, 16, 512], BF16, 'ikbd')
    kb.memset('dve', ikbd[:], 0.0, [R_p])
    for hs in range(2):
        for i4 in range(4):
            kb.dma('sp', ikbd[hs * 64:(hs + 1) * 64, i4 * 4:(i4 + 1) * 4, hs * 256:(hs + 1) * 256],
                   IN['ikT'][:, i4, :].rearrange('e (q c) -> e q c', c=256), reads=[G['R_in_ik']], writes=[R_p])
    kb.push()
    S = []
    for s_ in range(2):
        d = {'iq': kb.sb([128, 8, 64], F32, 'iq'), 'iw': kb.sb([128, 8], F32, 'iw'), 'aw': kb.sb([128, 8], F32, 'aw'),
             'tmp': kb.sb([128, 4, 8, 8], F32, 'itmp'), 'iqr': kb.sb([128, 8, 64], F32, 'iqr'),
             'iqb': kb.sb([128, 8, 64], BF16, 'iqb')}
        d['R'] = {n: Res() for n in ('iq', 'iw', 'tmp', 'iqr', 'iqb')}
        S.append(d)
    for t_ in range(8):
        d = S[t_ % 2]
        R = d['R']
        rows = slice(t_ * 128, (t_ + 1) * 128)
        kb.dma('sp', d['iq'][:].rearrange('p h e -> p (h e)'), zt[rows, 5656:6168], reads=[G['R_zt']], writes=[R['iq']])
        kb.dma('sp', d['iw'][:], zt[rows, 6232:6240], reads=[G['R_zt']], writes=[R['iw']])
        kb.copy('dve', d['iqr'][:], d['iq'][:], [R['iq']], [R['iqr']])
        rope_heads(kb, d['iq'][:], d['iqr'][:], 8, C['cosi'][:, t_, :], C['sini'][:, t_, :], d['tmp'][:], 8, R['iq'], R['iqr'],
                   R_c, R['tmp'])
        kb.act(d['aw'][:], d['iw'][:], AF.Abs, [R['iw']], [R['iw']], scale=8 ** -0.5 * 64 ** -0.5)
        kb.act(sgn[:, t_, :], d['iw'][:], AF.Sign, [R['iw']], [R_sgn])
        kb.tt(d['iqb'][:], d['iqr'][:], d['aw'][:].unsqueeze(2).to_broadcast([128, 8, 64]), ALU.mult, [R['iqr'], R['iw']],
              [R['iqb']])
        pb, R_pb = ps.bank_bf(6 + t_ % 2)
        for k in range(4):
            kb.tr(pb[:, k * 128:(k + 1) * 128], d['iqb'][:, 2 * k:2 * k + 2, :].rearrange('p h e -> p (h e)'), G['identb'][:],
                  [R['iqb'], R_c], [R_pb], inc=(k == 3))
        kb.cpalt(iqT[:, :, rows], pb[:, 0:512].rearrange('p (k t) -> p k t', k=4), [R_pb], [R_iqT])
    kb.barrier()
    kb.pop()
    sposr = kb.sb([128, SEQ], F32, 'sposr')
    kb.emit('pool', lambda e: e.iota(sposr[:], [[1, SEQ]], base=0, channel_multiplier=0,
                                     allow_small_or_imprecise_dtypes=True), (), [R_c])
    sc = [kb.sb([128, SEQ], F32, 'score') for _ in range(4)]
    R_sc = [Res() for _ in range(4)]
    rb = [kb.sb([128, 512], BF16, 'rbuf') for _ in range(4)]
    R_rb = [Res() for _ in range(4)]
    dg = [kb.sb([128, 8, 128], BF16, 'dg') for _ in range(2)]
    R_dg = [Res(), Res()]
    cmk = kb.sb([128, SEQ], BF16, 'cmk')
    R_cmk = Res()
    junk = [kb.sb([128, SEQ], BF16, 'junk') for _ in range(2)]
    R_junk = [Res(), Res()]
    Mb = [kb.sb([128, SEQ], BF16, 'Mb') for _ in range(2)]
    R_Mb = [Res(), Res()]
    mst = [kb.sb([128, 8, 128], BF16, 'mst') for _ in range(2)]
    R_mst = [Res(), Res()]
    bs = [kb.sb([128, 16], F32, 'bs') for _ in range(4)]
    R_bs = [Res() for _ in range(4)]
    LO, HI, W0, STP, MID, NMID, CNT, FLG, THR = (kb.sb([128, 4], F32, n) for n in
                                                 ('LO', 'HI', 'W0', 'STP', 'MID', 'NMID', 'CNT', 'FLG', 'THR'))
    R_L = Res()
    R_cn = [Res() for _ in range(4)]
    nr = 0
    nm = 0
    nsb = 0
    NACT = 3
    kb.memset('dve', THR[:, 0:1], 255.5, [R_L])
    kb.memset('dve', THR[:, 1:2], 2 * 255.5 - (SEQ - 2560), [R_L])
    kb.memset('dve', THR[:, 2:4], 2 * 255.5 - SEQ, [R_L])
    X1 = kb.sb([128, 2], F32, 'X1')
    R_x1 = [Res(), Res()]
    for grp in range(2):
        for sl in range(4):
            t_ = grp * 4 + sl
            rows = slice(t_ * 128, (t_ + 1) * 128)
            s_, R_s = sc[sl], R_sc[sl]
            dgt, R_dgt = dg[t_ % 2], R_dg[t_ % 2]
            for h in range(8):
                kb.ts(dgt[:, h, :], G['identb'][:], sgn[:, t_, h:h + 1], None, ALU.mult, None, [R_c, R_sgn], [R_dgt])
            for b256 in range(16):
                ks = slice(b256 * 256, (b256 + 1) * 256)
                pS, R_pS = ps.bank(4 + nsb % 2)
                nsb += 1
                pend = []
                for k in range(4):
                    pI, R_pI = ps.bank(k)
                    kb.mm(pI, iqT[:, k, rows], ikbd[:, b256, :], True, True, [R_iqT, R_p], [R_pI])
                    r_, R_r = rb[nr % 4], R_rb[nr % 4]
                    nr += 1
                    kb.act(r_[:], pI, AF.Relu, [R_pI], [R_r])
                    pend.append((k, r_, R_r))
                    if len(pend) > 3:
                        k0, r0, R_r0 = pend.pop(0)
                        for hs in range(2):
                            h0 = 2 * k0 + hs
                            kb.mm(pS[:, 0:256], dgt[:, h0, :], r0[:, hs * 256:(hs + 1) * 256], h0 == 0, False,
                                  [R_dgt, R_r0], [R_pS], inc=False)
                for (k0, r0, R_r0) in pend:
                    for hs in range(2):
                        h0 = 2 * k0 + hs
                        kb.mm(pS[:, 0:256], dgt[:, h0, :], r0[:, hs * 256:(hs + 1) * 256], h0 == 0, h0 == 7,
                              [R_dgt, R_r0], [R_pS], inc=(h0 == 7))
                kb.copy('dve', s_[:, ks], pS[:, 0:256], [R_pS], [R_s])
            B = bs[sl]
            R_B = R_bs[sl]
            kb.reduce(B[:, 0:1], s_[:], ALU.min, [R_s], [R_B])
            kb.ts(B[:, 0:1], B[:, 0:1], -1.0, None, ALU.add, None, [R_B], [R_B])
            kb.ts(cmk[:], sposr[:], C['tposc'][:, t_:t_ + 1], None, ALU.is_le, None, [R_c], [R_cmk])
            kb.stt(s_[:], s_[:], B[:, 0:1], cmk[:], ALU.subtract, ALU.mult, [R_s, R_B, R_cmk], [R_s])
            kb.reduce(HI[:, sl:sl + 1], s_[:], ALU.max, [R_s], [R_L])
        kb.memset('dve', LO[:], 0.5, [R_L])
        kb.ts(W0[:], HI[:], -0.5, None, ALU.add, None, [R_L], [R_L])
        for it in range(NBIS):
            ck = 0.5 ** (it + 1)
            kb.ts(STP[:], W0[:], ck, None, ALU.mult, None, [R_L], [R_L])
            kb.tt(MID[:], STP[:], LO[:], ALU.add, [R_L], [R_L])
            kb.ts(NMID[:], MID[:], -1.0, None, ALU.mult, None, [R_L], [R_L])
            HALF = 2560
            for sl in range(4):
                s_, R_s = sc[sl], R_sc[sl]
                if sl == 0:
                    kb.ts(junk[0][:], s_[:], MID[:, sl:sl + 1], 0.0, ALU.is_ge, ALU.add, [R_s, R_L], [R_junk[0], R_cn[sl]],
                          accum_out=CNT[:, sl:sl + 1])
                elif sl == 1:
                    kb.ts(junk[0][:, 0:HALF], s_[:, 0:HALF], MID[:, sl:sl + 1], 0.0, ALU.is_ge, ALU.add, [R_s, R_L],
                          [R_junk[0], R_x1[0]], accum_out=X1[:, 0:1])
                    kb.act(junk[1][:, 0:SEQ - HALF], s_[:, HALF:SEQ], AF.Sign, [R_s, R_L], [R_junk[1], R_x1[1]],
                           bias=NMID[:, sl:sl + 1], accum_out=X1[:, 1:2])
                else:
                    kb.act(junk[1][:], s_[:], AF.Sign, [R_s, R_L], [R_junk[1], R_cn[sl]], bias=NMID[:, sl:sl + 1],
                           accum_out=CNT[:, sl:sl + 1])
            kb.stt(CNT[:, 1:2], X1[:, 0:1], 2.0, X1[:, 1:2], ALU.mult, ALU.add, R_x1, [R_cn[1]])
            kb.tt(FLG[:], CNT[:], THR[:], ALU.is_ge, R_cn + [R_L], [R_L])
            kb.tt(FLG[:], FLG[:], STP[:], ALU.mult, [R_L], [R_L])
            kb.tt(LO[:], LO[:], FLG[:], ALU.add, [R_L], [R_L])
        for sl in range(4):
            t_ = grp * 4 + sl
            rows = slice(t_ * 128, (t_ + 1) * 128)
            s_, R_s = sc[sl], R_sc[sl]
            B = bs[sl]
            R_B = R_bs[sl]
            b = t_ % 2
            kb.ts(Mb[b][:], s_[:], LO[:, sl:sl + 1], None, ALU.is_ge, None, [R_s, R_L], [R_Mb[b]])
            for q4 in range(4):
                pb, R_pb = ps.bank_bf(6 + nm % 2)
                ms_, R_ms = mst[nm % 2], R_mst[nm % 2]
                nm += 1
                for k in range(8):
                    kt = q4 * 8 + k
                    kb.tr(pb[:, k * 128:(k + 1) * 128], Mb[b][:, kt * 128:(kt + 1) * 128], G['identb'][:], [R_Mb[b], R_c],
                          [R_pb], inc=(k == 7))
                kb.cpalt(ms_[:], pb.rearrange('p (k t) -> p k t', k=8), [R_pb], [R_ms])
                kb.dma('sp', mT[q4 * 1024:(q4 + 1) * 1024, rows].rearrange('(k p) t -> p k t', p=128), ms_[:], reads=[R_ms],
                       writes=[G['R_mT']])
    kb.barrier()
    kb.pop()
    acc = kb.sb([128, 8, 1024], F32, 'accC')
    R_acc = Res()
    kb.push()
    A = attn_alloc(kb)
    mask = kb.sb([128, 32, 512], BF16, 'maskd')
    R_mask = Res()
    mTv = mT.rearrange('(k p) t -> p k t', p=128)

    R_mq = [Res() for _ in range(4)]

    def mask_dsa(g, qb):
        for q4 in range(4):
            kb.dma('sp', mask[:, q4 * 8:(q4 + 1) * 8, :], mTv[:, q4 * 8:(q4 + 1) * 8, qb * 512:(qb + 1) * 512],
                   reads=[G['R_mT']], writes=[R_mq[q4]])
        return mask, [R_mq[k // 8] for k in range(32)]

    attn_dense(kb, ps, G, A, qdT, R_qd, kv_from_dram(kb, G, A, IN['kT'](2), IN['V'](2), G['R_in_kv2']), mask_dsa, acc, R_acc, None, True)
    kb.barrier()
    kb.pop()
    finish_tokmajor(kb, ps, G, zt, 4120, acc, R_acc, yT, 2048)
    kb.pop()


def stage_outproj(kb, ps, G, yT, w_ap, x_ap, xo_ap):
    kb.push()
    yTs = kb.sb([128, 32, 1024], BF16, 'yTs')
    R_y = Res()
    yv = yT.rearrange('(c p) t -> p c t', p=128)
    for q in range(4):
        kb.dma('sp', yTs[:, q * 8:(q + 1) * 8, :], yv[:, q * 8:(q + 1) * 8, :], reads=[G['R_yT']], writes=[R_y])
    wv = w_ap.rearrange('(c p) n -> p c n', p=128)
    wb = [kb.sb([128, 32, 512], BF16, 'wob') for _ in range(2)]
    R_wb = [Res(), Res()]
    xs = [kb.sb([128, 512], F32, 'xs') for _ in range(4)]
    R_xs = [Res() for _ in range(4)]
    nb = 0
    for ji in range(8):
        b = ji % 2
        c0 = ji * 512
        for q in range(4):
            kb.dma('pool', wb[b][:, q * 8:(q + 1) * 8, :], wv[:, q * 8:(q + 1) * 8, c0:c0 + 512], writes=[R_wb[b]])
        for t_ in range(8):
            rows = slice(t_ * 128, (t_ + 1) * 128)
            x_, R_x = xs[nb % 4], R_xs[nb % 4]
            kb.dma('sp', x_[:], x_ap[rows, c0:c0 + 512], writes=[R_x])
            pb, R_pb = ps.bank(nb % 4)
            nb += 1
            for c in range(32):
                kb.mm(pb, yTs[:, c, rows], wb[b][:, c, :], c == 0, c == 31, [R_y, R_wb[b]], [R_pb], inc=(c == 31))
            kb.tt(x_[:], x_[:], pb, ALU.add, [R_x, R_pb], [R_x])
            kb.dma('sp', xo_ap[rows, c0:c0 + 512], x_[:], reads=[R_x], writes=[G['R_out']])
    kb.barrier()
    kb.pop()


CONST_SPECS = [('identb', [128, 128], BF16), ('tposc', [128, 8], F32), ('tposr', [128, 1024], F32),
               ('cosq', [128, 8, 16], F32), ('sinq', [128, 8, 16], F32), ('cosi', [128, 8, 8], F32),
               ('sini', [128, 8, 8], F32), ('sposc', [128, 32], F32), ('nwposb', [128, 12], F32), ('cthr', [128, 256], F32),
               ('blkc', [128, 8], F32), ('kk', [128, 64], F32), ('e0', [128, 64], F32), ('ovE', [64, 32, 128], BF16)]

T_JOBS_M = [('T', 2048 + i * 512, 512, i * 512, False) for i in range(4)] + \
           [('T', 4608 + i * 512, 512, 2048 + i * 512, False) for i in range(8)] + [('T', 8704, 96, 6144, False)]
F_JOBS_M = [('F', i * 512, 512, i * 512, i < 2) for i in range(4)] + [('F', 4096, 512, 2048, False)] + \
           [('F', 8800 + i * 512, 512, 2560 + i * 512, i in (0, 1, 4, 5)) for i in range(8)]
K_COLS = np.concatenate([np.arange(0, 1024), np.arange(4096, 4608), np.arange(4608, 5632), np.arange(7704, 8216),
                         np.arange(8728, 8792)])
JOBS_K = [('F', 0, 512, 0, True), ('F', 512, 512, 512, True), ('F', 1024, 512, 2048, False),
          ('T', 1536, 512, 2048, False), ('T', 2048, 512, 2560, False), ('T', 2560, 512, 5144, False),
          ('T', 3072, 64, 6168, False)]


def setup_common(nc, kb, mode):
    G = {}
    C = {}
    R_c = Res()
    for name, shape, dt in CONST_SPECS:
        ap = nc.dram_tensor('c_' + name, shape, dt, kind='ExternalInput').ap()
        t = kb.sb(shape, dt, 'c_' + name)
        kb.dma('sp', t[:], ap, writes=[R_c])
        C[name] = t
    G['C'] = C
    G['R_c'] = R_c
    G['identb'] = C['identb']
    G['R_xT'] = Res()
    for n in ('R_zt', 'R_zf', 'R_yT', 'R_out', 'R_in', 'R_mT', 'R_lst', 'R_in_lru', 'R_in_hs', 'R_in_ik', 'R_in_kv0',
              'R_in_kv1', 'R_in_kv2'):
        G[n] = Res()
    return G


def build_fused(dbg=None):
    nc = bass.Bass('TRN2', target_bir_lowering=False)
    kb = KB(nc)
    ps = PS(nc)

    def din(name, shape, dt=F32):
        return nc.dram_tensor(name, shape, dt, kind='ExternalInput').ap()

    def dout(name, shape, dt=F32):
        return nc.dram_tensor(name, shape, dt, kind='ExternalOutput').ap()

    def dint(name, shape, dt=F32):
        return nc.dram_tensor(name, shape, dt).ap()

    x = din('x', [TOK, DM])
    xh0 = din('xh', [4, DM])
    G = setup_common(nc, kb, 'F')
    ohp_d = din('ohp', [128, 4])
    ohp = kb.sb([128, 4], F32, 'ohp')
    kb.dma('sp', ohp[:], ohp_d, writes=[G['R_c']])
    G['ohp'] = ohp
    W = {}
    Wl = [{'w_in': din('w_in%d' % l, [DM, NIN]), 'w_out': din('w_out%d' % l, [DM, DM])} for l in range(2)]
    for name, shape in (('ng', [DM]), ('lru_cw', [128, 8, 4]),
                        ('lru_vec', [128, 8, 4]), ('lru_wa', [8, 128, 128]), ('lru_wx', [8, 128, 128]),
                        ('kgain3', [3, 128]), ('cmp_peT', [128, 2, 32]), ('cmp_w1_k', [32, 128, 128]),
                        ('cmp_w1_v', [32, 128, 128]), ('sc_w', [128, 8, 3]), ('nsa_q_gain', [128]), ('nsa_k_gain', [128]),
                        ('dsa_q_gain', [128]), ('cmp_w2_k', [128, 128]), ('cmp_w2_v', [128, 128])):
        W[name] = din(name, [2] + shape)
    mlt = din('mlt', [128, 4])
    xo = dout('xo', [TOK, DM])
    zt = dint('zt', [TOK, NT])
    zf = dint('zf', [NF, XT])
    yT = dint('yT', [DM, TOK], BF16)
    mT = dint('mT', [SEQ, TOK], BF16)
    xres = dint('xres', [TOK, DM])
    lst = (dint('lru_hl', [1024, TOK]), dint('lru_ac', [1024, TOK]))
    xh_d = dint('xh_d', [4, DM])
    ex_xh = dint('ex_xh', [4, DM])
    g_xh = dint('g_xh', [16, DM])
    for l in range(2):
        P = {k: v[l] for k, v in W.items()}
        P.update(Wl[l])
        P['mlt'] = mlt
        exkv = [dint('ex_kv%d_%d' % (l, a), [512, TOK], BF16) for a in range(3)]
        gakv = [dint('g_kv%d_%d' % (l, a), [4 * 512, TOK], BF16) for a in range(3)]
        ex = {'kT': [t[0:256, :] for t in exkv],
              'V': [t[256:512, :].rearrange('r (q c) -> (r q) c', c=256) for t in exkv],
              'ikT': dint('ex_ik%d' % l, [64, TOK], BF16), 'hsT': dint('ex_hs%d' % l, [1024, 64]),
              'lru': dint('ex_lru%d' % l, [128, 16])}
        ga = {'kT': [t.rearrange('(i r) t -> i r t', i=4)[:, 0:256, :] for t in gakv],
              'V': [t.rearrange('(i r) (q c) -> i r q c', i=4, c=256)[:, 256:512] for t in gakv],
              'ikT': dint('g_ik%d' % l, [4 * 64, TOK], BF16), 'hsT': dint('g_hs%d' % l, [4 * 1024, 64]),
              'lru': dint('g_lru%d' % l, [4 * 128, 16])}
        x_src = x if l == 0 else xres
        xh_src = xh0 if l == 0 else xh_d
        x_dst = xres if l == 0 else xo
        kb.push()
        G['xT'] = kb.sb([128, 32, XT], BF16, 'xT')
        stage_norm_xT(kb, ps, G, x_src, xh_src, P['ng'])
        stage_inproj(kb, ps, G, P['w_in'], F_JOBS_M + T_JOBS_M, zt, zf)
        kb.pop()
        kb.push()
        wa_ = kb.sb([128, 8, 128], BF16, 'wa')
        wx_ = kb.sb([128, 8, 128], BF16, 'wx')
        R_w = Res()
        kb.dma('pool', wa_[:], P['lru_wa'].rearrange('h i j -> i h j'), writes=[R_w])
        kb.dma('pool', wx_[:], P['lru_wx'].rearrange('h i j -> i h j'), writes=[R_w])
        stage_kvprep(kb, ps, G, zt, zf, P, ex)
        R_g = Res()
        for k, rn in (('hsT', 'R_in_hs'), ('ikT', 'R_in_ik')):
            kb.allgather(ex[k], ga[k], reads=[G['R_out']], writes=[R_g, G[rn]])
        for a in (0, 1, 2):
            kb.allgather(exkv[a], gakv[a], reads=[G['R_out']], writes=[R_g, G['R_in_kv%d' % a]])
        R_lo = Res()
        stage_lru(kb, ps, G, zf, P, 'K', lru_out=ex['lru'], store=lst, pre=(wa_, wx_, R_w), R_lout=R_lo)
        kb.allgather(ex['lru'], ga['lru'], reads=[R_lo], writes=[R_g, G['R_in_lru']])
        kb.pop()
        stage_sconv(kb, ps, G, zf, P, yT)
        gkT = [t.rearrange('i (g d) t -> g d i t', g=2) for t in ga['kT']]
        gV = [t.rearrange('i r q (g d) -> i (r q) g d', g=2) for t in ga['V']]
        IN = {'kT': lambda a, gkT=gkT: (lambda g: gkT[a][g]),
              'V': lambda a, gV=gV: (lambda g, q4: gV[a][q4, :, g, :].rearrange('(k p) d -> p k d', p=128)),
              'ikT': ga['ikT'].rearrange('(i e) t -> e i t', i=4),
              'hsT': lambda i4, gh=ga['hsT'].rearrange('(i a e) c -> i e a c', i=4, a=8): gh[i4],
              'own_kT': ex['kT'], 'own_V': ex['V'], 'g_kT': ga['kT'], 'g_V': ga['V']}
        P['lru_all'] = ga['lru'].rearrange('(i p) ck -> p i ck', i=4)
        stage_lru_fin(kb, ps, G, zf, P, yT, lst)
        stage_nsa(kb, ps, G, zt, P, IN, yT)
        stage_dsa(kb, ps, G, zt, P, IN, yT, mT)
        stage_outproj(kb, ps, G, yT, P['w_out'], x_src, x_dst)
        if l == 0:
            if dbg is not None and 'x1' in dbg:
                d1 = dout('dbg_x1', [TOK, DM])
                kb.dma('sp', d1, xres, writes=[G['R_out']])
            R_h = Res()
            kb.dma('sp', ex_xh, xres[1020:1024, :], writes=[R_h])
            kb.allgather(ex_xh, g_xh, reads=[R_h], writes=[R_h])
            kb.push()
            ht = kb.sb([128, 4, 128], F32, 'ht')
            ha = kb.sb([128, 128], F32, 'ha')
            gv = g_xh.rearrange('(i r) (a b) -> r a i b', i=4, b=128)
            for r in range(4):
                kb.dma('sp', ht[r * 32:(r + 1) * 32], gv[r], reads=[R_h], writes=[R_h])
            kb.ts(ha[:], ht[:, 0, :], ohp[:, 0:1], None, ALU.mult, None, [R_h, G['R_c']], [R_h])
            for i in range(1, 4):
                kb.stt(ha[:], ht[:, i, :], ohp[:, i:i + 1], ha[:], ALU.mult, ALU.add, [R_h, G['R_c']], [R_h])
            xv = xh_d.rearrange('r (a b) -> r a b', b=128)
            for r in range(4):
                kb.dma('sp', xv[r], ha[r * 32:(r + 1) * 32, :], reads=[R_h], writes=[R_h])
            kb.barrier()
            kb.pop()
    kb.finish()
    return nc


def rope_tables(pos, r):
    half = r // 2
    inv = np.float32(THETA) ** (-np.arange(half, dtype=np.float32) * np.float32(2.0) / np.float32(r))
    ang = pos.astype(np.float32)[:, None] * inv[None, :].astype(np.float32)
    return np.cos(ang).astype(np.float32), np.sin(ang).astype(np.float32)


def core_consts(j):
    bf = ml_dtypes.bfloat16
    c = {}
    c['identb'] = np.eye(128, dtype=np.float32).astype(bf)
    pos = (1024 * j + np.arange(1024)).astype(np.int64)
    c['tposc'] = pos.reshape(8, 128).T.astype(np.float32).copy()
    c['tposr'] = np.broadcast_to(pos.astype(np.float32)[None, :], (128, 1024)).copy()
    cq, sq = rope_tables(pos, 32)
    ci, si = rope_tables(pos, 16)
    c['cosq'] = cq.reshape(8, 128, 16).transpose(1, 0, 2).copy()
    c['sinq'] = sq.reshape(8, 128, 16).transpose(1, 0, 2).copy()
    c['cosi'] = ci.reshape(8, 128, 8).transpose(1, 0, 2).copy()
    c['sini'] = si.reshape(8, 128, 8).transpose(1, 0, 2).copy()
    c['sposc'] = np.arange(4096).reshape(32, 128).T.astype(np.float32).copy()
    wpos = np.zeros((128, 12), np.float32)
    for kt in range(12):
        base = 1024 * j - 512 + 128 * kt
        wpos[:, kt] = base + np.arange(128)
        if base < 0:
            wpos[:, kt] = -1e6
    c['nwposb'] = -(wpos + np.float32(255.5))
    c['cthr'] = np.broadcast_to((16 * np.arange(256) + 31).astype(np.float32)[None, :], (128, 256)).copy()
    c['blkc'] = (pos // 64).reshape(8, 128).T.astype(np.float32).copy()
    c['kk'] = np.broadcast_to(np.arange(64, dtype=np.float32)[None, :], (128, 64)).copy()
    e0 = np.zeros((128, 64), np.float32)
    e0[:, 0] = 1.0
    c['e0'] = e0
    ov = np.zeros((64, 32, 128), np.float32)
    for kt in range(32):
        ov[2 * kt, kt, 0:64] = 1.0
        ov[2 * kt + 1, kt, 64:128] = 1.0
    c['ovE'] = ov.astype(bf)
    d = {'c_' + k: v for k, v in c.items()}
    ml = np.zeros((128, 4), np.float32)
    ml[:, :j] = 1.0
    d['mlt'] = ml
    oh = np.zeros((128, 4), np.float32)
    if j > 0:
        oh[:, j - 1] = 1.0
    d['ohp'] = oh
    return d


def pvec(v):
    return np.ascontiguousarray(v.reshape(v.shape[0], 8, 128).transpose(0, 2, 1))


def layout_params(p):
    f = lambda a: np.ascontiguousarray(a, dtype=np.float32)
    d = {}
    d['ng'] = f(p['norm_g'])
    for l in range(2):
        d['w_in%d' % l] = f(p['w_in'][l])
        d['w_out%d' % l] = f(p['w_out'][l])
    d['lru_cw'] = f(p['lru_conv_w'].reshape(2, 4, 8, 128).transpose(0, 3, 2, 1))
    d['lru_vec'] = f(np.stack([pvec(p['lru_conv_b']), pvec(p['lru_ba']), pvec(p['lru_bx']), pvec(p['lru_lambda'])], axis=-1))
    d['lru_wa'] = f(p['lru_wa'])
    d['lru_wx'] = f(p['lru_wx'])
    d['kgain3'] = f(np.stack([p['nsa_k_gain'], p['nsa_k_gain'], p['dsa_k_gain']], axis=1))
    d['cmp_peT'] = f(np.stack([p['cmp_pe_k'].transpose(0, 2, 1), p['cmp_pe_v'].transpose(0, 2, 1)], axis=2))
    d['cmp_w1_k'] = f(p['cmp_w1_k'])
    d['cmp_w1_v'] = f(p['cmp_w1_v'])
    d['sc_w'] = f(p['sc_conv_w'].reshape(2, 3, 8, 128).transpose(0, 3, 2, 1))
    d['nsa_q_gain'] = f(p['nsa_q_gain'])
    d['nsa_k_gain'] = f(p['nsa_k_gain'])
    d['dsa_q_gain'] = f(p['dsa_q_gain'])
    d['cmp_w2_k'] = f(p['cmp_w2_k'])
    d['cmp_w2_v'] = f(p['cmp_w2_v'])
    return d


_NC_CACHE = {}


def get_nc(dbg=None):
    key = tuple(dbg) if dbg else None
    if key not in _NC_CACHE:
        _NC_CACHE[key] = build_fused(dbg)
    return _NC_CACHE[key]


def run_fused(p, n_cores=8, dbg=None):
    x = np.ascontiguousarray(p['x'], dtype=np.float32)
    shared = layout_params(p)
    in_maps = []
    for c in range(n_cores):
        b, j = c // 4, c % 4
        m = dict(shared)
        m.update(core_consts(j))
        m['x'] = np.ascontiguousarray(x[b, j * 1024:(j + 1) * 1024, :])
        h = np.zeros((4, DM), np.float32)
        if j > 0:
            h[1:4] = x[b, j * 1024 - 3:j * 1024, :]
        m['xh'] = h
        in_maps.append(m)
    return run_bass_kernel_spmd(get_nc(dbg), in_maps, core_ids=list(range(n_cores))).results


def kernel(**inputs):
    p = {k: np.asarray(v) for k, v in inputs.items()}
    res = run_fused(p)
    out = np.zeros((2, SEQ, DM), np.float32)
    for c in range(8):
        out[c // 4, (c % 4) * 1024:(c % 4 + 1) * 1024, :] = res[c]['xo']
    return out
```
